# Optimizing a Trainium2 kernel written in Bass

```python
import jax, jax.numpy as jnp
from jax import lax
import numpy as np

D_MODEL = 1024
BATCH = 8
SEQ = 2048
DEPTH = 4
DEC_BATCH = 128
DEC_SEQ = 4
PAST_LEN = 16384
PAGE_SIZE = 128

N_AB = (DEPTH + 1) // 2
N_SSD = DEPTH // 2
ML_HEADS = 4
ML_DK = D_MODEL // 16
ML_DV = D_MODEL // 8
ML_QK = ML_HEADS * ML_DK
ML_V = ML_HEADS * ML_DV
GLA_HEADS = 4
GLA_DK = D_MODEL // 16
GLA_DV = D_MODEL // 8
GLA_QK = GLA_HEADS * GLA_DK
GLA_V = GLA_HEADS * GLA_DV
GLA_RANK = 16
GLA_TAU = 16.0
MIX_WIDTH = ML_V + GLA_V
AB_SPLITS = (ML_QK, ML_QK, ML_V, ML_V, ML_HEADS, ML_HEADS,
             GLA_QK, GLA_QK, GLA_V, GLA_V, GLA_RANK)
AB_IN = 2 * ML_QK + 2 * ML_V + 2 * ML_HEADS + 2 * GLA_QK + 2 * GLA_V + GLA_RANK
SSD_INNER = 2 * D_MODEL
SSD_HEADDIM = 64
SSD_HEADS = SSD_INNER // SSD_HEADDIM
SSD_STATE = 128
SSD_GROUPS = 4
SSD_HPG = SSD_HEADS // SSD_GROUPS
SSD_CONV = 4
SSD_CONV_DIM = SSD_INNER + 2 * SSD_GROUPS * SSD_STATE
SSD_IN = SSD_INNER + SSD_CONV_DIM + SSD_HEADS
D_FF = 4 * D_MODEL
CHUNK = 64
DN_ALPHA = (2 * DEPTH) ** 0.25
DN_BETA = (8 * DEPTH) ** -0.25
LN_EPS = 1e-5

kernel_name = "hybrid_mlstm_gla_ssd_deepnorm_step"


def _split(t, sizes):
    idx, acc = [], 0
    for s in sizes[:-1]:
        acc += s
        idx.append(acc)
    return jnp.split(t, idx, axis=-1)


def _chunk_len(T):
    return CHUNK if T % CHUNK == 0 else T


def _chunks(x, L):
    B, T = x.shape[:2]
    return jnp.moveaxis(x.reshape((B, T // L, L) + x.shape[2:]), 1, 0)


def _unchunk(y):
    NC, B, L = y.shape[:3]
    return jnp.moveaxis(y, 0, 1).reshape((B, NC * L) + y.shape[3:])


def _causal(L):
    return jnp.tril(jnp.ones((L, L), dtype=bool))


def _layernorm(x, g, b):
    xf = x.astype(jnp.float32)
    mu = xf.mean(-1, keepdims=True)
    var = jnp.square(xf - mu).mean(-1, keepdims=True)
    y = (xf - mu) * lax.rsqrt(var + LN_EPS) * g.astype(jnp.float32) + b.astype(jnp.float32)
    return y.astype(x.dtype)


def _headnorm(h, g):
    mu = h.mean(-1, keepdims=True)
    var = jnp.square(h - mu).mean(-1, keepdims=True)
    return (h - mu) * lax.rsqrt(var + LN_EPS) * g.astype(jnp.float32)


def _mlstm_chunk(carry, inp):
    C0, n0, m0 = carry
    q, k, v, ig, lf = inp
    L = q.shape[1]
    b = jnp.cumsum(lf, axis=1)
    g = b + m0[:, None, :]
    D = b[:, :, None, :] - b[:, None, :, :] + ig[:, None, :, :]
    D = jnp.where(_causal(L)[None, :, :, None], D, -jnp.inf)
    m = jnp.maximum(g, D.max(axis=2))
    w_intra = jnp.exp(D - m[:, :, None, :])
    w_inter = jnp.exp(g - m)
    s = jnp.einsum('bthk,bshk->btsh', q, k) * w_intra
    num = jnp.einsum('btsh,bshv->bthv', s, v) + w_inter[..., None] * jnp.einsum('bthk,bhkv->bthv', q, C0)
    den = s.sum(axis=2) + w_inter * jnp.einsum('bthk,bhk->bth', q, n0)
    h = num / jnp.maximum(jnp.abs(den), jnp.exp(-m))[..., None]
    mL = m[:, -1]
    wL = jnp.exp(D[:, -1] - mL[:, None, :])
    wL0 = jnp.exp(g[:, -1] - mL)
    C = wL0[..., None, None] * C0 + jnp.einsum('bsh,bshk,bshv->bhkv', wL, k, v)
    n = wL0[..., None] * n0 + jnp.einsum('bsh,bshk->bhk', wL, k)
    return (C, n, mL), h


def _gla_chunk(S0, inp):
    q, k, v, la = inp
    L = q.shape[1]
    A = jnp.cumsum(la, axis=1)
    diff = A[:, :, None] - A[:, None, :]
    diff = jnp.where(_causal(L)[None, :, :, None, None], diff, -jnp.inf)
    s = jnp.einsum('bthk,bshk,btshk->btsh', q, k, jnp.exp(diff))
    o = jnp.einsum('btsh,bshv->bthv', s, v) + jnp.einsum('bthk,bhkv->bthv', q * jnp.exp(A), S0)
    AL = A[:, -1]
    kd = k * jnp.exp(AL[:, None] - A)
    S = jnp.exp(AL)[..., None] * S0 + jnp.einsum('bshk,bshv->bhkv', kd, v)
    return S, o


def _ssd_chunk(h0, inp):
    x, Bm, Cm, dt, a = inp
    L = x.shape[1]
    cs = jnp.cumsum(a, axis=1)
    seg = cs[:, :, None] - cs[:, None]
    seg = jnp.where(_causal(L)[None, :, :, None, None], seg, -jnp.inf)
    w = jnp.exp(seg) * dt[:, None]
    CB = jnp.einsum('btgn,bsgn->btsg', Cm, Bm)
    y = jnp.einsum('btsg,btsgr,bsgrp->btgrp', CB, w, x)
    y = y + jnp.einsum('btgn,bgrpn->btgrp', Cm, h0) * jnp.exp(cs)[..., None]
    csL = cs[:, -1]
    wL = jnp.exp(csL[:, None] - cs) * dt
    h = jnp.exp(csL)[..., None, None] * h0 + jnp.einsum('bsgr,bsgrp,bsgn->bgrpn', wL, x, Bm)
    return h, y


def _ab_mixer(x, C0, n0, m0, S0, w_in, ig_bias, fg_bias, ml_norm, wa2, ba, gla_norm, w_out):
    B, T, _ = x.shape
    L = _chunk_len(T)
    f32 = jnp.float32
    mq, mk, mv, mo, mi, mf, gq, gk, gv, gg, ga = _split((x @ w_in).astype(f32), AB_SPLITS)
    q = mq.reshape(B, T, ML_HEADS, ML_DK)
    k = mk.reshape(B, T, ML_HEADS, ML_DK) * (ML_DK ** -0.5)
    v = mv.reshape(B, T, ML_HEADS, ML_DV)
    ig = mi + ig_bias.astype(f32)
    lf = jax.nn.log_sigmoid(mf + fg_bias.astype(f32))
    (C, n, m), h = lax.scan(_mlstm_chunk, (C0.astype(f32), n0.astype(f32), m0.astype(f32)),
                            (_chunks(q, L), _chunks(k, L), _chunks(v, L), _chunks(ig, L), _chunks(lf, L)))
    h = _headnorm(_unchunk(h), ml_norm.reshape(ML_HEADS, ML_DV))
    ml_out = h.reshape(B, T, ML_V) * jax.nn.sigmoid(mo)
    q2 = gq.reshape(B, T, GLA_HEADS, GLA_DK) * (GLA_DK ** -0.5)
    k2 = gk.reshape(B, T, GLA_HEADS, GLA_DK)
    v2 = gv.reshape(B, T, GLA_HEADS, GLA_DV)
    la = jax.nn.log_sigmoid(ga @ wa2.astype(f32) + ba.astype(f32)) / GLA_TAU
    la = la.reshape(B, T, GLA_HEADS, GLA_DK)
    S, o = lax.scan(_gla_chunk, S0.astype(f32),
                    (_chunks(q2, L), _chunks(k2, L), _chunks(v2, L), _chunks(la, L)))
    o = _headnorm(_unchunk(o), gla_norm.reshape(GLA_HEADS, GLA_DV))
    gla_out = o.reshape(B, T, GLA_V) * jax.nn.silu(gg)
    mix = jnp.concatenate([ml_out, gla_out], axis=-1).astype(x.dtype) @ w_out
    return mix, C.astype(C0.dtype), n.astype(n0.dtype), m.astype(m0.dtype), S.astype(S0.dtype)


def _ssd_mixer(x, h0, conv0, w_in, conv_w, conv_b, dt_bias, a_log, d_skip, norm_g, w_out):
    B, T, _ = x.shape
    L = _chunk_len(T)
    f32 = jnp.float32
    z, xbc, dtr = _split(x @ w_in, (SSD_INNER, SSD_CONV_DIM, SSD_HEADS))
    cat = jnp.concatenate([conv0.astype(xbc.dtype), xbc], axis=1)
    conv = conv_b.astype(f32)
    for w in range(SSD_CONV):
        conv = conv + cat[:, w:w + T].astype(f32) * conv_w[w].astype(f32)
    new_conv = cat[:, T:]
    xbc = jax.nn.silu(conv)
    xs, Bm, Cm = _split(xbc, (SSD_INNER, SSD_GROUPS * SSD_STATE, SSD_GROUPS * SSD_STATE))
    xs = xs.reshape(B, T, SSD_GROUPS, SSD_HPG, SSD_HEADDIM)
    Bm = Bm.reshape(B, T, SSD_GROUPS, SSD_STATE)
    Cm = Cm.reshape(B, T, SSD_GROUPS, SSD_STATE)
    dt = jax.nn.softplus(dtr.astype(f32) + dt_bias.astype(f32)).reshape(B, T, SSD_GROUPS, SSD_HPG)
    A = -jnp.exp(a_log.astype(f32)).reshape(SSD_GROUPS, SSD_HPG)
    h0f = h0.astype(f32).reshape(B, SSD_GROUPS, SSD_HPG, SSD_HEADDIM, SSD_STATE)
    h, y = lax.scan(_ssd_chunk, h0f,
                    (_chunks(xs, L), _chunks(Bm, L), _chunks(Cm, L), _chunks(dt, L), _chunks(dt * A, L)))
    y = _unchunk(y) + d_skip.astype(f32).reshape(SSD_GROUPS, SSD_HPG)[..., None] * xs
    y = y.reshape(B, T, SSD_INNER) * jax.nn.silu(z.astype(f32))
    yg = y.reshape(B, T, SSD_GROUPS, SSD_INNER // SSD_GROUPS)
    yg = yg * lax.rsqrt(jnp.square(yg).mean(-1, keepdims=True) + LN_EPS)
    y = yg.reshape(B, T, SSD_INNER) * norm_g.astype(f32)
    mix = y.astype(x.dtype) @ w_out
    h = h.reshape(B, SSD_HEADS, SSD_HEADDIM, SSD_STATE).astype(h0.dtype)
    return mix, h, new_conv.astype(conv0.dtype)


def _mlp(x, w1, w2):
    return jnp.square(jax.nn.relu(x @ w1)) @ w2


def _trunk(x, mC, mn, mm, gS, sh, sconv, P):
    nC, nn_, nm, nS, nh, nconv = [], [], [], [], [], []
    for l in range(DEPTH):
        j = l // 2
        if l % 2 == 0:
            mix, c, n, m, s = _ab_mixer(x, mC[j], mn[j], mm[j], gS[j], P['ab_w_in'][j], P['ab_ig_bias'][j],
                                        P['ab_fg_bias'][j], P['ab_ml_norm'][j], P['ab_gla_wa2'][j],
                                        P['ab_gla_ba'][j], P['ab_gla_norm'][j], P['ab_w_out'][j])
            nC.append(c); nn_.append(n); nm.append(m); nS.append(s)
        else:
            mix, h, cv = _ssd_mixer(x, sh[j], sconv[j], P['ssd_w_in'][j], P['ssd_conv_w'][j], P['ssd_conv_b'][j],
                                    P['ssd_dt_bias'][j], P['ssd_a_log'][j], P['ssd_d'][j], P['ssd_norm'][j],
                                    P['ssd_w_out'][j])
            nh.append(h); nconv.append(cv)
        x = _layernorm(DN_ALPHA * x + mix, P['ln_mix_g'][l], P['ln_mix_b'][l])
        x = _layernorm(DN_ALPHA * x + _mlp(x, P['mlp_w1'][l], P['mlp_w2'][l]), P['ln_mlp_g'][l], P['ln_mlp_b'][l])
    return x, jnp.stack(nC), jnp.stack(nn_), jnp.stack(nm), jnp.stack(nS), jnp.stack(nh), jnp.stack(nconv)


def setup_inputs(seed: int = 0) -> dict:
    key = jax.random.key(seed)
    keys = jax.random.split(key, 40)
    cnt = [0]

    def nk():
        cnt[0] += 1
        return keys[cnt[0] - 1]

    def nrm(shape, scale):
        return scale * jax.random.normal(nk(), shape, jnp.float32)

    def gain(shape):
        return 1.0 + 0.02 * jax.random.normal(nk(), shape, jnp.float32)

    u = jax.random.uniform(nk(), (N_SSD, SSD_HEADS), jnp.float32)
    dt0 = jnp.exp(np.log(1e-3) + u * (np.log(0.1) - np.log(1e-3)))
    dt_bias = dt0 + jnp.log(-jnp.expm1(-dt0))
    a_log = jnp.log(jax.random.uniform(nk(), (N_SSD, SSD_HEADS), jnp.float32, 1.0, 16.0))
    return {
        'x_prompt': nrm((BATCH, SEQ, D_MODEL), 1.0),
        'x_sample': nrm((DEC_BATCH, DEC_SEQ, D_MODEL), 1.0),
        'state_mlstm_C': nrm((N_AB, DEC_BATCH, ML_HEADS, ML_DK, ML_DV), 1.0),
        'state_mlstm_n': nrm((N_AB, DEC_BATCH, ML_HEADS, ML_DK), 1.0),
        'state_mlstm_m': nrm((N_AB, DEC_BATCH, ML_HEADS), 1.0),
        'state_gla_S': nrm((N_AB, DEC_BATCH, GLA_HEADS, GLA_DK, GLA_DV), 0.5),
        'state_ssd_h': nrm((N_SSD, DEC_BATCH, SSD_HEADS, SSD_HEADDIM, SSD_STATE), 0.5),
        'state_ssd_conv': nrm((N_SSD, DEC_BATCH, SSD_CONV - 1, SSD_CONV_DIM), 1.0),
        'ab_w_in': nrm((N_AB, D_MODEL, AB_IN), D_MODEL ** -0.5),
        'ab_ig_bias': nrm((N_AB, ML_HEADS), 0.1),
        'ab_fg_bias': 3.0 + nrm((N_AB, ML_HEADS), 0.5),
        'ab_ml_norm': gain((N_AB, ML_V)),
        'ab_gla_wa2': nrm((N_AB, GLA_RANK, GLA_QK), GLA_RANK ** -0.5),
        'ab_gla_ba': nrm((N_AB, GLA_QK), 0.1),
        'ab_gla_norm': gain((N_AB, GLA_V)),
        'ab_w_out': nrm((N_AB, MIX_WIDTH, D_MODEL), MIX_WIDTH ** -0.5 * DN_BETA),
        'ssd_w_in': nrm((N_SSD, D_MODEL, SSD_IN), D_MODEL ** -0.5),
        'ssd_conv_w': nrm((N_SSD, SSD_CONV, SSD_CONV_DIM), SSD_CONV ** -0.5),
        'ssd_conv_b': nrm((N_SSD, SSD_CONV_DIM), 0.01),
        'ssd_dt_bias': dt_bias,
        'ssd_a_log': a_log,
        'ssd_d': 1.0 + nrm((N_SSD, SSD_HEADS), 0.1),
        'ssd_norm': gain((N_SSD, SSD_INNER)),
        'ssd_w_out': nrm((N_SSD, SSD_INNER, D_MODEL), SSD_INNER ** -0.5 * DN_BETA),
        'mlp_w1': nrm((DEPTH, D_MODEL, D_FF), D_MODEL ** -0.5),
        'mlp_w2': nrm((DEPTH, D_FF, D_MODEL), D_FF ** -0.5 * DN_BETA),
        'ln_mix_g': gain((DEPTH, D_MODEL)),
        'ln_mix_b': nrm((DEPTH, D_MODEL), 0.01),
        'ln_mlp_g': gain((DEPTH, D_MODEL)),
        'ln_mlp_b': nrm((DEPTH, D_MODEL), 0.01),
    }


def reference(x_prompt, x_sample, state_mlstm_C, state_mlstm_n, state_mlstm_m, state_gla_S, state_ssd_h,
              state_ssd_conv, ab_w_in, ab_ig_bias, ab_fg_bias, ab_ml_norm, ab_gla_wa2, ab_gla_ba, ab_gla_norm,
              ab_w_out, ssd_w_in, ssd_conv_w, ssd_conv_b, ssd_dt_bias, ssd_a_log, ssd_d, ssd_norm, ssd_w_out,
              mlp_w1, mlp_w2, ln_mix_g, ln_mix_b, ln_mlp_g, ln_mlp_b):
    P = {'ab_w_in': ab_w_in, 'ab_ig_bias': ab_ig_bias, 'ab_fg_bias': ab_fg_bias, 'ab_ml_norm': ab_ml_norm,
         'ab_gla_wa2': ab_gla_wa2, 'ab_gla_ba': ab_gla_ba, 'ab_gla_norm': ab_gla_norm, 'ab_w_out': ab_w_out,
         'ssd_w_in': ssd_w_in, 'ssd_conv_w': ssd_conv_w, 'ssd_conv_b': ssd_conv_b, 'ssd_dt_bias': ssd_dt_bias,
         'ssd_a_log': ssd_a_log, 'ssd_d': ssd_d, 'ssd_norm': ssd_norm, 'ssd_w_out': ssd_w_out,
         'mlp_w1': mlp_w1, 'mlp_w2': mlp_w2, 'ln_mix_g': ln_mix_g, 'ln_mix_b': ln_mix_b,
         'ln_mlp_g': ln_mlp_g, 'ln_mlp_b': ln_mlp_b}
    Bp = x_prompt.shape[0]
    dtp = x_prompt.dtype
    z_C = jnp.zeros((N_AB, Bp, ML_HEADS, ML_DK, ML_DV), dtp)
    z_n = jnp.zeros((N_AB, Bp, ML_HEADS, ML_DK), dtp)
    z_m = jnp.zeros((N_AB, Bp, ML_HEADS), dtp)
    z_S = jnp.zeros((N_AB, Bp, GLA_HEADS, GLA_DK, GLA_DV), dtp)
    z_h = jnp.zeros((N_SSD, Bp, SSD_HEADS, SSD_HEADDIM, SSD_STATE), dtp)
    z_cv = jnp.zeros((N_SSD, Bp, SSD_CONV - 1, SSD_CONV_DIM), dtp)
    y_prompt, p_C, p_n, p_m, p_S, p_h, p_cv = _trunk(x_prompt, z_C, z_n, z_m, z_S, z_h, z_cv, P)
    y_sample, s_C, s_n, s_m, s_S, s_h, s_cv = _trunk(x_sample, state_mlstm_C, state_mlstm_n, state_mlstm_m,
                                                     state_gla_S, state_ssd_h, state_ssd_conv, P)
    return (y_prompt, y_sample, p_C, p_n, p_m, p_S, p_h, p_cv, s_C, s_n, s_m, s_S, s_h, s_cv)
```

```python
from contextlib import ExitStack
import numpy as np
import concourse.bass as bass
import concourse.mybir as mybir
from concourse.bass_utils import run_bass_kernel_spmd

F32 = mybir.dt.float32
BF16 = mybir.dt.bfloat16
AF = mybir.ActivationFunctionType
ALU = mybir.AluOpType
AX = mybir.AxisListType

ENGS = ("pe", "act", "dve", "pool", "sp")
NDMASEM = 8

D = 1024
KD = 8
DEPTH = 4
DFF = 4096
LN_EPS = 1e-5
DN_ALPHA = (2 * DEPTH) ** 0.25
AB_IN = 3096
SSD_IN = 5152
NEG = -30000.0


class Buf:
    __slots__ = ("name", "w", "r", "slot", "excl", "persist")

    def __init__(self, name="", excl=False, persist=False):
        self.name = name
        self.excl = excl
        self.slot = None
        self.persist = persist
        self.w = None
        self.r = []


class Slot:
    __slots__ = ("sem", "nd")

    def __init__(self):
        self.sem = None
        self.nd = 0


class Op:
    __slots__ = ("eng", "emit", "deps", "sig", "dma", "sem", "val", "chan")

    def __init__(self, eng, emit, dma):
        self.eng = eng
        self.emit = emit
        self.dma = dma
        self.deps = []
        self.sig = False
        self.sem = None
        self.val = 0
        self.chan = None


class Sched:
    def __init__(self, nc):
        self.nc = nc
        self.ops = {e: [] for e in ENGS}
        self.slots = []
        self.free = {e: [] for e in ENGS}
        self.live = []
        self.since_barrier = []

    def op(self, eng, emit, reads=(), writes=(), dma=False, extra=(), nobar=False):
        o = Op(eng, emit, dma)
        self.count = getattr(self, "count", 0) + 1
        if self.count > getattr(self, "cut", 1 << 60):
            return o
        deps = {}
        for b in reads:
            if b.w is not None:
                deps[id(b.w)] = (b.w, True)
            if b.excl:
                for r in b.r:
                    if id(r) not in deps:
                        deps[id(r)] = (r, False)
        for b in writes:
            if b.w is not None and id(b.w) not in deps:
                deps[id(b.w)] = (b.w, False)
            for r in b.r:
                if id(r) not in deps:
                    deps[id(r)] = (r, False)
        for d in extra:
            deps[id(d)] = (d, True)
        for d, raw in deps.values():
            if d is o:
                continue
            if d.eng == eng and not d.dma and not dma:
                if eng == "pe" or (not raw and eng != "pool"):
                    continue
            o.deps.append(d)
        for b in reads:
            b.r.append(o)
        for b in writes:
            b.w = o
            b.r = []
        self.ops[eng].append(o)
        if dma:
            ch = writes[0] if writes else reads[0]
            if ch.slot is None:
                if self.free[eng]:
                    ch.slot = self.free[eng].pop()
                else:
                    ch.slot = Slot()
                    self.slots.append(ch.slot)
                if not ch.persist:
                    self.live.append((ch, eng))
            ch.slot.nd += 1
            o.chan = ch.slot
            o.val = 16 * ch.slot.nd
            if not nobar:
                self.since_barrier.append(o)
        return o

    def barrier(self):
        last = [self.ops[e][-1] for e in ENGS if self.ops[e] and not self.ops[e][-1].dma]
        pend = list(self.since_barrier)
        self.since_barrier = []
        for e in ("pe", "act", "dve", "pool", "sp"):
            self.op(e, lambda h: h.nop(), extra=last + pend)
        for b, e in self.live:
            self.free[e].append(b.slot)
            b.slot = None
        self.live = []

    def finalize(self, stack):
        nc = self.nc
        for e in ENGS:
            for o in self.ops[e]:
                for d in o.deps:
                    d.sig = True
        self.esem = {}
        for e in ENGS:
            self.esem[e] = stack.enter_context(nc.semaphore("s_" + e))
        for i, sl in enumerate(self.slots):
            sl.sem = stack.enter_context(nc.semaphore("d%d" % i))
        for e in ENGS:
            cnt = 0
            for o in self.ops[e]:
                if o.dma:
                    o.sem = o.chan.sem
                elif o.sig:
                    cnt += 1
                    o.sem = self.esem[e]
                    o.val = cnt

    def emit_engine(self, e, h):
        waited = {}
        for o in self.ops[e]:
            need = {}
            for d in o.deps:
                k = d.sem.num
                if need.get(k, (None, 0))[1] < d.val:
                    need[k] = (d.sem, d.val)
            ws = []
            for k, (s, v) in need.items():
                if waited.get(k, 0) >= v:
                    continue
                waited[k] = v
                ws.append((s, v))
            for (s, v) in ws[1:]:
                h.wait_ge(s, v)
            ins = o.emit(h)
            if ws:
                ins._wait_ge(ws[0][0], ws[0][1])
            if o.dma:
                ins.then_inc(o.sem, 16)
            elif o.sig:
                ins.then_inc(o.sem, 1)
        if e == "sp":
            for sl in self.slots:
                if waited.get(sl.sem.num, 0) < 16 * sl.nd:
                    h.wait_ge(sl.sem, 16 * sl.nd)

    def run_block(self, block):
        s = self

        @block.tensor
        def _(h):
            s.emit_engine("pe", h)

        @block.scalar
        def _(h):
            s.emit_engine("act", h)

        @block.vector
        def _(h):
            s.emit_engine("dve", h)

        @block.gpsimd
        def _(h):
            s.emit_engine("pool", h)

        @block.sync
        def _(h):
            s.emit_engine("sp", h)


class Cfg:
    def __init__(self, seq=2048, nseq=16, slen=4, layers=(0, 1, 2, 3), parts=("mix", "mlp")):
        self.seq = seq
        self.nseq = nseq
        self.slen = slen
        self.ts_real = nseq * slen
        self.ts = 128
        self.T = seq + self.ts
        self.layers = tuple(layers)
        self.parts = tuple(parts)
        self.groups = [(i, min(512, seq - i)) for i in range(0, seq, 512)] + [(seq, self.ts)]
        self.tiles = [(i, 128) for i in range(0, seq, 128)] + [(seq, self.ts)]


class Prog:
    def __init__(self, cfg):
        self.cfg = cfg
        self.nc = bass.Bass("TRN2", target_bir_lowering=False)
        self.S = Sched(self.nc)
        self.S.cut = getattr(cfg, "cut", 1 << 60)
        self.rr = 0

    def sb(self, st, name, shape, dt):
        self.uid = getattr(self, "uid", 0) + 1
        return st.enter_context(self.nc.sbuf_tensor("%s_%d" % (name, self.uid), shape, dt))

    def psbank(self):
        i = self.rr % 4
        self.rr += 1
        return self.pb[i], self.pbB[i]

    def tile_bufs(self, arr, t0, n):
        a = t0 // 128
        b = (t0 + n + 127) // 128
        return arr[a:b]

    def mm(self, out, lhsT, rhs, start, stop, reads, writes):
        return self.S.op("pe", lambda h: h.matmul(out, lhsT, rhs, start=start, stop=stop), reads, writes)

    def tr(self, out, in_, ident, reads, writes):
        if in_.dtype == BF16:
            return self.S.op("pe", lambda h: h.transpose(out, in_, ident), reads, writes)
        return self.S.op("pe", lambda h: h.matmul(out, in_, ident, start=True, stop=True), reads, writes)

    def build(self):
        cfg, nc, S = self.cfg, self.nc, self.S
        T = cfg.T
        dram = {}

        only = getattr(cfg, "only", None)

        def din(name, shape):
            if only is not None and name not in only:
                return None
            dram[name] = nc.dram_tensor(name, list(shape), F32, kind="ExternalInput").ap()
            return dram[name]

        def dout(name, shape):
            if only is not None and name not in only:
                return None
            dram[name] = nc.dram_tensor(name, list(shape), F32, kind="ExternalOutput").ap()
            return dram[name]

        self.dram = dram
        ns = cfg.nseq
        din("xp", [cfg.seq, D])
        din("xs", [cfg.ts_real, D])
        din("st_mC", [2, ns, 4, 64, 128])
        din("st_mn", [2, ns, 4, 64])
        din("st_mm", [2, ns, 4])
        din("st_gS", [2, ns, 4, 64, 128])
        din("st_sh", [2, ns, 32, 64, 128])
        din("st_cv", [2, ns, 3, 3072])
        din("ab_w_in", [2, D, AB_IN])
        din("ab_ig_bias", [2, 4])
        din("ab_fg_bias", [2, 4])
        din("ab_ml_norm", [2, 512])
        din("ab_gla_wa2", [2, 16, 256])
        din("ab_gla_ba", [2, 256])
        din("ab_gla_norm", [2, 512])
        din("ab_w_out", [2, D, D])
        din("ssd_w_in", [2, D, SSD_IN])
        din("ssd_conv_w", [2, 4, 3072])
        din("ssd_conv_b", [2, 3072])
        din("ssd_dt_bias", [2, 32])
        din("ssd_a_log", [2, 32])
        din("ssd_d", [2, 32])
        din("ssd_norm", [2, 2048])
        din("ssd_w_out", [2, 2048, D])
        din("mlp_w1", [DEPTH, D, DFF])
        din("mlp_w2", [DEPTH, DFF, D])
        din("ln_mix_g", [DEPTH, D])
        din("ln_mix_b", [DEPTH, D])
        din("ln_mlp_g", [DEPTH, D])
        din("ln_mlp_b", [DEPTH, D])
        dout("yp", [cfg.seq, D])
        dout("ys", [cfg.ts_real, D])
        dout("p_C", [2, 4, 64, 128])
        dout("p_n", [2, 4, 64])
        dout("p_m", [2, 4])
        dout("p_S", [2, 4, 64, 128])
        dout("p_h", [2, 32, 64, 128])
        dout("p_cv", [2, 3, 3072])
        dout("s_C", [2, ns, 4, 64, 128])
        dout("s_n", [2, ns, 4, 64])
        dout("s_m", [2, ns, 4])
        dout("s_S", [2, ns, 4, 64, 128])
        dout("s_h", [2, ns, 32, 64, 128])
        dout("s_cv", [2, ns, 3, 3072])

        with ExitStack() as st:
            self.st = st
            self.xres = self.sb(st, "xres", [128, KD, T], F32)
            self.xbf = self.sb(st, "xbf", [128, KD, T], BF16)
            ntile = len(cfg.tiles)
            self.Bxres = [Buf("xres%d" % i) for i in range(ntile)]
            self.Bxbf = [Buf("xbf%d" % i) for i in range(ntile)]
            WCAP = 16512
            self.wbuf = [self.sb(st, "wbuf%d" % i, [128, WCAP], BF16) for i in range(2)]
            self.Bw = [[Buf("w%d_%d" % (i, j), persist=True) for j in range(24)] for i in range(2)]
            self.wslot = 0
            self.identf = self.sb(st, "identf", [128, 128], F32)
            self.ident = self.sb(st, "ident", [128, 128], BF16)
            self.ones_bf = self.sb(st, "ones_bf", [128, 128], BF16)
            self.lnp = self.sb(st, "lnp", [128, 4 * DEPTH, KD], F32)
            self.Bconst = Buf("const")
            self.pb = [st.enter_context(nc.psum_tensor("pb%d" % i, [128, 512], F32)) for i in range(7)]
            self.pbB = [Buf("pb%d" % i, excl=True) for i in range(7)]
            self.pbf = st.enter_context(nc.psum_tensor("pbf", [128, 1024], BF16))
            self.BpbfB = Buf("pbf", excl=True)

            self.setup_consts()
            self.wq = self.weight_specs()
            self.wq_i = 0
            self.wq_ready = self.load_w(self.wq[0]) if self.wq else None
            self.load_x()
            for l in cfg.layers:
                if "mix" in cfg.parts:
                    if l % 2 == 0:
                        self.ab_layer(l)
                    else:
                        self.ssd_layer(l)
                if "mlp" in cfg.parts:
                    self.mlp_layer(l)
            self.store_y()
            S.finalize(st)
            with nc.Block() as block:
                S.run_block(block)
        return nc

    def setup_consts(self):
        S, d = self.S, self.dram
        identf, ident, ones_bf = self.identf, self.ident, self.ones_bf
        Bc = self.Bconst
        S.op("pool", lambda h: h.memset(identf[:], 0.0), writes=[Bc])
        S.op("pool", lambda h: h.affine_select(out=identf[:], in_=identf[:], pattern=[[-1, 128]],
                                               compare_op=ALU.not_equal, fill=1.0, base=0,
                                               channel_multiplier=1), reads=[Bc], writes=[Bc])
        S.op("dve", lambda h: h.tensor_copy(ident[:], identf[:]), reads=[Bc], writes=[Bc])
        S.op("dve", lambda h: h.memset(ones_bf[:], 1.0), writes=[Bc])

    def load_x(self):
        cfg, S, d = self.cfg, self.S, self.dram
        with ExitStack() as st:
            xin = [self.sb(st, "xin%d" % i, [128, D], F32) for i in range(2)]
            Bxin = [Buf() for _ in range(2)]
            lnraw = self.sb(st, "lnraw", [128, 128], F32)
            Blr = [Buf() for _ in range(4)]
            for k, nm in enumerate(("ln_mix_g", "ln_mix_b", "ln_mlp_g", "ln_mlp_b")):
                S.op("sp", lambda h, k=k, nm=nm: h.dma_start(out=lnraw[k * 32:(k + 1) * 32, :], in_=d[nm].rearrange("l (c p) -> (l c) p", p=128)),
                     writes=[Blr[k]], dma=True)
            pl, Bpl = self.psbank()
            self.mm(pl[:, 0:128], lnraw[:, :], self.identf[:, :], True, True, Blr + [self.Bconst], [Bpl])
            S.op("dve", lambda h, pl=pl: h.tensor_copy(self.lnp[:, :, :].rearrange("p a c -> p (a c)"), pl[:, 0:128]), reads=[Bpl], writes=[self.Bconst])
            S.op("pool", lambda h: h.memset(self.xres[:, :, cfg.seq:cfg.T], 0.0), writes=[self.Bxres[-1]])
            S.op("pool", lambda h: h.memset(self.xbf[:, :, cfg.seq:cfg.T], 0.0), writes=[self.Bxbf[-1]])
            for ti, (t0, n) in enumerate(cfg.tiles):
                if t0 >= cfg.seq:
                    n = cfg.ts_real
                src = d["xp"][t0:t0 + n, :] if t0 < cfg.seq else d["xs"][:, :]
                xi, Bx = xin[ti % 2], Bxin[ti % 2]
                S.op("sp", lambda h, xi=xi, src=src, n=n: h.dma_start(out=xi[0:n, :], in_=src), writes=[Bx], dma=True)
                for half in range(2):
                    pt, Bp = self.psbank()
                    for c4 in range(4):
                        c = half * 4 + c4
                        self.tr(pt[:, c4 * 128:c4 * 128 + n], xi[0:n, c * 128:(c + 1) * 128], self.identf[0:n, 0:n],
                                [Bx, self.Bconst], [Bp])
                    pv = pt[:, :].rearrange("p (c t) -> p c t", t=128)[:, :, 0:n]
                    xr = self.xres[:, half * 4:half * 4 + 4, t0:t0 + n]
                    xb = self.xbf[:, half * 4:half * 4 + 4, t0:t0 + n]
                    S.op("dve", lambda h, xr=xr, pv=pv: h.tensor_copy(xr, pv), reads=[Bp], writes=[self.Bxres[ti]])
                    S.op("act", lambda h, xb=xb, pv=pv: h.activation(out=xb, in_=pv, func=AF.Copy), reads=[Bp],
                         writes=[self.Bxbf[ti]])
        S.barrier()

    def store_y(self):
        cfg, S, d = self.cfg, self.S, self.dram
        S.barrier()
        with ExitStack() as st:
            yo = [self.sb(st, "yo%d" % i, [128, D], F32) for i in range(2)]
            Byo = [Buf() for _ in range(2)]
            for ti, (t0, n) in enumerate(cfg.tiles):
                if t0 >= cfg.seq:
                    n = cfg.ts_real
                dst = d["yp"][t0:t0 + n, :] if t0 < cfg.seq else d["ys"][:, :]
                y, By = yo[ti % 2], Byo[ti % 2]
                for half in range(2):
                    pt, Bp = self.psbank()
                    for c4 in range(4):
                        c = half * 4 + c4
                        self.tr(pt[0:n, c4 * 128:(c4 + 1) * 128], self.xres[:, c, t0:t0 + n], self.identf[:, :],
                                [self.Bxres[ti], self.Bconst], [Bp])
                    eng = "dve" if half == 0 else "act"
                    if eng == "dve":
                        S.op("dve", lambda h, y=y, pt=pt, half=half, n=n: h.tensor_copy(y[0:n, half * 512:(half + 1) * 512], pt[0:n, :]),
                             reads=[Bp], writes=[By])
                    else:
                        S.op("act", lambda h, y=y, pt=pt, half=half, n=n: h.activation(out=y[0:n, half * 512:(half + 1) * 512], in_=pt[0:n, :], func=AF.Copy),
                             reads=[Bp], writes=[By])
                S.op("sp", lambda h, y=y, dst=dst, n=n: h.dma_start(out=dst, in_=y[0:n, :]), reads=[By], dma=True)

    def weight_specs(self):
        cfg, d = self.cfg, self.dram
        skip = getattr(cfg, "skip", ())
        out = []
        for l in cfg.layers:
            j = l // 2
            if "mix" in cfg.parts:
                if l % 2 == 0:
                    w_in, w_out = d["ab_w_in"][j], d["ab_w_out"][j]
                    if "ml" not in skip:
                        out.append([(w_in[:, 0:1544], 8, 1544), (w_out[0:512, :], 4, 1024)])
                    if "gla" not in skip:
                        out.append([(w_in[:, 1544:3096], 8, 1552), (w_out[512:1024, :], 4, 1024)])
                else:
                    w_in, w_out = d["ssd_w_in"][j], d["ssd_w_out"][j]
                    for g in getattr(cfg, "ssd_groups", (0, 1, 2, 3)):
                        out.append([(w_in[:, g * 512:(g + 1) * 512], 8, 512),
                                    (w_in[:, 2048 + g * 512:2048 + (g + 1) * 512], 8, 512),
                                    (w_in[:, 4096 + g * 128:4096 + (g + 1) * 128], 8, 128),
                                    (w_in[:, 4608 + g * 128:4608 + (g + 1) * 128], 8, 128),
                                    (w_in[:, 5120 + g * 8:5120 + (g + 1) * 8], 8, 8),
                                    (w_out[g * 512:(g + 1) * 512, :], 4, 1024)])
            if "mlp" in cfg.parts:
                w1, w2 = d["mlp_w1"][l], d["mlp_w2"][l]
                for q in range(4):
                    out.append([(w1[:, q * 1024:(q + 1) * 1024], 8, 1024), (w2[q * 1024:(q + 1) * 1024, :], 8, 1024)])
        return out

    def take_w(self):
        if not hasattr(self, "wq"):
            self.wq = self.weight_specs()
            self.wq_i = 0
            self.wq_ready = self.load_w(self.wq[0]) if self.wq else None
        cur = self.wq_ready
        self.wq_i += 1
        self.wq_ready = self.load_w(self.wq[self.wq_i]) if self.wq_i < len(self.wq) else None
        self.cur_slot = cur[2]
        return cur[0], cur[1]

    def load_w(self, pieces):
        S = self.S
        slot = self.wslot
        self.wslot ^= 1
        wb = self.wbuf[slot]
        old_users = []
        for B_ in self.Bw[slot]:
            old_users.extend(B_.r)
            if B_.w is not None:
                old_users.append(B_.w)
        off = 0
        views, bufs = [], []
        j = 0
        for (src, k, cols) in pieces:
            v = wb[:, off:off + k * cols].rearrange("p (k c) -> p k c", c=cols)
            bl = []
            if cols < 512:
                B = self.Bw[slot][j % 24]
                j += 1
                S.op("pool", lambda h, v=v, src=src: h.dma_start(out=v, in_=src.rearrange("(k p) c -> p k c", p=128)), writes=[B], dma=True, nobar=True, extra=old_users)
                views.append(v)
                bufs.append([B] * k)
                off += k * cols
                continue
            for kk in range(k):
                B = self.Bw[slot][j % 24]
                j += 1
                S.op("pool", lambda h, v=v, src=src, kk=kk: h.dma_start(out=v[:, kk, :], in_=src[kk * 128:(kk + 1) * 128, :]),
                     writes=[B], dma=True, nobar=True, extra=old_users)
                bl.append(B)
            views.append(v)
            bufs.append(bl)
            off += k * cols
        assert off <= 16512, off
        return views, bufs, slot

    def layer_norm(self, kind, l, t0, n, sc):
        if n > 256:
            for o in range(0, n, 256):
                self.layer_norm(kind, l, t0 + o, min(256, n - o), sc)
            return
        S = self.S
        Bxr = self.tile_bufs(self.Bxres, t0, n)
        Bxb = self.tile_bufs(self.Bxbf, t0, n)
        u = self.xres[:, :, t0:t0 + n]
        ub, usq, mean, var, rstd, nmr, tmp = sc["ub"], sc["usq"], sc["mean"], sc["var"], sc["rstd"], sc["nmr"], sc["tmp"]
        Bub, Busq, Bst, Btmp = sc["Bub"], sc["Busq"], sc["Bst"], sc["Btmp"]
        S.op("act", lambda h: h.activation(out=ub[:, :, 0:n], in_=u, func=AF.Copy), reads=Bxr, writes=[Bub])
        S.op("act", lambda h: h.activation(out=usq[:, :, 0:n], in_=u, func=AF.Square), reads=Bxr, writes=[Busq])
        p1, B1 = self.psbank()
        for c in range(KD):
            self.mm(p1[:, 0:n], self.ones_bf[:, :], ub[:, c, 0:n], c == 0, c == KD - 1, [Bub, self.Bconst], [B1])
        p2, B2 = self.psbank()
        for c in range(KD):
            self.mm(p2[:, 0:n], self.ones_bf[:, :], usq[:, c, 0:n], c == 0, c == KD - 1, [Busq, self.Bconst], [B2])
        S.op("dve", lambda h: h.tensor_scalar(mean[:, 0:n], p1[:, 0:n], 1.0 / D, None, ALU.mult), reads=[B1], writes=[Bst])
        S.op("dve", lambda h: h.tensor_tensor(var[:, 0:n], mean[:, 0:n], mean[:, 0:n], ALU.mult), reads=[Bst], writes=[Bst])
        S.op("dve", lambda h: h.scalar_tensor_tensor(var[:, 0:n], p2[:, 0:n], 1.0 / D, var[:, 0:n], ALU.mult, ALU.subtract),
             reads=[B2, Bst], writes=[Bst])
        S.op("dve", lambda h: h.tensor_scalar(var[:, 0:n], var[:, 0:n], LN_EPS, None, ALU.add), reads=[Bst], writes=[Bst])
        S.op("act", lambda h: h.activation(out=rstd[:, 0:n], in_=var[:, 0:n], func=AF.Ln), reads=[Bst], writes=[Bst])
        S.op("act", lambda h: h.activation(out=rstd[:, 0:n], in_=rstd[:, 0:n], func=AF.Exp, scale=-0.5), reads=[Bst], writes=[Bst])
        S.op("dve", lambda h: h.scalar_tensor_tensor(nmr[:, 0:n], mean[:, 0:n], -1.0, rstd[:, 0:n], ALU.mult, ALU.mult),
             reads=[Bst], writes=[Bst])
        gi = (0 if kind == "mix" else 2) * DEPTH + l
        bi = gi + DEPTH
        for c in range(KD):
            tc_ = tmp[:, c % 2, 0:n]
            Bt = Btmp[c % 2]
            eng = "dve"
            S.op(eng, lambda h, tc_=tc_, c=c: h.tensor_tensor(tc_, self.xres[:, c, t0:t0 + n], rstd[:, 0:n], ALU.mult),
                 reads=Bxr + [Bst], writes=[Bt])
            S.op(eng, lambda h, tc_=tc_: h.tensor_tensor(tc_, tc_, nmr[:, 0:n], ALU.add), reads=[Bt, Bst], writes=[Bt])
            g_ap = self.lnp[:, gi, c:c + 1]
            b_ap = self.lnp[:, bi, c:c + 1]
            S.op("act", lambda h, tc_=tc_, c=c, g_ap=g_ap, b_ap=b_ap: h.activation(out=self.xres[:, c, t0:t0 + n], in_=tc_, func=AF.Identity,
                                                                                   bias=b_ap, scale=g_ap),
                 reads=[Bt, self.Bconst], writes=Bxr)
            S.op("act", lambda h, tc_=tc_, c=c, g_ap=g_ap, b_ap=b_ap: h.activation(out=self.xbf[:, c, t0:t0 + n], in_=tc_, func=AF.Identity,
                                                                                   bias=b_ap, scale=g_ap),
                 reads=[Bt, self.Bconst], writes=Bxb)

    def ln_scratch(self, st):
        sc = {}
        sc["ub"] = self.sb(st, "ln_ub", [128, KD, 256], BF16)
        sc["usq"] = self.sb(st, "ln_usq", [128, KD, 256], BF16)
        for nm in ("mean", "var", "rstd", "nmr"):
            sc[nm] = self.sb(st, "ln_" + nm, [128, 256], F32)
        sc["tmp"] = self.sb(st, "ln_tmp", [128, 2, 256], F32)
        sc["Bub"], sc["Busq"], sc["Bst"] = Buf(), Buf(), Buf()
        sc["Btmp"] = [Buf(), Buf()]
        return sc

    def accum_u(self, first, ps, c, t0, n):
        S = self.S
        pt, Bp = ps
        Bxr = self.tile_bufs(self.Bxres, t0, n)
        xr = self.xres[:, c, t0:t0 + n]
        if first:
            S.op("dve", lambda h: h.scalar_tensor_tensor(xr, xr, DN_ALPHA, pt[:, 0:n], ALU.mult, ALU.add),
                 reads=Bxr + [Bp], writes=Bxr)
        else:
            S.op("dve", lambda h: h.tensor_tensor(xr, xr, pt[:, 0:n], ALU.add), reads=Bxr + [Bp], writes=Bxr)

    def mlp_layer(self, l):
        cfg, S, d = self.cfg, self.S, self.dram
        with ExitStack() as st:
            hq = [self.sb(st, "hq%d" % i, [128, 8, 512], BF16) for i in range(2)]
            Bhq = [Buf(), Buf()]
            rl = [self.sb(st, "rl%d" % i, [128, 512], BF16) for i in range(2)]
            Brl = [Buf(), Buf()]
            sc = self.ln_scratch(st)
            w1 = d["mlp_w1"][l]
            w2 = d["mlp_w2"][l]
            gi = 0
            ri = 0
            for q in range(4):
                (w1v, w2v), (B1, B2) = self.take_w()
                def w1_part(t0, n, hh, Bh, w1v=w1v, B1=B1):
                    nonlocal ri
                    Bxb = self.tile_bufs(self.Bxbf, t0, n)
                    for hc in range(8):
                        ps = self.psbank()
                        for kc in range(KD):
                            self.mm(ps[0][:, 0:n], w1v[:, kc, hc * 128:(hc + 1) * 128], self.xbf[:, kc, t0:t0 + n],
                                    kc == 0, kc == KD - 1, Bxb + [B1[kc]], [ps[1]])
                        r, Br = rl[ri % 2], Brl[ri % 2]
                        ri += 1
                        S.op("act", lambda h, r=r, ps=ps, n=n: h.activation(out=r[:, 0:n], in_=ps[0][:, 0:n], func=AF.Relu),
                             reads=[ps[1]], writes=[Br])
                        S.op("dve", lambda h, r=r, hh=hh, hc=hc, n=n: h.tensor_tensor(hh[:, hc, 0:n], r[:, 0:n], r[:, 0:n], ALU.mult),
                             reads=[Br], writes=[Bh])

                def w2_part(t0, n, hh, Bh, q=q, w2v=w2v, B2=B2):
                    for fc in range(KD):
                        ps = self.psbank()
                        for hc in range(8):
                            self.mm(ps[0][:, 0:n], w2v[:, hc, fc * 128:(fc + 1) * 128], hh[:, hc, 0:n],
                                    hc == 0, hc == 7, [Bh, B2[hc]], [ps[1]])
                        self.accum_u(q == 0, ps, fc, t0, n)
                    if q == 3:
                        self.layer_norm("mlp", l, t0, n, sc)

                prev = None
                for (t0, n) in cfg.groups:
                    hh, Bh = hq[gi % 2], Bhq[gi % 2]
                    gi += 1
                    w1_part(t0, n, hh, Bh)
                    if prev is not None:
                        w2_part(*prev)
                    prev = (t0, n, hh, Bh)
                w2_part(*prev)
        S.barrier()

    def mixer_consts(self, st, small=False, need01=True):
        S = self.S
        K_ = {}
        B = Buf("mixconst")
        K_["B"] = B
        ns, sl = 128 // self.cfg.slen, self.cfg.slen
        ts = 128
        mneg_p = self.sb(st, "mneg_p", [128, 128], BF16)
        m01_p = self.sb(st, "m01_p", [128, 128], BF16) if need01 else None
        mneg_s = self.sb(st, "mneg_s", [128, 128], BF16)
        m01_s = self.sb(st, "m01_s", [128, 128], BF16) if need01 else None
        tokmask = self.sb(st, "tokmask", [128, 16], BF16)
        QW = 64 if small else 128
        qmask = self.sb(st, "qmask", [128, 16, QW], BF16)
        onesf = self.sb(st, "onesf", [128, 128], F32)
        rst_s = self.sb(st, "rst_s", [128, 128], F32)
        with ExitStack() as st2:
            t1 = self.sb(st2, "mc_t1", [128, 128], F32)
            t2 = self.sb(st2, "mc_t2", [128, 128], F32)
            t3 = self.sb(st2, "mc_t3", [128, 16, QW], F32)
            Bt = Buf()
            S.op("dve", lambda h: h.memset(onesf[:], 1.0), writes=[B])
            S.op("pool", lambda h: h.memset(t1[:], 1.0), writes=[Bt])
            S.op("pool", lambda h: h.affine_select(out=t1[:], in_=t1[:], pattern=[[1, 128]], compare_op=ALU.is_ge, fill=0.0,
                                                   base=0, channel_multiplier=-1), reads=[Bt], writes=[Bt])
            if need01:
                S.op("dve", lambda h: h.tensor_copy(m01_p[:], t1[:]), reads=[Bt], writes=[B])
            S.op("dve", lambda h: h.tensor_scalar(mneg_p[:], t1[:], -1.0, -NEG, ALU.add, ALU.mult), reads=[Bt], writes=[B])
            S.op("pool", lambda h: h.memset(t2[:, 0:ns], 1.0), writes=[Bt])
            S.op("pool", lambda h: h.affine_select(out=t2[:, 0:ns], in_=t2[:, 0:ns], pattern=[[-sl, ns]], compare_op=ALU.is_ge, fill=0.0,
                                                   base=0, channel_multiplier=1), reads=[Bt], writes=[Bt])
            S.op("pool", lambda h: h.affine_select(out=t2[:, 0:ns], in_=t2[:, 0:ns], pattern=[[sl, ns]], compare_op=ALU.is_ge, fill=0.0,
                                                   base=sl - 1, channel_multiplier=-1), reads=[Bt], writes=[Bt])
            S.op("dve", lambda h: h.tensor_copy(tokmask[:], t2[:, 0:16]), reads=[Bt], writes=[B])
            if need01:
                S.op("dve", lambda h: h.memset(m01_s[:], 0.0), writes=[B])
            S.op("dve", lambda h: h.memset(mneg_s[:], 0.0), writes=[B])
            bd = t2[0:ts, 0:ns].unsqueeze(2).broadcast_to([ts, ns, sl])
            S.op("dve", lambda h: h.tensor_tensor(t1[0:ts, 0:ts].rearrange("p (j t) -> p j t", t=sl), t1[0:ts, 0:ts].rearrange("p (j t) -> p j t", t=sl), bd, ALU.mult),
                 reads=[Bt], writes=[Bt])
            if need01:
                S.op("dve", lambda h: h.tensor_copy(m01_s[0:ts, 0:ts], t1[0:ts, 0:ts]), reads=[Bt], writes=[B])
            S.op("dve", lambda h: h.tensor_scalar(mneg_s[0:ts, 0:ts], t1[0:ts, 0:ts], -1.0, -NEG, ALU.add, ALU.mult), reads=[Bt], writes=[B])
            S.op("pool", lambda h: h.memset(t3[:], 1.0), writes=[Bt])
            S.op("pool", lambda h: h.affine_select(out=t3[:], in_=t3[:], pattern=[[-sl, 16], [1, QW]], compare_op=ALU.is_ge, fill=0.0,
                                                   base=0, channel_multiplier=0), reads=[Bt], writes=[Bt])
            S.op("pool", lambda h: h.affine_select(out=t3[:], in_=t3[:], pattern=[[sl, 16], [-1, QW]], compare_op=ALU.is_ge, fill=0.0,
                                                   base=sl - 1, channel_multiplier=0), reads=[Bt], writes=[Bt])
            S.op("dve", lambda h: h.tensor_copy(qmask[:], t3[:]), reads=[Bt], writes=[B])
            S.op("dve", lambda h: h.memset(rst_s[:], 1.0), writes=[B])
            S.op("dve", lambda h: h.memset(rst_s[:, :].rearrange("p (j t) -> p j t", t=sl)[:, :, 0:1], 0.0), reads=[B], writes=[B])
            S.barrier()
        K_.update(mneg_p=mneg_p, m01_p=m01_p, mneg_s=mneg_s, m01_s=m01_s, tokmask=tokmask, qmask=qmask, onesf=onesf, rst_s=rst_s)
        return K_

    def chunks(self):
        cfg = self.cfg
        out = [(t0, 128, 1, 128, False) for t0 in range(0, cfg.seq, 128)]
        out.append((cfg.seq, cfg.ts, cfg.nseq, cfg.slen, True))
        return out

    def out_proj(self, wo, Bwo, nrow, srcT, Bsrc, t0, n, first):
        for fc in range(KD):
            ps = self.psbank()
            for r in range(nrow):
                self.mm(ps[0][:, 0:n], wo[:, r, fc * 128:(fc + 1) * 128], srcT[:, r, 0:n], r == 0, r == nrow - 1,
                        [Bsrc, Bwo[r]], [ps[1]])
            self.accum_u(first, ps, fc, t0, n)

    def ab_layer(self, l):
        j = l // 2
        skip = getattr(self.cfg, "skip", ())
        first = True
        if "ml" not in skip:
            self.ml_phase(l, j, first)
            first = False
        if "gla" not in skip:
            self.gla_phase(l, j, first)
        with ExitStack() as st:
            sc = self.ln_scratch(st)
            for (t0, n) in self.cfg.groups:
                self.layer_norm("mix", l, t0, n, sc)
        self.S.barrier()

    def ml_phase(self, l, j, first):
        cfg, S, d = self.cfg, self.S, self.dram
        NS = cfg.nseq
        with ExitStack() as st:
            MC = self.mixer_consts(st, need01=False)
            Bmc = MC["B"]
            onesf = MC["onesf"]
            w_in = d["ab_w_in"][j]
            w_out = d["ab_w_out"][j]
            (wv, wo), (Bwi, Bwo) = self.take_w()
            gb = self.sb(st, "ml_gb", [4, 2], F32)
            gnorm = self.sb(st, "ml_gnorm", [128, 512], BF16)
            NB = 128 // cfg.slen
            m0T = self.sb(st, "ml_m0T", [4, NB], F32)
            zc = self.sb(st, "ml_zc", [4, 1], F32)
            Bp = Buf("mlparams")
            Bp2 = Buf("mlparams2")
            Bp3 = Buf("mlparams3")
            Bp4 = Buf("mlparams4")
            S.op("sp", lambda h: h.dma_start(out=gb[:, 0:1], in_=d["ab_ig_bias"][j].rearrange("(h o) -> h o", o=1)), writes=[Bp], dma=True)
            S.op("sp", lambda h: h.dma_start(out=gb[:, 1:2], in_=d["ab_fg_bias"][j].rearrange("(h o) -> h o", o=1)), writes=[Bp2], dma=True)
            S.op("dve", lambda h: h.tensor_scalar(gb[:, 1:2], gb[:, 1:2], -1.0, None, ALU.mult), reads=[Bp2], writes=[Bp2])
            S.op("pool", lambda h: h.dma_start(out=gnorm[:], in_=d["ab_ml_norm"][j].partition_broadcast(128)), writes=[Bp3], dma=True)
            S.op("dve", lambda h: h.memset(m0T[:], 0.0), writes=[Bp4])
            S.op("sp", lambda h: h.dma_start(out=m0T[:, 0:NS], in_=d["st_mm"][j].rearrange("s h -> h s"), allow_slow_non_contiguous=True),
                 reads=[Bp4], writes=[Bp4], dma=True)
            S.op("dve", lambda h: h.memset(zc[:], 0.0), writes=[Bp])
            Bpar = [Bp, Bp2, Bp3, Bp4]
            NSLOT = 9
            IG, SP, CSP, A_, M_, NEGM, WE, ENM, TMP = range(9)
            G1 = self.sb(st, "ml_G", [4, NSLOT, 128], F32)
            G = [G1, G1]
            BG1 = Buf()
            BG = [BG1, BG1]
            carry = self.sb(st, "ml_carry", [4, 2], F32)
            Bcarry = Buf()
            RW = self.sb(st, "ml_RW", [4, 160], F32)
            RWD = self.sb(st, "ml_RWD", [4, 4, 160], F32)
            negMD = self.sb(st, "ml_negMD", [4, 4, 128], F32)
            blk = self.sb(st, "ml_blk", [4, 2, NB], F32)
            BRW, BRWD, BnMD, Bblk = Buf(), Buf(), Buf(), Buf()
            cols = self.sb(st, "ml_cols", [128, 12], F32)
            Bcols = Buf()
            qT = self.sb(st, "ml_qT", [128, 2, 128], BF16)
            kT = self.sb(st, "ml_kT", [128, 2, 128], BF16)
            BqT, BkT = Buf(), Buf()
            ktok = self.sb(st, "ml_ktok", [128, 256], BF16)
            vext = self.sb(st, "ml_vext", [128, 4, 129], BF16)
            gs = self.sb(st, "ml_gs", [128, 512], BF16)
            Bktok, Bvext, Bgs = Buf(), Buf(), Buf()
            E = [self.sb(st, "ml_E%d" % i, [128, 128], BF16) for i in range(2)]
            PT = [self.sb(st, "ml_PT%d" % i, [128, 128], BF16) for i in range(2)]
            BE, BPT = [Buf(), Buf()], [Buf(), Buf()]
            qsT = self.sb(st, "ml_qsT", [128, 128], BF16)
            qsx = [self.sb(st, "ml_qsx%d" % i, [128, 128], BF16) for i in range(2)]
            Bqsxj = [Buf(), Buf()]
            Cbj = [self.sb(st, "ml_Cbj%d" % i, [128, 129], BF16) for i in range(2)]
            BCbj = [Buf(), Buf()]
            kwxj = [self.sb(st, "ml_kwxj%d" % i, [128, 64], BF16) for i in range(2)]
            Bkwxj = [Buf(), Buf()]
            dec = self.sb(st, "ml_dec", [128, NS], F32)
            BqsT, Bqsx, Bdec = Buf(), Buf(), Buf()
            kw = self.sb(st, "ml_kw", [128, 64], BF16)
            Bkw, Bkwx = Buf(), Buf()
            htok = self.sb(st, "ml_htok", [128, 4, 128], F32)
            stats = self.sb(st, "ml_stats", [128, 4, 8], F32)
            mv = self.sb(st, "ml_mv", [128, 4, 2], F32)
            rstd = self.sb(st, "ml_rstd", [128, 4], F32)
            dn = self.sb(st, "ml_dn", [128, 4], F32)
            Bhtok, Bstats, Bmv, Brstd, Bdn = Buf(), Buf(), Buf(), Buf(), Buf()
            sgt = htok[:, :, :].rearrange("p h v -> p (h v)")
            Bsgt = Bhtok
            mltok = self.sb(st, "ml_mltok", [128, 512], BF16)
            Bmltok = Buf()
            mlT = self.sb(st, "ml_mlT", [128, 4, 512], BF16)
            BmlT = Buf()
            Cfp = self.sb(st, "ml_Cfp", [128, 2, 129], F32)
            Cbp = self.sb(st, "ml_Cbp", [128, 2, 129], BF16)
            Cfs = self.sb(st, "ml_Cfs", [128, NS, 129], F32)
            BCfp, BCbp = [Buf(), Buf()], [Buf(), Buf()]
            BCfs = Buf()
            S.op("pool", lambda h: h.memset(Cfp[:], 0.0), writes=BCfp)
            S.op("pool", lambda h: h.memset(Cbp[:], 0.0), writes=BCbp)
            S.op("pool", lambda h: h.memset(vext[:, :, 128:129], 1.0), writes=[Bvext])
            S.op("pool", lambda h: h.memset(RW[:], 0.0), writes=[BRW])
            id4 = self.identf[0:4, 0:4]
            chunks = self.chunks()
            nprompt = len(chunks) - 1
            grp_start = 0
            for ci, (t0, L, nseq, blen, samp) in enumerate(chunks):
                nblk = L // blen
                Bxb = self.tile_bufs(self.Bxbf, t0, L)
                Gc, Gp = G[ci % 2], G[(ci + 1) % 2]
                BGc, BGp = BG[ci % 2], BG[(ci + 1) % 2]
                xk = lambda kc: self.xbf[:, kc, t0:t0 + L]
                pg = self.psbank()
                for kc in range(KD):
                    self.mm(pg[0][0:4, 0:L], wv[:, kc, 1536:1540], xk(kc), kc == 0, kc == KD - 1, Bxb + [Bwi[kc]], [pg[1]])
                for kc in range(KD):
                    self.mm(pg[0][0:4, 128:128 + L], wv[:, kc, 1540:1544], xk(kc), kc == 0, kc == KD - 1, Bxb + [Bwi[kc]], [pg[1]])
                S.op("act", lambda h, Gc=Gc, pg=pg, L=L: h.activation(out=Gc[:, IG, 0:L], in_=pg[0][0:4, 0:L], func=AF.Identity, bias=gb[:, 0:1], scale=1.0),
                     reads=[pg[1]] + Bpar, writes=[BGc])
                S.op("act", lambda h, Gc=Gc, pg=pg, L=L: h.activation(out=Gc[:, TMP, 0:L], in_=pg[0][0:4, 128:128 + L], func=AF.Exp, bias=gb[:, 1:2], scale=-1.0),
                     reads=[pg[1]] + Bpar, writes=[BGc])
                S.op("act", lambda h, Gc=Gc, L=L: h.activation(out=Gc[:, SP, 0:L], in_=Gc[:, TMP, 0:L], func=AF.Ln, bias=1.0, scale=1.0),
                     reads=[BGc], writes=[BGc])
                pq = self.psbank()
                for i4 in range(4):
                    for kc in range(KD):
                        self.mm(pq[0][:, i4 * 128:i4 * 128 + L], wv[:, kc, i4 * 128:(i4 + 1) * 128], xk(kc), kc == 0, kc == KD - 1,
                                Bxb + [Bwi[kc]], [pq[1]])
                pqv = pq[0][:, :].rearrange("p (c t) -> p c t", t=128)
                S.op("act", lambda h, pqv=pqv, L=L: h.activation(out=qT[:, :, 0:L], in_=pqv[:, 0:2, 0:L], func=AF.Copy), reads=[pq[1]], writes=[BqT])
                S.op("act", lambda h, pqv=pqv, L=L: h.mul(kT[:, :, 0:L], pqv[:, 2:4, 0:L], 0.125), reads=[pq[1]], writes=[BkT])
                pk = self.psbank()
                for kc in range(KD):
                    self.mm(pk[0][0:L, 0:256], self.xbf[:, kc, t0:t0 + L], wv[:, kc, 256:512], kc == 0, kc == KD - 1, Bxb + [Bwi[kc]], [pk[1]])
                S.op("act", lambda h, pk=pk, L=L: h.mul(ktok[0:L, :], pk[0][0:L, 0:256], 0.125), reads=[pk[1]], writes=[Bktok])
                pv_ = self.psbank()
                for kc in range(KD):
                    self.mm(pv_[0][0:L, :], self.xbf[:, kc, t0:t0 + L], wv[:, kc, 512:1024], kc == 0, kc == KD - 1, Bxb + [Bwi[kc]], [pv_[1]])
                S.op("dve", lambda h, pv_=pv_, L=L: h.tensor_copy(vext[0:L, :, 0:128], pv_[0][0:L, :].rearrange("p (h v) -> p h v", v=128)),
                     reads=[pv_[1]], writes=[Bvext])
                po = self.psbank()
                for kc in range(KD):
                    self.mm(po[0][0:L, :], self.xbf[:, kc, t0:t0 + L], wv[:, kc, 1024:1536], kc == 0, kc == KD - 1, Bxb + [Bwi[kc]], [po[1]])
                S.op("act", lambda h, po=po, L=L: h.activation(out=sgt[0:L, :], in_=po[0][0:L, :], func=AF.Exp, scale=-1.0), reads=[po[1]], writes=[Bsgt])
                S.op("act", lambda h, L=L: h.activation(out=sgt[0:L, :], in_=sgt[0:L, :], func=AF.Ln, bias=1.0, scale=1.0), reads=[Bsgt], writes=[Bsgt])
                S.op("act", lambda h, L=L: h.activation(out=gs[0:L, :], in_=sgt[0:L, :], func=AF.Exp, scale=-1.0), reads=[Bsgt], writes=[Bgs])
                S.op("dve", lambda h, L=L: h.tensor_tensor(gs[0:L, :], gs[0:L, :], gnorm[0:L, :], ALU.mult), reads=[Bgs] + Bpar, writes=[Bgs])
                if not samp:
                    ini_c = 0.0 if ci == 0 else carry[:, 0:1]
                    ini_m = 0.0 if ci == 0 else carry[:, 1:2]
                    S.op("dve", lambda h, Gc=Gc, ini_c=ini_c, L=L: h.tensor_tensor_scan(Gc[:, CSP, 0:L], onesf[0:4, 0:L], Gc[:, SP, 0:L], ini_c, ALU.mult, ALU.add),
                         reads=[BGc, Bcarry, Bmc], writes=[BGc])
                    S.op("dve", lambda h, Gc=Gc, L=L: h.tensor_tensor(Gc[:, A_, 0:L], Gc[:, IG, 0:L], Gc[:, CSP, 0:L], ALU.add), reads=[BGc], writes=[BGc])
                    S.op("dve", lambda h, Gc=Gc, ini_m=ini_m, L=L: h.tensor_tensor_scan(Gc[:, M_, 0:L], Gc[:, A_, 0:L], Gc[:, A_, 0:L], ini_m, ALU.max, ALU.max),
                         reads=[BGc, Bcarry], writes=[BGc])
                    Mprev = zc[:, 0:1] if ci == 0 else carry[:, 1:2]
                    Me = Gc[:, M_, L - 1:L]
                    cspe = Gc[:, CSP, L - 1:L]
                else:
                    v3 = lambda slot, Gc=Gc: Gc[:, slot, 0:L].rearrange("p (s t) -> p s t", t=blen)
                    S.op("dve", lambda h, v3=v3: h.tensor_copy(v3(CSP)[:, :, 0:1], v3(SP)[:, :, 0:1]), reads=[BGc], writes=[BGc])
                    for t in range(1, blen):
                        S.op("dve", lambda h, v3=v3, t=t: h.tensor_tensor(v3(CSP)[:, :, t:t + 1], v3(CSP)[:, :, t - 1:t], v3(SP)[:, :, t:t + 1], ALU.add),
                             reads=[BGc], writes=[BGc])
                    S.op("dve", lambda h, Gc=Gc, L=L: h.tensor_tensor(Gc[:, A_, 0:L], Gc[:, IG, 0:L], Gc[:, CSP, 0:L], ALU.add), reads=[BGc], writes=[BGc])
                    S.op("dve", lambda h, v3=v3: h.tensor_tensor(v3(M_)[:, :, 0:1], v3(A_)[:, :, 0:1], m0T[:, :].unsqueeze(2), ALU.max),
                         reads=[BGc] + Bpar, writes=[BGc])
                    for t in range(1, blen):
                        S.op("dve", lambda h, v3=v3, t=t: h.tensor_tensor(v3(M_)[:, :, t:t + 1], v3(M_)[:, :, t - 1:t], v3(A_)[:, :, t:t + 1], ALU.max),
                             reads=[BGc], writes=[BGc])
                    Mprev = m0T[:, 0:nblk]
                    Me = v3(M_)[:, :, blen - 1]
                    cspe = v3(CSP)[:, :, blen - 1]
                b3 = lambda ap, L=L, nblk=nblk, blen=blen: ap.unsqueeze(2).broadcast_to([4, nblk, blen])
                g3 = lambda slot, Gc=Gc, L=L, blen=blen: Gc[:, slot, 0:L].rearrange("p (s t) -> p s t", t=blen)
                rdg = [BGc, Bcarry] + Bpar
                S.op("dve", lambda h, g3=g3, b3=b3, Mprev=Mprev: h.tensor_tensor(g3(TMP), b3(Mprev), g3(M_), ALU.subtract), reads=rdg, writes=[BGc])
                S.op("act", lambda h, Gc=Gc, L=L: h.activation(out=RW[:, 0:L], in_=Gc[:, TMP, 0:L], func=AF.Exp), reads=[BGc], writes=[BRW])
                S.op("dve", lambda h, g3=g3, b3=b3, Me=Me: h.tensor_tensor(g3(TMP), g3(A_), b3(Me), ALU.subtract), reads=rdg, writes=[BGc])
                S.op("act", lambda h, Gc=Gc, L=L: h.activation(out=Gc[:, WE, 0:L], in_=Gc[:, TMP, 0:L], func=AF.Exp), reads=[BGc], writes=[BGc])
                S.op("dve", lambda h, Gc=Gc, L=L: h.tensor_tensor(Gc[:, TMP, 0:L], Gc[:, CSP, 0:L], Gc[:, M_, 0:L], ALU.subtract), reads=[BGc], writes=[BGc])
                S.op("act", lambda h, Gc=Gc, L=L: h.activation(out=Gc[:, ENM, 0:L], in_=Gc[:, TMP, 0:L], func=AF.Exp), reads=[BGc], writes=[BGc])
                S.op("dve", lambda h, Gc=Gc, L=L: h.tensor_scalar(Gc[:, NEGM, 0:L], Gc[:, M_, 0:L], -1.0, None, ALU.mult), reads=[BGc], writes=[BGc])
                S.op("dve", lambda h, Mprev=Mprev, Me=Me, nblk=nblk: h.tensor_tensor(blk[:, 0, 0:nblk], Mprev, Me, ALU.subtract), reads=rdg, writes=[Bblk])
                S.op("act", lambda h, nblk=nblk: h.activation(out=RW[:, 128:128 + nblk], in_=blk[:, 0, 0:nblk], func=AF.Exp), reads=[Bblk], writes=[BRW])
                S.op("dve", lambda h, Me=Me, cspe=cspe, nblk=nblk: h.tensor_tensor(blk[:, 1, 0:nblk], Me, cspe, ALU.subtract), reads=rdg, writes=[Bblk])
                if samp:
                    S.op("sp", lambda h: h.dma_start(out=d["s_m"][j].rearrange("s h -> h s"), in_=blk[:, 1, 0:NS], allow_slow_non_contiguous=True),
                         reads=[Bblk], dma=True)
                elif ci == nprompt - 1:
                    S.op("sp", lambda h: h.dma_start(out=d["p_m"][j].rearrange("(h o) -> h o", o=1), in_=blk[:, 1, 0:1]), reads=[Bblk], dma=True)
                S.op("dve", lambda h, Gc=Gc, L=L: h.tensor_tensor(negMD[:, :, 0:L], Gc[:, NEGM, 0:L].unsqueeze(1).broadcast_to([4, 4, L]),
                                                                 id4.unsqueeze(2).broadcast_to([4, 4, L]), ALU.mult),
                     reads=[BGc, self.Bconst], writes=[BnMD])
                S.op("dve", lambda h: h.tensor_tensor(RWD[:, :, :], RW[:, :].unsqueeze(1).broadcast_to([4, 4, 160]),
                                                      id4.unsqueeze(2).broadcast_to([4, 4, 160]), ALU.mult),
                     reads=[BRW, self.Bconst], writes=[BRWD])
                pc = self.psbank()
                for qi, slot in enumerate((A_, WE, ENM)):
                    self.mm(pc[0][0:L, qi * 4:qi * 4 + 4], Gc[:, slot, 0:L], id4, True, True, [BGc, self.Bconst], [pc[1]])
                S.op("dve", lambda h, pc=pc, L=L: h.tensor_copy(cols[0:L, :], pc[0][0:L, 0:12]), reads=[pc[1]], writes=[Bcols])
                mneg = MC["mneg_s"] if samp else MC["mneg_p"]
                for p in range(2):
                    if samp:
                        BCf, BCb = BCfs, None
                        srcC = d["st_mC"][j][:, 2 * p:2 * p + 2, :, :].rearrange("s hh d v -> (hh d) s v")
                        srcn = d["st_mn"][j][:, 2 * p:2 * p + 2, :].rearrange("s hh d -> (hh d) s")
                        for q4 in range(0, NS, 4):
                            S.op("sp", lambda h, srcC=srcC, q4=q4: h.dma_start(out=Cfs[:, q4:q4 + 4, 0:128], in_=srcC[:, q4:q4 + 4, :]), writes=[BCfs], dma=True)
                        for q4 in range(0, NS, 4):
                            S.op("sp", lambda h, srcn=srcn, q4=q4: h.dma_start(out=Cfs[:, q4:q4 + 4, 128:129], in_=srcn[:, q4:q4 + 4].unsqueeze(2), allow_slow_non_contiguous=True),
                                 reads=[BCfs], writes=[BCfs], dma=True)
                        Cfv, Cbv = Cfs[:, :, :], None
                    else:
                        BCf, BCb = BCfp[p], BCbp[p]
                        Cfv = Cfp[:, p:p + 1, :]
                        Cbv = Cbp[:, p:p + 1, :]
                    pw = self.psbank()
                    for hh in range(2):
                        self.mm(pw[0][hh * 64:(hh + 1) * 64, 0:160], onesf[0:4, 0:64], RWD[:, 2 * p + hh, :], True, True, [BRWD, Bmc], [pw[1]])
                    S.op("dve", lambda h, pw=pw, p=p, L=L: h.tensor_tensor(qsT[:, 0:L], qT[:, p, 0:L], pw[0][:, 0:L], ALU.mult), reads=[pw[1], BqT], writes=[BqsT])
                    S.op("act", lambda h, pw=pw, nseq=nseq: h.activation(out=dec[:, 0:nseq], in_=pw[0][:, 128:128 + nseq], func=AF.Copy), reads=[pw[1]], writes=[Bdec])
                    PN = [(self.pb[4], self.pbB[4]), (self.pb[5], self.pbB[5])]
                    pab = []
                    for hh in range(2):
                        hd = 2 * p + hh
                        o = hh * 64
                        pa = self.psbank()
                        self.mm(pa[0][0:L, 0:L], kT[o:o + 64, p, 0:L], qT[o:o + 64, p, 0:L], True, True, [BkT, BqT], [pa[1]])
                        pb_ = self.psbank()
                        self.mm(pb_[0][0:L, 0:L], onesf[0:4, 0:L], negMD[:, hd, 0:L], True, False, [BnMD, Bmc], [pb_[1]])
                        self.mm(pb_[0][0:L, 0:L], self.ident[0:L, 0:L], mneg[0:L, 0:L], False, True, [Bmc, self.Bconst], [pb_[1]])
                        pab.append((pa, pb_))
                    for hh in range(2):
                        hd = 2 * p + hh
                        o = hh * 64
                        e_, Be_ = E[hd % 2], BE[hd % 2]
                        pt_, Bpt_ = PT[hd % 2], BPT[hd % 2]
                        pa, pb_ = pab[hh]
                        S.op("act", lambda h, e_=e_, pb_=pb_, hd=hd, L=L: h.activation(out=e_[0:L, 0:L], in_=pb_[0][0:L, 0:L], func=AF.Exp, bias=cols[0:L, hd:hd + 1], scale=1.0),
                             reads=[pb_[1], Bcols], writes=[Be_])
                        S.op("dve", lambda h, e_=e_, pt_=pt_, pa=pa, L=L: h.tensor_tensor(pt_[0:L, 0:L], e_[0:L, 0:L], pa[0][0:L, 0:L], ALU.mult),
                             reads=[Be_, pa[1]], writes=[Bpt_])
                        self.mm(PN[hh][0][0:L, 0:129], pt_[0:L, 0:L], vext[0:L, hd, :], True, False, [Bpt_, Bvext], [PN[hh][1]])
                    for jj in range(nseq):
                        if samp:
                            qx_, Bqx_ = qsx[jj % 2], Bqsxj[jj % 2]
                            cb_, Bcb_ = Cbj[jj % 2], BCbj[jj % 2]
                            S.op("dve", lambda h, qx_=qx_, jj=jj, L=L: h.tensor_tensor(qx_[:, 0:L], qsT[:, 0:L], MC["qmask"][:, jj, 0:L], ALU.mult),
                                 reads=[BqsT, Bmc], writes=[Bqx_])
                            S.op("act", lambda h, cb_=cb_, jj=jj: h.activation(out=cb_[:, :], in_=Cfs[:, jj, :], func=AF.Copy), reads=[BCfs], writes=[Bcb_])
                        for hh in range(2):
                            o = hh * 64
                            if samp:
                                self.mm(PN[hh][0][0:L, 0:129], qx_[o:o + 64, 0:L], cb_[o:o + 64, :], False, jj == nseq - 1, [Bqx_, Bcb_], [PN[hh][1]])
                            else:
                                self.mm(PN[hh][0][0:L, 0:129], qsT[o:o + 64, 0:L], Cbv[o:o + 64, jj, :], False, jj == nseq - 1, [BqsT, BCb], [PN[hh][1]])
                    for hh in range(2):
                        hd = 2 * p + hh
                        o = hh * 64
                        pn = PN[hh]
                        S.op("act", lambda h, pn=pn, hd=hd, L=L: h.activation(out=dn[0:L, hd:hd + 1], in_=pn[0][0:L, 128:129], func=AF.Abs),
                             reads=[pn[1]], writes=[Bdn])
                        S.op("dve", lambda h, hd=hd, L=L: h.tensor_tensor(dn[0:L, hd:hd + 1], dn[0:L, hd:hd + 1], cols[0:L, 8 + hd:9 + hd], ALU.max),
                             reads=[Bdn, Bcols], writes=[Bdn])
                        S.op("dve", lambda h, hd=hd, L=L: h.reciprocal(dn[0:L, hd:hd + 1], dn[0:L, hd:hd + 1]), reads=[Bdn], writes=[Bdn])
                        S.op("dve", lambda h, pn=pn, hd=hd, L=L: h.tensor_scalar(htok[0:L, hd, :], pn[0][0:L, 0:128], dn[0:L, hd:hd + 1], None, ALU.mult),
                             reads=[pn[1], Bdn], writes=[Bhtok])
                        S.op("dve", lambda h, hd=hd, L=L: h.bn_stats(stats[0:L, hd, 0:6], htok[0:L, hd, :]), reads=[Bhtok], writes=[Bstats])
                        S.op("dve", lambda h, hd=hd, L=L: h.bn_aggr(mv[0:L, hd, :], stats[0:L, hd, 0:6]), reads=[Bstats], writes=[Bmv])
                        S.op("dve", lambda h, hd=hd, L=L: h.tensor_scalar(kw[0:L, :], ktok[0:L, hd * 64:(hd + 1) * 64], cols[0:L, 4 + hd:5 + hd], None, ALU.mult),
                             reads=[Bktok, Bcols], writes=[Bkw])
                        for r0 in range(0, nseq, 4):
                            nr = min(4, nseq - r0)
                            pu = self.psbank()
                            for jj in range(r0, r0 + nr):
                                if samp:
                                    kx_, Bkx_ = kwxj[jj % 2], Bkwxj[jj % 2]
                                    S.op("dve", lambda h, kx_=kx_, jj=jj, L=L: h.tensor_scalar(kx_[0:L, :], kw[0:L, :], MC["tokmask"][0:L, jj:jj + 1], None, ALU.mult),
                                         reads=[Bkw, Bmc], writes=[Bkx_])
                                    self.mm(pu[0][o:o + 64, (jj - r0) * 128:(jj - r0 + 1) * 128], kx_[0:L, :], vext[0:L, hd, 0:128], True, True, [Bkx_, Bvext], [pu[1]])
                                else:
                                    self.mm(pu[0][o:o + 64, (jj - r0) * 128:(jj - r0 + 1) * 128], kw[0:L, :], vext[0:L, hd, 0:128], True, True, [Bkw, Bvext], [pu[1]])
                            cf = Cfv[o:o + 64, r0:r0 + nr, 0:128]
                            S.op("dve", lambda h, cf=cf, r0=r0, nr=nr, o=o: h.tensor_tensor(cf, cf, dec[o:o + 64, r0:r0 + nr].unsqueeze(2).broadcast_to([64, nr, 128]), ALU.mult),
                                 reads=[BCf, Bdec], writes=[BCf])
                            S.op("dve", lambda h, cf=cf, pu=pu, nr=nr, o=o: h.tensor_tensor(cf, cf, pu[0][o:o + 64, 0:nr * 128].rearrange("p (j v) -> p j v", v=128), ALU.add),
                                 reads=[BCf, pu[1]], writes=[BCf])
                        pn2 = self.psbank()
                        rhs_n = MC["tokmask"][0:L, 0:nseq] if samp else self.ones_bf[0:L, 0:1]
                        self.mm(pn2[0][o:o + 64, 0:nseq], kw[0:L, :], rhs_n, True, True, [Bkw, Bmc, self.Bconst], [pn2[1]])
                        cn = Cfv[o:o + 64, 0:nseq, 128:129]
                        S.op("dve", lambda h, cn=cn, nseq=nseq, o=o: h.tensor_tensor(cn, cn, dec[o:o + 64, 0:nseq].unsqueeze(2), ALU.mult), reads=[BCf, Bdec], writes=[BCf])
                        S.op("dve", lambda h, cn=cn, pn2=pn2, nseq=nseq, o=o: h.tensor_tensor(cn, cn, pn2[0][o:o + 64, 0:nseq].unsqueeze(2), ALU.add),
                             reads=[BCf, pn2[1]], writes=[BCf])
                    if not samp:
                        S.op("act", lambda h, Cfv=Cfv, Cbv=Cbv: h.activation(out=Cbv, in_=Cfv, func=AF.Copy), reads=[BCf], writes=[BCb])
                    if samp:
                        dstC = d["s_C"][j][:, 2 * p:2 * p + 2, :, :].rearrange("s hh d v -> (hh d) s v")
                        dstn = d["s_n"][j][:, 2 * p:2 * p + 2, :].rearrange("s hh d -> (hh d) s")
                        for q4 in range(0, NS, 4):
                            S.op("sp", lambda h, dstC=dstC, q4=q4: h.dma_start(out=dstC[:, q4:q4 + 4, :], in_=Cfs[:, q4:q4 + 4, 0:128]), reads=[BCfs], dma=True)
                        for q4 in range(0, NS, 4):
                            S.op("sp", lambda h, dstn=dstn, q4=q4: h.dma_start(out=dstn[:, q4:q4 + 4].unsqueeze(2), in_=Cfs[:, q4:q4 + 4, 128:129], allow_slow_non_contiguous=True),
                                 reads=[BCfs], dma=True)
                    elif ci == nprompt - 1:
                        dstC = d["p_C"][j][2 * p:2 * p + 2, :, :].rearrange("hh d v -> (hh d) v")
                        dstn = d["p_n"][j][2 * p:2 * p + 2, :].rearrange("hh (d o) -> (hh d) o", o=1)
                        S.op("sp", lambda h, dstC=dstC, p=p: h.dma_start(out=dstC, in_=Cfp[:, p, 0:128]), reads=[BCfp[p]], dma=True)
                        S.op("sp", lambda h, dstn=dstn, p=p: h.dma_start(out=dstn, in_=Cfp[:, p, 128:129]), reads=[BCfp[p]], dma=True)
                S.op("dve", lambda h, L=L: h.tensor_scalar(rstd[0:L, :], mv[0:L, :, 1], LN_EPS, None, ALU.add), reads=[Bmv], writes=[Brstd])
                S.op("act", lambda h, L=L: h.activation(out=rstd[0:L, :], in_=rstd[0:L, :], func=AF.Ln), reads=[Brstd], writes=[Brstd])
                S.op("act", lambda h, L=L: h.activation(out=rstd[0:L, :], in_=rstd[0:L, :], func=AF.Exp, scale=-0.5), reads=[Brstd], writes=[Brstd])
                for hd in range(4):
                    S.op("dve", lambda h, hd=hd, L=L: h.tensor_scalar(htok[0:L, hd, :], htok[0:L, hd, :], mv[0:L, hd, 0:1], rstd[0:L, hd:hd + 1], ALU.subtract, ALU.mult),
                         reads=[Bhtok, Bmv, Brstd], writes=[Bhtok])
                S.op("dve", lambda h, L=L: h.tensor_tensor(mltok[0:L, :], htok[0:L, :, :].rearrange("p h v -> p (h v)"), gs[0:L, :], ALU.mult),
                     reads=[Bhtok, Bgs], writes=[Bmltok])
                S.op("dve", lambda h, Gc=Gc, L=L: h.tensor_copy(carry[:, 0:1], Gc[:, CSP, L - 1:L]), reads=[BGc], writes=[Bcarry])
                S.op("dve", lambda h, Gc=Gc, L=L: h.tensor_copy(carry[:, 1:2], Gc[:, M_, L - 1:L]), reads=[BGc], writes=[Bcarry])
                gs0 = (t0 // 512) * 512 if t0 < cfg.seq else t0
                toff = t0 - gs0
                for hd in range(4):
                    self.tr(self.pbf[:, hd * 128:hd * 128 + L], mltok[0:L, hd * 128:(hd + 1) * 128], self.ident[0:L, 0:L], [Bmltok, self.Bconst], [self.BpbfB])
                S.op("act", lambda h, L=L, toff=toff: h.activation(out=mlT[:, :, toff:toff + L], in_=self.pbf[:, 0:512].rearrange("p (h t) -> p h t", t=128)[:, :, 0:L], func=AF.Copy),
                     reads=[self.BpbfB], writes=[BmlT])
                gend = t0 + L
                if t0 >= cfg.seq or gend % 512 == 0 or gend == cfg.seq:
                    self.out_proj(wo, Bwo, 4, mlT, BmlT, gs0, gend - gs0, first)
        S.barrier()

    def gla_phase(self, l, j, first):
        cfg, S, d = self.cfg, self.S, self.dram
        NS = cfg.nseq
        with ExitStack() as st:
            MC = self.mixer_consts(st)
            Bmc = MC["B"]
            onesf = MC["onesf"]
            w_in = d["ab_w_in"][j]
            w_out = d["ab_w_out"][j]
            (wv, wo), (Bwi, Bwo) = self.take_w()
            wa2 = self.sb(st, "gl_wa2", [16, 256], BF16)
            nba = self.sb(st, "gl_nba", [128, 2], F32)
            gnorm = self.sb(st, "gl_gnorm", [128, 512], BF16)
            Bq1, Bq2, Bq3 = Buf(), Buf(), Buf()
            S.op("pool", lambda h: h.dma_start(out=wa2[:], in_=d["ab_gla_wa2"][j]), writes=[Bq1], dma=True)
            S.op("sp", lambda h: h.dma_start(out=nba[:], in_=d["ab_gla_ba"][j].rearrange("(c p) -> p c", p=128), allow_slow_non_contiguous=True),
                 writes=[Bq2], dma=True)
            S.op("dve", lambda h: h.tensor_scalar(nba[:], nba[:], -1.0, None, ALU.mult), reads=[Bq2], writes=[Bq2])
            S.op("pool", lambda h: h.dma_start(out=gnorm[:], in_=d["ab_gla_norm"][j].partition_broadcast(128)), writes=[Bq3], dma=True)
            Bpar = [Bq1, Bq2, Bq3]
            gaT = self.sb(st, "gl_gaT", [16, 128], BF16)
            BgaT = Buf()
            spT = self.sb(st, "gl_spT", [128, 2, 128], F32)
            spc = self.sb(st, "gl_spc", [128, 2, 128], F32)
            eA = self.sb(st, "gl_eA", [128, 2, 128], F32)
            enA = self.sb(st, "gl_enA", [128, 2, 128], F32)
            BspT, Bspc, BeA, BenA = Buf(), Buf(), Buf(), Buf()
            qT = self.sb(st, "gl_qT", [128, 2, 128], BF16)
            kT = self.sb(st, "gl_kT", [128, 2, 128], BF16)
            BqT, BkT = Buf(), Buf()
            ktl = self.sb(st, "gl_ktl", [128, 128], BF16)
            kxj = [self.sb(st, "gl_kxj%d" % i, [128, 128], BF16) for i in range(2)]
            qxj = [self.sb(st, "gl_qxj%d" % i, [128, 128], BF16) for i in range(2)]
            Sbj = [self.sb(st, "gl_Sbj%d" % i, [128, 128], BF16) for i in range(2)]
            Bkxj, Bqxj, BSbj = [Buf(), Buf()], [Buf(), Buf()], [Buf(), Buf()]
            Bktl = Buf()
            vtok = self.sb(st, "gl_vtok", [128, 4, 128], BF16)
            gs = self.sb(st, "gl_gs", [128, 512], BF16)
            Bvtok, Bgs = Buf(), Buf()
            PT = [self.sb(st, "gl_PT%d" % i, [128, 128], BF16) for i in range(2)]
            BPT = [Buf(), Buf()]
            otok = self.sb(st, "gl_otok", [128, 4, 128], F32)
            stats = self.sb(st, "gl_stats", [128, 4, 8], F32)
            mv = self.sb(st, "gl_mv", [128, 4, 2], F32)
            rstd = self.sb(st, "gl_rstd", [128, 4], F32)
            Botok, Bstats, Bmv, Brstd = Buf(), Buf(), Buf(), Buf()
            gltok = self.sb(st, "gl_gltok", [128, 512], BF16)
            Bgltok = Buf()
            glT = self.sb(st, "gl_glT", [128, 4, 512], BF16)
            BglT = Buf()
            Sfp = self.sb(st, "gl_Sfp", [128, 2, 128], F32)
            Sbp = self.sb(st, "gl_Sbp", [128, 2, 128], BF16)
            Sfs = self.sb(st, "gl_Sfs", [128, NS, 128], F32)
            BSfp, BSbp = [Buf(), Buf()], [Buf(), Buf()]
            BSfs = Buf()
            S.op("pool", lambda h: h.memset(Sfp[:], 0.0), writes=BSfp)
            S.op("pool", lambda h: h.memset(Sbp[:], 0.0), writes=BSbp)
            chunks = self.chunks()
            nprompt = len(chunks) - 1
            grp_start = 0
            for ci, (t0, L, nseq, blen, samp) in enumerate(chunks):
                nblk = L // blen
                Bxb = self.tile_bufs(self.Bxbf, t0, L)
                xk = lambda kc: self.xbf[:, kc, t0:t0 + L]
                pg = self.psbank()
                for kc in range(KD):
                    self.mm(pg[0][0:16, 0:L], wv[:, kc, 1536:1552], xk(kc), kc == 0, kc == KD - 1, Bxb + [Bwi[kc]], [pg[1]])
                S.op("act", lambda h, pg=pg, L=L: h.activation(out=gaT[:, 0:L], in_=pg[0][0:16, 0:L], func=AF.Copy), reads=[pg[1]], writes=[BgaT])
                pz = self.psbank()
                for p in range(2):
                    self.mm(pz[0][:, p * 128:p * 128 + L], wa2[:, p * 128:(p + 1) * 128], gaT[:, 0:L], True, True, [BgaT] + Bpar, [pz[1]])
                for p in range(2):
                    S.op("act", lambda h, pz=pz, p=p, L=L: h.activation(out=spT[:, p, 0:L], in_=pz[0][:, p * 128:p * 128 + L], func=AF.Exp, bias=nba[:, p:p + 1], scale=-1.0),
                         reads=[pz[1]] + Bpar, writes=[BspT])
                S.op("act", lambda h, L=L: h.activation(out=spT[:, :, 0:L], in_=spT[:, :, 0:L], func=AF.Ln, bias=1.0, scale=1.0), reads=[BspT], writes=[BspT])
                for p in range(2):
                    d0 = MC["rst_s"][:, 0:L] if samp else onesf[:, 0:L]
                    S.op("dve", lambda h, p=p, d0=d0, L=L: h.tensor_tensor_scan(spc[:, p, 0:L], d0, spT[:, p, 0:L], 0.0, ALU.mult, ALU.add),
                         reads=[BspT, Bmc], writes=[Bspc])
                S.op("act", lambda h, L=L: h.activation(out=eA[:, :, 0:L], in_=spc[:, :, 0:L], func=AF.Exp, scale=-1.0 / 16.0), reads=[Bspc], writes=[BeA])
                S.op("act", lambda h, L=L: h.activation(out=enA[:, :, 0:L], in_=spc[:, :, 0:L], func=AF.Exp, scale=1.0 / 16.0), reads=[Bspc], writes=[BenA])
                pq = self.psbank()
                for i4 in range(4):
                    for kc in range(KD):
                        self.mm(pq[0][:, i4 * 128:i4 * 128 + L], wv[:, kc, i4 * 128:(i4 + 1) * 128], xk(kc), kc == 0, kc == KD - 1,
                                Bxb + [Bwi[kc]], [pq[1]])
                pqv = pq[0][:, :].rearrange("p (c t) -> p c t", t=128)
                S.op("dve", lambda h, pqv=pqv, L=L: h.scalar_tensor_tensor(qT[:, :, 0:L], pqv[:, 0:2, 0:L], 0.125, eA[:, :, 0:L], ALU.mult, ALU.mult),
                     reads=[pq[1], BeA], writes=[BqT])
                S.op("dve", lambda h, pqv=pqv, L=L: h.tensor_tensor(kT[:, :, 0:L], pqv[:, 2:4, 0:L], enA[:, :, 0:L], ALU.mult),
                     reads=[pq[1], BenA], writes=[BkT])
                pv_ = self.psbank()
                for kc in range(KD):
                    self.mm(pv_[0][0:L, :], self.xbf[:, kc, t0:t0 + L], wv[:, kc, 512:1024], kc == 0, kc == KD - 1, Bxb + [Bwi[kc]], [pv_[1]])
                S.op("act", lambda h, pv_=pv_, L=L: h.activation(out=vtok[0:L, :, :], in_=pv_[0][0:L, :].rearrange("p (h v) -> p h v", v=128), func=AF.Copy),
                     reads=[pv_[1]], writes=[Bvtok])
                po = self.psbank()
                for kc in range(KD):
                    self.mm(po[0][0:L, :], self.xbf[:, kc, t0:t0 + L], wv[:, kc, 1024:1536], kc == 0, kc == KD - 1, Bxb + [Bwi[kc]], [po[1]])
                S.op("act", lambda h, po=po, L=L: h.activation(out=gs[0:L, :], in_=po[0][0:L, :], func=AF.Silu), reads=[po[1]], writes=[Bgs])
                S.op("dve", lambda h, L=L: h.tensor_tensor(gs[0:L, :], gs[0:L, :], gnorm[0:L, :], ALU.mult), reads=[Bgs] + Bpar, writes=[Bgs])
                m01 = MC["m01_s"] if samp else MC["m01_p"]
                for p in range(2):
                    if samp:
                        BSf, BSb = BSfs, None
                        srcS = d["st_gS"][j][:, 2 * p:2 * p + 2, :, :].rearrange("s hh d v -> (hh d) s v")
                        for q4 in range(0, NS, 4):
                            S.op("sp", lambda h, srcS=srcS, q4=q4: h.dma_start(out=Sfs[:, q4:q4 + 4, :], in_=srcS[:, q4:q4 + 4, :]), writes=[BSfs], dma=True)
                        Sfv, Sbv = Sfs[:, :, :], None
                    else:
                        BSf, BSb = BSfp[p], BSbp[p]
                        Sfv = Sfp[:, p:p + 1, :]
                        Sbv = Sbp[:, p:p + 1, :]
                    self.tr(self.pbf[0:L, 0:128], kT[:, p, 0:L], self.ident[:, :], [BkT, self.Bconst], [self.BpbfB])
                    S.op("act", lambda h, L=L: h.activation(out=ktl[0:L, :], in_=self.pbf[0:L, 0:128], func=AF.Copy), reads=[self.BpbfB], writes=[Bktl])
                    eAL = eA[:, p, 0:L].rearrange("p (s t) -> p s t", t=blen)[:, :, blen - 1]
                    PN = [(self.pb[4], self.pbB[4]), (self.pb[5], self.pbB[5])]
                    pas = []
                    for hh in range(2):
                        o = hh * 64
                        pa = self.psbank()
                        self.mm(pa[0][0:L, 0:L], kT[o:o + 64, p, 0:L], qT[o:o + 64, p, 0:L], True, True, [BkT, BqT], [pa[1]])
                        pas.append(pa)
                    for hh in range(2):
                        hd = 2 * p + hh
                        o = hh * 64
                        pt_, Bpt_ = PT[hd % 2], BPT[hd % 2]
                        pa = pas[hh]
                        S.op("dve", lambda h, pt_=pt_, pa=pa, L=L, m01=m01: h.tensor_tensor(pt_[0:L, 0:L], pa[0][0:L, 0:L], m01[0:L, 0:L], ALU.mult),
                             reads=[pa[1], Bmc], writes=[Bpt_])
                        self.mm(PN[hh][0][0:L, 0:128], pt_[0:L, 0:L], vtok[0:L, hd, :], True, False, [Bpt_, Bvtok], [PN[hh][1]])
                    for jj in range(nseq):
                        if samp:
                            qx_, Bqx_ = qxj[jj % 2], Bqxj[jj % 2]
                            sb_, Bsb_ = Sbj[jj % 2], BSbj[jj % 2]
                            S.op("dve", lambda h, qx_=qx_, jj=jj, p=p, L=L: h.tensor_tensor(qx_[:, 0:L], qT[:, p, 0:L], MC["qmask"][:, jj, 0:L], ALU.mult),
                                 reads=[BqT, Bmc], writes=[Bqx_])
                            S.op("act", lambda h, sb_=sb_, jj=jj: h.activation(out=sb_[:, :], in_=Sfs[:, jj, :], func=AF.Copy), reads=[BSfs], writes=[Bsb_])
                        for hh in range(2):
                            o = hh * 64
                            if samp:
                                self.mm(PN[hh][0][0:L, 0:128], qx_[o:o + 64, 0:L], sb_[o:o + 64, :], False, jj == nseq - 1, [Bqx_, Bsb_], [PN[hh][1]])
                            else:
                                self.mm(PN[hh][0][0:L, 0:128], qT[o:o + 64, p, 0:L], Sbv[o:o + 64, jj, :], False, jj == nseq - 1, [BqT, BSb], [PN[hh][1]])
                    for hh in range(2):
                        hd = 2 * p + hh
                        o = hh * 64
                        pn = PN[hh]
                        S.op("act", lambda h, pn=pn, hd=hd, L=L: h.activation(out=otok[0:L, hd, :], in_=pn[0][0:L, 0:128], func=AF.Copy), reads=[pn[1]], writes=[Botok])
                        S.op("dve", lambda h, hd=hd, L=L: h.bn_stats(stats[0:L, hd, 0:6], otok[0:L, hd, :]), reads=[Botok], writes=[Bstats])
                        S.op("dve", lambda h, hd=hd, L=L: h.bn_aggr(mv[0:L, hd, :], stats[0:L, hd, 0:6]), reads=[Bstats], writes=[Bmv])
                        for r0 in range(0, nseq, 4):
                            nr = min(4, nseq - r0)
                            pu = self.psbank()
                            for jj in range(r0, r0 + nr):
                                if samp:
                                    kx_, Bkx_ = kxj[jj % 2], Bkxj[jj % 2]
                                    S.op("dve", lambda h, kx_=kx_, jj=jj, o=o, L=L: h.tensor_scalar(kx_[0:L, 0:64], ktl[0:L, o:o + 64], MC["tokmask"][0:L, jj:jj + 1], None, ALU.mult),
                                         reads=[Bktl, Bmc], writes=[Bkx_])
                                    self.mm(pu[0][o:o + 64, (jj - r0) * 128:(jj - r0 + 1) * 128], kx_[0:L, 0:64], vtok[0:L, hd, :], True, True, [Bkx_, Bvtok], [pu[1]])
                                else:
                                    self.mm(pu[0][o:o + 64, (jj - r0) * 128:(jj - r0 + 1) * 128], ktl[0:L, o:o + 64], vtok[0:L, hd, :], True, True, [Bktl, Bvtok], [pu[1]])
                            sf = Sfv[o:o + 64, r0:r0 + nr, :]
                            S.op("dve", lambda h, sf=sf, pu=pu, nr=nr, o=o: h.tensor_tensor(sf, sf, pu[0][o:o + 64, 0:nr * 128].rearrange("p (j v) -> p j v", v=128), ALU.add),
                                 reads=[BSf, pu[1]], writes=[BSf])
                            S.op("dve", lambda h, sf=sf, eAL=eAL, r0=r0, nr=nr, o=o: h.tensor_tensor(sf, sf, eAL[o:o + 64, r0:r0 + nr].unsqueeze(2).broadcast_to([64, nr, 128]), ALU.mult),
                                 reads=[BSf, BeA], writes=[BSf])
                    if not samp:
                        S.op("act", lambda h, Sfv=Sfv, Sbv=Sbv: h.activation(out=Sbv, in_=Sfv, func=AF.Copy), reads=[BSf], writes=[BSb])
                    if samp:
                        dstS = d["s_S"][j][:, 2 * p:2 * p + 2, :, :].rearrange("s hh d v -> (hh d) s v")
                        for q4 in range(0, NS, 4):
                            S.op("sp", lambda h, dstS=dstS, q4=q4: h.dma_start(out=dstS[:, q4:q4 + 4, :], in_=Sfs[:, q4:q4 + 4, :]), reads=[BSfs], dma=True)
                    elif ci == nprompt - 1:
                        dstS = d["p_S"][j][2 * p:2 * p + 2, :, :].rearrange("hh d v -> (hh d) v")
                        S.op("sp", lambda h, dstS=dstS, p=p: h.dma_start(out=dstS, in_=Sfp[:, p, :]), reads=[BSfp[p]], dma=True)
                S.op("dve", lambda h, L=L: h.tensor_scalar(rstd[0:L, :], mv[0:L, :, 1], LN_EPS, None, ALU.add), reads=[Bmv], writes=[Brstd])
                S.op("act", lambda h, L=L: h.activation(out=rstd[0:L, :], in_=rstd[0:L, :], func=AF.Ln), reads=[Brstd], writes=[Brstd])
                S.op("act", lambda h, L=L: h.activation(out=rstd[0:L, :], in_=rstd[0:L, :], func=AF.Exp, scale=-0.5), reads=[Brstd], writes=[Brstd])
                for hd in range(4):
                    S.op("dve", lambda h, hd=hd, L=L: h.tensor_scalar(otok[0:L, hd, :], otok[0:L, hd, :], mv[0:L, hd, 0:1], rstd[0:L, hd:hd + 1], ALU.subtract, ALU.mult),
                         reads=[Botok, Bmv, Brstd], writes=[Botok])
                S.op("dve", lambda h, L=L: h.tensor_tensor(gltok[0:L, :], otok[0:L, :, :].rearrange("p h v -> p (h v)"), gs[0:L, :], ALU.mult),
                     reads=[Botok, Bgs], writes=[Bgltok])
                gs0 = (t0 // 512) * 512 if t0 < cfg.seq else t0
                toff = t0 - gs0
                for hd in range(4):
                    self.tr(self.pbf[:, hd * 128:hd * 128 + L], gltok[0:L, hd * 128:(hd + 1) * 128], self.ident[0:L, 0:L], [Bgltok, self.Bconst], [self.BpbfB])
                S.op("act", lambda h, L=L, toff=toff: h.activation(out=glT[:, :, toff:toff + L], in_=self.pbf[:, 0:512].rearrange("p (h t) -> p h t", t=128)[:, :, 0:L], func=AF.Copy),
                     reads=[self.BpbfB], writes=[BglT])
                gend = t0 + L
                if getattr(cfg, "dbg_stop", None) == "gla_c0" and ci == 0:
                    S.cut = S.count
                if t0 >= cfg.seq or gend % 512 == 0 or gend == cfg.seq:
                    self.out_proj(wo, Bwo, 4, glT, BglT, gs0, gend - gs0, first)
        S.barrier()

    def ssd_layer(self, l):
        cfg, S, d = self.cfg, self.S, self.dram
        j = l // 2
        with ExitStack() as st:
            cw = self.sb(st, "sd_cw", [128, 24, 4], F32)
            cb = self.sb(st, "sd_cb", [128, 24], F32)
            Bcw = [Buf() for _ in range(5)]
            cwraw = self.sb(st, "sd_cwraw", [128, 128], F32)
            Braw = [Buf(), Buf(), Buf()]
            S.op("pool", lambda h: h.memset(cwraw[:], 0.0), writes=[Braw[2]])
            S.op("sp", lambda h: h.dma_start(out=cwraw[0:96, :], in_=d["ssd_conv_w"][j].rearrange("w (c p) -> (w c) p", p=128)), reads=[Braw[2]], writes=[Braw[0]], dma=True)
            S.op("sp", lambda h: h.dma_start(out=cwraw[96:120, :], in_=d["ssd_conv_b"][j].rearrange("(c p) -> c p", p=128)), reads=[Braw[2]], writes=[Braw[1]], dma=True)
            pcw, Bpcw = self.psbank()
            self.mm(pcw[:, 0:128], cwraw[:, :], self.identf[:, :], True, True, Braw + [self.Bconst], [Bpcw])
            S.op("dve", lambda h, pcw=pcw: h.tensor_copy(cw[:, :, :], pcw[:, 0:96].rearrange("p (w c) -> p c w", w=4)), reads=[Bpcw], writes=[Bcw[0]])
            S.op("dve", lambda h, pcw=pcw: h.tensor_copy(cb[:, :], pcw[:, 96:120]), reads=[Bpcw], writes=[Bcw[4]])
            groups = getattr(cfg, "ssd_groups", (0, 1, 2, 3))
            for gi, g in enumerate(groups):
                self.ssd_group(l, j, g, gi == 0, cw, cb, Bcw)
            sc = self.ln_scratch(st)
            for (t0, n) in cfg.groups:
                self.layer_norm("mix", l, t0, n, sc)
        S.barrier()

    def ssd_group(self, l, j, g, first, cw, cb, Bcw):
        cfg, S, d = self.cfg, self.S, self.dram
        NS = cfg.nseq
        with ExitStack() as st:
            MC = self.mixer_consts(st, small=True)
            Bmc = MC["B"]
            onesf = MC["onesf"]
            w_in = d["ssd_w_in"][j]
            w_out = d["ssd_w_out"][j]
            (wz, wx, wB, wC, wdt, wo), (Bz, Bwx, BwB, BwC, Bwdt, Bwo) = self.take_w()
            CH = [g * 512 + cc * 128 for cc in range(4)] + [2048 + g * 128, 2560 + g * 128]
            CHI = [c // 128 for c in CH]
            COLS = [(g * 512, 0, 512), (2048 + g * 128, 512, 128), (2560 + g * 128, 640, 128)]
            par8 = self.sb(st, "sd_par8", [8, 2], F32)
            Dbc = self.sb(st, "sd_Dbc", [128, 8], F32)
            normg = self.sb(st, "sd_normg", [128, 512], BF16)
            Bp = [Buf() for _ in range(4)]
            S.op("sp", lambda h: h.dma_start(out=par8[:, 0:1], in_=d["ssd_dt_bias"][j][g * 8:(g + 1) * 8].rearrange("(h o) -> h o", o=1)), writes=[Bp[0]], dma=True)
            S.op("sp", lambda h: h.dma_start(out=par8[:, 1:2], in_=d["ssd_a_log"][j][g * 8:(g + 1) * 8].rearrange("(h o) -> h o", o=1)), writes=[Bp[1]], dma=True)
            S.op("act", lambda h: h.activation(out=par8[:, 1:2], in_=par8[:, 1:2], func=AF.Exp), reads=[Bp[1]], writes=[Bp[1]])
            S.op("dve", lambda h: h.tensor_scalar(par8[:, 1:2], par8[:, 1:2], -1.0, None, ALU.mult), reads=[Bp[1]], writes=[Bp[1]])
            S.op("sp", lambda h: h.dma_start(out=Dbc[:], in_=d["ssd_d"][j][g * 8:(g + 1) * 8].partition_broadcast(128)), writes=[Bp[2]], dma=True)
            S.op("pool", lambda h: h.dma_start(out=normg[:], in_=d["ssd_norm"][j][g * 512:(g + 1) * 512].partition_broadcast(128)), writes=[Bp[3]], dma=True)
            id8 = self.identf[0:8, 0:8]
            DT, CS, NCS, ECS, WL, TMP = range(6)
            G8 = self.sb(st, "sd_G8", [8, 6, 128], F32)
            BG8 = Buf()
            edL = self.sb(st, "sd_edL", [8, 32], F32)
            edLD = self.sb(st, "sd_edLD", [8, 8, 16], F32)
            csD = self.sb(st, "sd_csD", [8, 4, 128], F32)
            BedL, BedLD, BcsD = Buf(), Buf(), Buf()
            cols2 = [self.sb(st, "sd_cols%d" % i, [128, 32], F32) for i in range(2)]
            Bcols2 = [Buf(), Buf()]
            decb = self.sb(st, "sd_decb", [128, 8, 16], F32)
            Bdecb = Buf()
            ext = self.sb(st, "sd_ext", [128, 6, 232], F32)
            Bext = Buf()
            nct = self.sb(st, "sd_nct", [128, 6, 48], F32)
            Bnct = Buf()
            acc = self.sb(st, "sd_acc", [128, 1, 128], F32)
            Bacc1 = Buf()
            Bacc = [Bacc1, Bacc1]
            xc = self.sb(st, "sd_xc", [128, 6, 128], BF16)
            Bxc = Buf()
            cvt = self.sb(st, "sd_cvt", [48, 768], F32)
            Bcvt = Buf()
            zs2 = [self.sb(st, "sd_zs%d" % i, [128, 512], BF16) for i in range(2)]
            Bzs2 = [Buf(), Buf()]
            xD2 = [self.sb(st, "sd_xD%d" % i, [128, 512], BF16) for i in range(2)]
            BxD2 = [Buf(), Buf()]
            xtok = self.sb(st, "sd_xtok", [128, 640], BF16)
            xdt = self.sb(st, "sd_xdt", [128, 512], BF16)
            xw = self.sb(st, "sd_xw", [128, 512], BF16)
            Bxtok, Bxdt, Bxw = Buf(), Buf(), Buf()
            CBT = self.sb(st, "sd_CBT", [128, 128], BF16)
            BCBT = Buf()
            E = [self.sb(st, "sd_E%d" % i, [128, 128], BF16) for i in range(2)]
            PT = [self.sb(st, "sd_PT%d" % i, [128, 128], BF16) for i in range(2)]
            BE, BPT = [Buf(), Buf()], [Buf(), Buf()]
            Cxj = self.sb(st, "sd_Cxj", [128, 128], BF16)
            Bxj = self.sb(st, "sd_Bxj", [128, 128], BF16)
            BCxj, BBxj = Buf(), Buf()
            ytok = self.sb(st, "sd_ytok", [128, 512], F32)
            yn = self.sb(st, "sd_yn", [128, 512], BF16)
            ss = self.sb(st, "sd_ss", [128, 1], F32)
            Bytok, Byn, Bss = Buf(), Buf(), Buf()
            ynT = self.wbuf[self.cur_slot][:, 14400:14400 + 2048].rearrange("p (h t) -> p h t", t=512)
            BynT = Buf()
            hnat = ytok[:, :].rearrange("p (r n) -> p r n", n=128)
            Bhnat = Bytok
            hTf = self.sb(st, "sd_hTf", [128, 512], F32)
            hTb = self.sb(st, "sd_hTb", [128, 512], BF16)
            BhTf, BhTb = Buf(), Buf()
            PY, BPY = self.pb[4], self.pbB[4]
            PI, BPI = self.pb[5], self.pbB[5]
            S.op("pool", lambda h: h.memset(hTf[:], 0.0), writes=[BhTf])
            S.op("pool", lambda h: h.memset(hTb[:], 0.0), writes=[BhTb])
            S.op("pool", lambda h: h.memset(ext[:], 0.0), writes=[Bext])
            S.op("pool", lambda h: h.memset(Cxj[:], 0.0), writes=[BCxj])
            chunks = self.chunks()
            nprompt = len(chunks) - 1
            pending = None
            for ci, (t0, L, nseq, blen, samp) in enumerate(chunks):
                nblk = L // blen
                W = 3 + blen
                Bxb = self.tile_bufs(self.Bxbf, t0, L)
                xk = lambda kc: self.xbf[:, kc, t0:t0 + L]
                extv = ext[:, :, 0:nblk * W].rearrange("p c (b w) -> p c b w", w=W)
                g8 = lambda slot, blen=blen, L=L: G8[:, slot, 0:L].rearrange("p (b t) -> p b t", t=blen)
                cols, Bcols = cols2[ci % 2], Bcols2[ci % 2]
                zs, Bzs = zs2[ci % 2], Bzs2[ci % 2]
                xD, BxD = xD2[ci % 2], BxD2[ci % 2]
                if samp:
                    S.op("pool", lambda h: h.memset(ext[:], 0.0), writes=[Bext])
                    for (c0, lc, n) in COLS:
                        S.op("sp", lambda h, c0=c0, lc=lc, n=n: h.dma_start(out=cvt[:, lc:lc + n], in_=d["st_cv"][j].rearrange("s w c -> (s w) c")[:, c0:c0 + n]),
                             writes=[Bcvt], dma=True)
                    pcv = self.psbank()
                    for cc in range(6):
                        self.mm(pcv[0][:, cc * 48:(cc + 1) * 48], cvt[0:48, cc * 128:(cc + 1) * 128], self.identf[0:48, 0:48], True, True, [Bcvt, self.Bconst], [pcv[1]])
                    S.op("dve", lambda h, pcv=pcv, extv=extv: h.tensor_copy(extv[:, :, 0:16, 0:3], pcv[0][:, 0:288].rearrange("p (c s w) -> p c s w", s=16, w=3)),
                         reads=[pcv[1]], writes=[Bext])
                elif ci > 0:
                    S.op("dve", lambda h, extv=extv, blen=blen: h.tensor_copy(extv[:, :, 0, 0:3], extv[:, :, 0, blen:blen + 3]), reads=[Bext], writes=[Bext])
                pd = self.psbank()
                for kc in range(KD):
                    self.mm(pd[0][0:8, 0:L], wdt[:, kc, :], xk(kc), kc == 0, kc == KD - 1, Bxb + [Bwdt[kc]], [pd[1]])
                px = self.psbank()
                for cc in range(4):
                    for kc in range(KD):
                        self.mm(px[0][:, cc * 128:cc * 128 + L], wx[:, kc, cc * 128:(cc + 1) * 128], xk(kc), kc == 0, kc == KD - 1, Bxb + [Bwx[kc]], [px[1]])
                pbc = self.psbank()
                for kc in range(KD):
                    self.mm(pbc[0][:, 0:L], wB[:, kc, :], xk(kc), kc == 0, kc == KD - 1, Bxb + [BwB[kc]], [pbc[1]])
                for kc in range(KD):
                    self.mm(pbc[0][:, 128:128 + L], wC[:, kc, :], xk(kc), kc == 0, kc == KD - 1, Bxb + [BwC[kc]], [pbc[1]])
                pz = (self.pb[6], self.pbB[6])
                for kc in range(KD):
                    self.mm(pz[0][0:L, :], self.xbf[:, kc, t0:t0 + L], wz[:, kc, :], kc == 0, kc == KD - 1, Bxb + [Bz[kc]], [pz[1]])
                S.op("act", lambda h, pd=pd, L=L: h.activation(out=G8[:, TMP, 0:L], in_=pd[0][0:8, 0:L], func=AF.Exp, bias=par8[:, 0:1], scale=1.0),
                     reads=[pd[1], Bp[0]], writes=[BG8])
                S.op("act", lambda h, L=L: h.activation(out=G8[:, DT, 0:L], in_=G8[:, TMP, 0:L], func=AF.Ln, bias=1.0, scale=1.0), reads=[BG8], writes=[BG8])
                S.op("act", lambda h, px=px, extv=extv, blen=blen: h.activation(out=extv[:, 0:4, :, 3:3 + blen], in_=px[0][:, :].rearrange("p (c b t) -> p c b t", c=4, t=blen), func=AF.Copy),
                     reads=[px[1]], writes=[Bext])
                S.op("act", lambda h, pbc=pbc, extv=extv, blen=blen: h.activation(out=extv[:, 4:6, :, 3:3 + blen], in_=pbc[0][:, 0:256].rearrange("p (c b t) -> p c b t", c=2, t=blen), func=AF.Copy),
                     reads=[pbc[1]], writes=[Bext])
                S.op("dve", lambda h, L=L: h.tensor_scalar(G8[:, TMP, 0:L], G8[:, DT, 0:L], par8[:, 1:2], None, ALU.mult), reads=[BG8, Bp[1]], writes=[BG8])
                d0 = MC["rst_s"][0:8, 0:L] if samp else onesf[0:8, 0:L]
                S.op("dve", lambda h, d0=d0, L=L: h.tensor_tensor_scan(G8[:, CS, 0:L], d0, G8[:, TMP, 0:L], 0.0, ALU.mult, ALU.add), reads=[BG8, Bmc], writes=[BG8])
                csL = g8(CS)[:, :, blen - 1]
                S.op("dve", lambda h, g8=g8, csL=csL, nblk=nblk, blen=blen: h.tensor_tensor(g8(TMP), csL.unsqueeze(2).broadcast_to([8, nblk, blen]), g8(CS), ALU.subtract),
                     reads=[BG8], writes=[BG8])
                S.op("act", lambda h, L=L: h.activation(out=G8[:, WL, 0:L], in_=G8[:, TMP, 0:L], func=AF.Exp), reads=[BG8], writes=[BG8])
                S.op("dve", lambda h, L=L: h.tensor_tensor(G8[:, WL, 0:L], G8[:, WL, 0:L], G8[:, DT, 0:L], ALU.mult), reads=[BG8], writes=[BG8])
                S.op("dve", lambda h, L=L: h.tensor_scalar(G8[:, NCS, 0:L], G8[:, CS, 0:L], -1.0, None, ALU.mult), reads=[BG8], writes=[BG8])
                S.op("act", lambda h, L=L: h.activation(out=G8[:, ECS, 0:L], in_=G8[:, CS, 0:L], func=AF.Exp), reads=[BG8], writes=[BG8])
                S.op("act", lambda h, csL=csL, nblk=nblk: h.activation(out=edL[:, 0:nblk], in_=csL, func=AF.Exp), reads=[BG8], writes=[BedL])
                pc = self.psbank()
                for qi, slot in enumerate((NCS, ECS, DT, WL)):
                    self.mm(pc[0][0:L, qi * 8:qi * 8 + 8], G8[:, slot, 0:L], id8, True, True, [BG8, self.Bconst], [pc[1]])
                S.op("dve", lambda h, pc=pc, L=L, cols=cols: h.tensor_copy(cols[0:L, :], pc[0][0:L, 0:32]), reads=[pc[1]], writes=[Bcols])
                S.op("dve", lambda h, nseq=nseq: h.tensor_tensor(edLD[:, :, 0:nseq], edL[:, 0:nseq].unsqueeze(1).broadcast_to([8, 8, nseq]),
                                                                 id8.unsqueeze(2).broadcast_to([8, 8, nseq]), ALU.mult),
                     reads=[BedL, self.Bconst], writes=[BedLD])
                pdc = self.psbank()
                if nseq == 16:
                    self.mm(pdc[0][:, 0:128], onesf[0:8, 0:128], edLD[:, :, :].rearrange("p h s -> p (h s)"), True, True, [BedLD, Bmc], [pdc[1]])
                    S.op("act", lambda h, pdc=pdc: h.activation(out=decb[:, :, 0:16], in_=pdc[0][:, 0:128].rearrange("p (h s) -> p h s", s=16), func=AF.Copy),
                         reads=[pdc[1]], writes=[Bdecb])
                else:
                    assert nseq == 1
                    self.mm(pdc[0][:, 0:8], onesf[0:8, 0:128], edLD[:, :, 0], True, True, [BedLD, Bmc], [pdc[1]])
                    S.op("act", lambda h, pdc=pdc: h.activation(out=decb[:, :, 0:1], in_=pdc[0][:, 0:8].unsqueeze(2), func=AF.Copy),
                         reads=[pdc[1]], writes=[Bdecb])
                if samp or ci == nprompt - 1:
                    nrow = 48 if samp else 3
                    if samp:
                        S.op("dve", lambda h, extv=extv: h.tensor_copy(nct[:, :, :].rearrange("p c (s w) -> p c s w", w=3), extv[:, :, 0:16, 4:7]), reads=[Bext], writes=[Bnct])
                    else:
                        S.op("dve", lambda h, extv=extv, blen=blen: h.tensor_copy(nct[:, :, 0:3], extv[:, :, 0, blen:blen + 3]), reads=[Bext], writes=[Bnct])
                    for half, (c_lo, c_hi) in enumerate(((0, 4), (4, 6))):
                        pco = self.psbank()
                        for cc in range(c_lo, c_hi):
                            lhs = nct[:, cc, 0:nrow]
                            self.mm(pco[0][0:nrow, (cc - c_lo) * 128:(cc - c_lo + 1) * 128], lhs, self.identf[:, :], True, True, [Bnct, self.Bconst], [pco[1]])
                        ncol = (c_hi - c_lo) * 128
                        S.op("dve", lambda h, pco=pco, c_lo=c_lo, ncol=ncol, nrow=nrow: h.tensor_copy(cvt[0:nrow, c_lo * 128:c_lo * 128 + ncol], pco[0][0:nrow, 0:ncol]),
                             reads=[pco[1]], writes=[Bcvt])
                    for (c0, lc, n) in COLS:
                        if samp:
                            dst = d["s_cv"][j].rearrange("s w c -> (s w) c")[:, c0:c0 + n]
                        else:
                            dst = d["p_cv"][j][:, c0:c0 + n]
                        S.op("sp", lambda h, dst=dst, lc=lc, n=n, nrow=nrow: h.dma_start(out=dst, in_=cvt[0:nrow, lc:lc + n]), reads=[Bcvt], dma=True)
                for cc in range(6):
                    eng = "dve"
                    a_ = acc[:, 0, 0:L].rearrange("p (b t) -> p b t", t=blen)
                    Ba_ = Bacc[cc % 2]
                    ci_ = CHI[cc]
                    S.op(eng, lambda h, a_=a_, cc=cc, ci_=ci_, extv=extv, blen=blen: h.tensor_scalar(a_, extv[:, cc, :, 0:blen], cw[:, ci_, 0:1], cb[:, ci_:ci_ + 1], ALU.mult, ALU.add),
                         reads=[Bext] + Bcw, writes=[Ba_])
                    for w in range(1, 4):
                        if eng == "dve":
                            S.op(eng, lambda h, a_=a_, cc=cc, ci_=ci_, w=w, extv=extv, blen=blen: h.scalar_tensor_tensor(a_, extv[:, cc, :, w:w + blen], cw[:, ci_, w:w + 1], a_, ALU.mult, ALU.add),
                                 reads=[Bext, Ba_] + Bcw, writes=[Ba_])
                        else:
                            t_ = ctmp[:, 0:L].rearrange("p (b t) -> p b t", t=blen)
                            S.op(eng, lambda h, t_=t_, cc=cc, ci_=ci_, w=w, extv=extv, blen=blen: h.tensor_scalar(t_, extv[:, cc, :, w:w + blen], cw[:, ci_, w:w + 1], None, ALU.mult),
                                 reads=[Bext] + Bcw, writes=[Bctmp])
                            S.op(eng, lambda h, a_=a_, t_=t_: h.tensor_tensor(a_, a_, t_, ALU.add), reads=[Ba_, Bctmp], writes=[Ba_])
                    S.op("act", lambda h, cc=cc, L=L: h.activation(out=xc[:, cc, 0:L], in_=acc[:, 0, 0:L], func=AF.Silu), reads=[Ba_], writes=[Bxc])
                S.op("act", lambda h, pz=pz, L=L, zs=zs: h.activation(out=zs[0:L, :], in_=pz[0][0:L, :], func=AF.Silu), reads=[pz[1]], writes=[Bzs])
                for cc in range(5):
                    self.tr(self.pbf[0:L, cc * 128:(cc + 1) * 128], xc[:, cc, 0:L], self.ident[:, :], [Bxc, self.Bconst], [self.BpbfB])
                S.op("act", lambda h, L=L: h.activation(out=xtok[0:L, :], in_=self.pbf[0:L, 0:640], func=AF.Copy), reads=[self.BpbfB], writes=[Bxtok])
                x3 = lambda t, L=L: t[0:L, 0:512].rearrange("p (h e) -> p h e", e=64)
                c3 = lambda k, L=L, cols=cols: cols[0:L, k * 8:(k + 1) * 8].unsqueeze(2).broadcast_to([L, 8, 64])
                S.op("dve", lambda h, x3=x3, c3=c3: h.tensor_tensor(x3(xdt), x3(xtok), c3(2), ALU.mult), reads=[Bxtok, Bcols], writes=[Bxdt])
                S.op("dve", lambda h, x3=x3, c3=c3: h.tensor_tensor(x3(xw), x3(xtok), c3(3), ALU.mult), reads=[Bxtok, Bcols], writes=[Bxw])
                pcb = self.psbank()
                self.mm(pcb[0][0:L, 0:L], xc[:, 4, 0:L], xc[:, 5, 0:L], True, True, [Bxc], [pcb[1]])
                S.op("act", lambda h, pcb=pcb, L=L: h.activation(out=CBT[0:L, 0:L], in_=pcb[0][0:L, 0:L], func=AF.Copy), reads=[pcb[1]], writes=[BCBT])
                S.op("dve", lambda h, x3=x3, L=L, xD=xD: h.tensor_tensor(x3(xD), x3(xtok), Dbc[0:L, :].unsqueeze(2).broadcast_to([L, 8, 64]), ALU.mult),
                     reads=[Bxtok, Bp[2]], writes=[BxD])
                if pending is not None:
                    pending()
                mneg = MC["mneg_s"] if samp else MC["mneg_p"]
                def half_front(hf, L=L, mneg=mneg):
                    S.op("dve", lambda h, hf=hf, L=L: h.tensor_tensor(csD[:, :, 0:L], G8[:, CS, 0:L].unsqueeze(1).broadcast_to([8, 4, L]),
                                                                     self.identf[0:8, 4 * hf:4 * hf + 4].unsqueeze(2).broadcast_to([8, 4, L]), ALU.mult),
                         reads=[BG8, self.Bconst], writes=[BcsD])
                    pbq = self.psbank()
                    self.mm(pbq[0][0:L, 0:512], onesf[0:8, 0:L], csD[:, :, :].rearrange("p a t -> p (a t)"), True, False, [BcsD, Bmc], [pbq[1]])
                    for q in range(4):
                        self.mm(pbq[0][0:L, q * 128:q * 128 + L], self.ident[0:L, 0:L], mneg[0:L, 0:L], False, q == 3, [Bmc, self.Bconst], [pbq[1]])
                    return pbq
                assert L == 128
                pbh = {0: half_front(0)}
                for hh in range(8):
                    e_, Be_ = E[hh % 2], BE[hh % 2]
                    pt_, Bpt_ = PT[hh % 2], BPT[hh % 2]
                    pbq = pbh[hh // 4]
                    q = hh % 4
                    S.op("act", lambda h, e_=e_, pbq=pbq, hh=hh, q=q, L=L, cols=cols: h.activation(out=e_[0:L, 0:L], in_=pbq[0][0:L, q * 128:q * 128 + L], func=AF.Exp, bias=cols[0:L, hh:hh + 1], scale=1.0),
                         reads=[pbq[1], Bcols], writes=[Be_])
                    S.op("dve", lambda h, e_=e_, pt_=pt_, L=L: h.tensor_tensor(pt_[0:L, 0:L], e_[0:L, 0:L], CBT[0:L, 0:L], ALU.mult),
                         reads=[Be_, BCBT], writes=[Bpt_])
                    if hh == 0:
                        pbh[1] = half_front(1)
                    self.mm(PY[0:L, hh * 64:(hh + 1) * 64], pt_[0:L, 0:L], xdt[0:L, hh * 64:(hh + 1) * 64], True, True, [Bpt_, Bxdt], [BPY])
                d3 = lambda jj: decb[:, :, jj:jj + 1].broadcast_to([128, 8, 64])
                h3 = hTf[:, :].rearrange("p (h e) -> p h e", e=64)
                if not samp:
                    self.mm(PI[0:L, :], xc[:, 5, 0:L], hTb[:, :], True, True, [Bxc, BhTb], [BPI])
                    pu = self.psbank()
                    self.mm(pu[0][:, :], xtok[0:L, 512:640], xw[0:L, :], True, True, [Bxtok, Bxw], [pu[1]])
                    S.op("dve", lambda h, d3=d3: h.tensor_tensor(h3, h3, d3(0), ALU.mult), reads=[BhTf, Bdecb], writes=[BhTf])
                    S.op("dve", lambda h, pu=pu: h.tensor_tensor(hTf[:, :], hTf[:, :], pu[0][:, :], ALU.add), reads=[BhTf, pu[1]], writes=[BhTf])
                    S.op("act", lambda h: h.activation(out=hTb[:, :], in_=hTf[:, :], func=AF.Copy), reads=[BhTf], writes=[BhTb])
                    if ci == nprompt - 1:
                        pt2 = self.psbank()
                        for pr in range(4):
                            self.mm(pt2[0][:, pr * 128:(pr + 1) * 128], hTf[:, pr * 128:(pr + 1) * 128], self.identf[:, :], True, True, [BhTf, self.Bconst], [pt2[1]])
                        S.op("act", lambda h, pt2=pt2: h.activation(out=hnat[:, :, :], in_=pt2[0][:, :].rearrange("p (r n) -> p r n", n=128), func=AF.Copy), reads=[pt2[1]], writes=[Bhnat])
                        dsth = d["p_h"][j][g * 8:(g + 1) * 8].rearrange("(pr hh) p n -> (hh p) pr n", hh=2)
                        S.op("sp", lambda h, dsth=dsth: h.dma_start(out=dsth, in_=hnat[:, :, :]), reads=[Bhnat], dma=True)
                else:
                    extf = ext[:, :, :].rearrange("p c w -> p (c w)")
                    hin = [extf[:, k * 512:(k + 1) * 512].rearrange("p (r n) -> p r n", n=128) for k in range(2)]
                    Bhin = [Buf(), Buf()]
                    S.op("dve", lambda h: h.memset(extf[:, 1024:1026], 0.0), writes=[Bext] + Bhin)
                    def load_state(jj):
                        srch = d["st_sh"][j][jj, g * 8:(g + 1) * 8].rearrange("(pr hh) p n -> (hh p) pr n", hh=2)
                        S.op("sp", lambda h, srch=srch, k=jj % 2: h.dma_start(out=hin[k], in_=srch), writes=[Bhin[jj % 2]], dma=True)
                    load_state(0)
                    for jj in range(nseq):
                        if jj + 1 < nseq:
                            load_state(jj + 1)
                        hin_, Bhin_ = hin[jj % 2], Bhin[jj % 2]
                        pt1 = self.psbank()
                        for pr in range(4):
                            self.mm(pt1[0][:, pr * 128:(pr + 1) * 128], hin_[:, pr, :], self.identf[:, :], True, True, [Bhin_, self.Bconst], [pt1[1]])
                        S.op("act", lambda h, pt1=pt1: h.activation(out=hTb[:, :], in_=pt1[0][:, :], func=AF.Copy), reads=[pt1[1]], writes=[BhTb])
                        S.op("dve", lambda h, jj=jj: h.tensor_tensor(Cxj[:, 0:64], xc[:, 5, 0:64], MC["qmask"][:, jj, 0:64], ALU.mult), reads=[Bxc, Bmc], writes=[BCxj])
                        self.mm(PI[0:L, :], Cxj[:, 0:L], hTb[:, :], jj == 0, jj == nseq - 1, [BCxj, BhTb], [BPI])
                        S.op("dve", lambda h, jj=jj, L=L: h.tensor_scalar(Bxj[0:L, :], xtok[0:L, 512:640], MC["tokmask"][0:L, jj:jj + 1], None, ALU.mult),
                             reads=[Bxtok, Bmc], writes=[BBxj])
                        pu = self.psbank()
                        self.mm(pu[0][:, :], Bxj[0:L, :], xw[0:L, :], True, True, [BBxj, Bxw], [pu[1]])
                        S.op("dve", lambda h, d3=d3, jj=jj, pt1=pt1: h.tensor_tensor(h3, pt1[0][:, :].rearrange("p (h e) -> p h e", e=64), d3(jj), ALU.mult),
                             reads=[pt1[1], Bdecb], writes=[BhTf])
                        S.op("dve", lambda h, pu=pu: h.tensor_tensor(hTf[:, :], hTf[:, :], pu[0][:, :], ALU.add), reads=[BhTf, pu[1]], writes=[BhTf])
                        pt2 = self.psbank()
                        for pr in range(4):
                            self.mm(pt2[0][:, pr * 128:(pr + 1) * 128], hTf[:, pr * 128:(pr + 1) * 128], self.identf[:, :], True, True, [BhTf, self.Bconst], [pt2[1]])
                        S.op("act", lambda h, pt2=pt2: h.activation(out=hnat[:, :, :], in_=pt2[0][:, :].rearrange("p (r n) -> p r n", n=128), func=AF.Copy), reads=[pt2[1]], writes=[Bhnat])
                        dsth = d["s_h"][j][jj, g * 8:(g + 1) * 8].rearrange("(pr hh) p n -> (hh p) pr n", hh=2)
                        S.op("sp", lambda h, dsth=dsth: h.dma_start(out=dsth, in_=hnat[:, :, :]), reads=[Bhnat], dma=True)
                def tail(t0=t0, L=L, cols=cols, Bcols=Bcols, zs=zs, Bzs=Bzs, xD=xD, BxD=BxD, c3=c3):
                    y3 = ytok[0:L, :].rearrange("p (h e) -> p h e", e=64)
                    S.op("dve", lambda h, y3=y3, c3=c3, L=L: h.tensor_tensor(y3, PI[0:L, :].rearrange("p (h e) -> p h e", e=64), c3(1), ALU.mult), reads=[BPI, Bcols], writes=[Bytok])
                    S.op("dve", lambda h, L=L: h.tensor_tensor(ytok[0:L, :], ytok[0:L, :], PY[0:L, :], ALU.add), reads=[Bytok, BPY], writes=[Bytok])
                    S.op("dve", lambda h, L=L, xD=xD: h.tensor_tensor(ytok[0:L, :], ytok[0:L, :], xD[0:L, :], ALU.add), reads=[Bytok, BxD], writes=[Bytok])
                    S.op("dve", lambda h, L=L, zs=zs: h.tensor_tensor(ytok[0:L, :], ytok[0:L, :], zs[0:L, :], ALU.mult), reads=[Bytok, Bzs], writes=[Bytok])
                    S.op("act", lambda h, L=L: h.activation(out=yn[0:L, :], in_=ytok[0:L, :], func=AF.Square, accum_out=ss[0:L, 0:1]), reads=[Bytok], writes=[Byn, Bss])
                    S.op("dve", lambda h, L=L: h.tensor_scalar(ss[0:L, :], ss[0:L, :], 1.0 / 512.0, LN_EPS, ALU.mult, ALU.add), reads=[Bss], writes=[Bss])
                    S.op("act", lambda h, L=L: h.activation(out=ss[0:L, :], in_=ss[0:L, :], func=AF.Ln), reads=[Bss], writes=[Bss])
                    S.op("act", lambda h, L=L: h.activation(out=ss[0:L, :], in_=ss[0:L, :], func=AF.Exp, scale=-0.5), reads=[Bss], writes=[Bss])
                    S.op("dve", lambda h, L=L: h.scalar_tensor_tensor(yn[0:L, :], ytok[0:L, :], ss[0:L, 0:1], normg[0:L, :], ALU.mult, ALU.mult),
                         reads=[Bytok, Bss, Bp[3], Byn], writes=[Byn])
                    for cc in range(4):
                        self.tr(self.pbf[:, cc * 128:cc * 128 + L], yn[0:L, cc * 128:(cc + 1) * 128], self.ident[0:L, 0:L], [Byn, self.Bconst], [self.BpbfB])
                    gs0 = (t0 // 512) * 512 if t0 < cfg.seq else t0
                    toff = t0 - gs0
                    S.op("act", lambda h, L=L, toff=toff: h.activation(out=ynT[:, :, toff:toff + L], in_=self.pbf[:, 0:512].rearrange("p (h t) -> p h t", t=128)[:, :, 0:L], func=AF.Copy),
                         reads=[self.BpbfB], writes=[BynT])
                    gend = t0 + L
                    if t0 >= cfg.seq or gend % 512 == 0 or gend == cfg.seq:
                        self.out_proj(wo, Bwo, 4, ynT, BynT, gs0, gend - gs0, first)
                pending = tail
            pending()
        S.barrier()


_W_NAMES = ("ab_w_in", "ab_ig_bias", "ab_fg_bias", "ab_ml_norm", "ab_gla_wa2", "ab_gla_ba", "ab_gla_norm",
            "ab_w_out", "ssd_w_in", "ssd_conv_w", "ssd_conv_b", "ssd_dt_bias", "ssd_a_log", "ssd_d", "ssd_norm",
            "ssd_w_out", "mlp_w1", "mlp_w2", "ln_mix_g", "ln_mix_b", "ln_mlp_g", "ln_mlp_b")


def run(cfg, inputs, ncores=8):
    prog = Prog(cfg)
    nc = prog.build()
    f = lambda a: np.ascontiguousarray(np.asarray(a, dtype=np.float32))
    W = {k: f(inputs[k]) for k in _W_NAMES}
    ns = cfg.nseq
    in_maps = []
    for c in range(ncores):
        m = dict(W)
        m["xp"] = f(inputs["x_prompt"][c])
        m["xs"] = f(inputs["x_sample"][c * ns:(c + 1) * ns]).reshape(cfg.ts_real, D)
        m["st_mC"] = f(inputs["state_mlstm_C"][:, c * ns:(c + 1) * ns])
        m["st_mn"] = f(inputs["state_mlstm_n"][:, c * ns:(c + 1) * ns])
        m["st_mm"] = f(inputs["state_mlstm_m"][:, c * ns:(c + 1) * ns])
        m["st_gS"] = f(inputs["state_gla_S"][:, c * ns:(c + 1) * ns])
        m["st_sh"] = f(inputs["state_ssd_h"][:, c * ns:(c + 1) * ns])
        m["st_cv"] = f(inputs["state_ssd_conv"][:, c * ns:(c + 1) * ns])
        in_maps.append(m)
    res = run_bass_kernel_spmd(nc, in_maps, core_ids=list(range(ncores)))
    R = res.results
    cat = lambda k, ax: np.concatenate([np.expand_dims(r[k], ax) if False else r[k] for r in R], axis=ax)
    y_prompt = np.stack([r["yp"] for r in R], 0)
    y_sample = np.concatenate([r["ys"].reshape(ns, cfg.slen, D) for r in R], 0)
    outs = [y_prompt, y_sample]
    for k in ("p_C", "p_n", "p_m", "p_S", "p_h", "p_cv"):
        outs.append(np.stack([r[k] for r in R], 1))
    for k in ("s_C", "s_n", "s_m", "s_S", "s_h", "s_cv"):
        outs.append(np.concatenate([r[k] for r in R], 1))
    return tuple(np.ascontiguousarray(o.astype(np.float32)) for o in outs)


def kernel(**inputs):
    return run(Cfg(), inputs)
```

```python
from contextlib import ExitStack
import numpy as np
import concourse.bass as bass
import concourse.mybir as mybir
from concourse.bass_utils import run_bass_kernel_spmd

F32 = mybir.dt.float32
BF16 = mybir.dt.bfloat16
AF = mybir.ActivationFunctionType
ALU = mybir.AluOpType
AX = mybir.AxisListType

ENGS = ("pe", "act", "dve", "pool", "sp")
NDMASEM = 8

D = 1024
KD = 8
DEPTH = 4
DFF = 4096
LN_EPS = 1e-5
DN_ALPHA = (2 * DEPTH) ** 0.25
AB_IN = 3096
SSD_IN = 5152
NEG = -30000.0


class Buf:
    __slots__ = ("name", "w", "r", "slot", "excl", "persist")

    def __init__(self, name="", excl=False, persist=False):
        self.name = name
        self.excl = excl
        self.slot = None
        self.persist = persist
        self.w = None
        self.r = []


class Slot:
    __slots__ = ("sem", "nd")

    def __init__(self):
        self.sem = None
        self.nd = 0


class Op:
    __slots__ = ("eng", "emit", "deps", "sig", "dma", "sem", "val", "chan")

    def __init__(self, eng, emit, dma):
        self.eng = eng
        self.emit = emit
        self.dma = dma
        self.deps = []
        self.sig = False
        self.sem = None
        self.val = 0
        self.chan = None


class Sched:
    def __init__(self, nc):
        self.nc = nc
        self.ops = {e: [] for e in ENGS}
        self.slots = []
        self.free = {e: [] for e in ENGS}
        self.live = []
        self.since_barrier = []

    def op(self, eng, emit, reads=(), writes=(), dma=False, extra=(), nobar=False):
        o = Op(eng, emit, dma)
        self.count = getattr(self, "count", 0) + 1
        if self.count > getattr(self, "cut", 1 << 60):
            return o
        deps = {}
        for b in reads:
            if b.w is not None:
                deps[id(b.w)] = (b.w, True)
            if b.excl:
                for r in b.r:
                    if id(r) not in deps:
                        deps[id(r)] = (r, False)
        for b in writes:
            if b.w is not None and id(b.w) not in deps:
                deps[id(b.w)] = (b.w, False)
            for r in b.r:
                if id(r) not in deps:
                    deps[id(r)] = (r, False)
        for d in extra:
            deps[id(d)] = (d, True)
        for d, raw in deps.values():
            if d is o:
                continue
            if d.eng == eng and not d.dma and not dma:
                if eng == "pe" or (not raw and eng != "pool"):
                    continue
            o.deps.append(d)
        for b in reads:
            b.r.append(o)
        for b in writes:
            b.w = o
            b.r = []
        self.ops[eng].append(o)
        if dma:
            ch = writes[0] if writes else reads[0]
            if ch.slot is None:
                if self.free[eng]:
                    ch.slot = self.free[eng].pop()
                else:
                    ch.slot = Slot()
                    self.slots.append(ch.slot)
                if not ch.persist:
                    self.live.append((ch, eng))
            ch.slot.nd += 1
            o.chan = ch.slot
            o.val = 16 * ch.slot.nd
            if not nobar:
                self.since_barrier.append(o)
        return o

    def barrier(self):
        last = [self.ops[e][-1] for e in ENGS if self.ops[e] and not self.ops[e][-1].dma]
        pend = list(self.since_barrier)
        self.since_barrier = []
        for e in ("pe", "act", "dve", "pool", "sp"):
            self.op(e, lambda h: h.nop(), extra=last + pend)
        for b, e in self.live:
            self.free[e].append(b.slot)
            b.slot = None
        self.live = []

    def finalize(self, stack):
        nc = self.nc
        for e in ENGS:
            for o in self.ops[e]:
                for d in o.deps:
                    d.sig = True
        self.esem = {}
        for e in ENGS:
            self.esem[e] = stack.enter_context(nc.semaphore("s_" + e))
        for i, sl in enumerate(self.slots):
            sl.sem = stack.enter_context(nc.semaphore("d%d" % i))
        for e in ENGS:
            cnt = 0
            for o in self.ops[e]:
                if o.dma:
                    o.sem = o.chan.sem
                elif o.sig:
                    cnt += 1
                    o.sem = self.esem[e]
                    o.val = cnt

    def emit_engine(self, e, h):
        waited = {}
        for o in self.ops[e]:
            need = {}
            for d in o.deps:
                k = d.sem.num
                if need.get(k, (None, 0))[1] < d.val:
                    need[k] = (d.sem, d.val)
            ws = []
            for k, (s, v) in need.items():
                if waited.get(k, 0) >= v:
                    continue
                waited[k] = v
                ws.append((s, v))
            for (s, v) in ws[1:]:
                h.wait_ge(s, v)
            ins = o.emit(h)
            if ws:
                ins._wait_ge(ws[0][0], ws[0][1])
            if o.dma:
                ins.then_inc(o.sem, 16)
            elif o.sig:
                ins.then_inc(o.sem, 1)
        if e == "sp":
            for sl in self.slots:
                if waited.get(sl.sem.num, 0) < 16 * sl.nd:
                    h.wait_ge(sl.sem, 16 * sl.nd)

    def run_block(self, block):
        s = self

        @block.tensor
        def _(h):
            s.emit_engine("pe", h)

        @block.scalar
        def _(h):
            s.emit_engine("act", h)

        @block.vector
        def _(h):
            s.emit_engine("dve", h)

        @block.gpsimd
        def _(h):
            s.emit_engine("pool", h)

        @block.sync
        def _(h):
            s.emit_engine("sp", h)


class Cfg:
    def __init__(self, seq=2048, nseq=16, slen=4, layers=(0, 1, 2, 3), parts=("mix", "mlp")):
        self.seq = seq
        self.nseq = nseq
        self.slen = slen
        self.ts_real = nseq * slen
        self.ts = 128
        self.T = seq + self.ts
        self.layers = tuple(layers)
        self.parts = tuple(parts)
        self.groups = [(i, min(512, seq - i)) for i in range(0, seq, 512)] + [(seq, self.ts)]
        self.tiles = [(i, 128) for i in range(0, seq, 128)] + [(seq, self.ts)]


class Prog:
    def __init__(self, cfg):
        self.cfg = cfg
        self.nc = bass.Bass("TRN2", target_bir_lowering=False)
        self.S = Sched(self.nc)
        self.S.cut = getattr(cfg, "cut", 1 << 60)
        self.rr = 0

    def sb(self, st, name, shape, dt):
        self.uid = getattr(self, "uid", 0) + 1
        return st.enter_context(self.nc.sbuf_tensor("%s_%d" % (name, self.uid), shape, dt))

    def psbank(self):
        i = self.rr % getattr(self, "nrot", 4)
        self.rr += 1
        return self.pb[i], self.pbB[i]

    def tile_bufs(self, arr, t0, n):
        a = t0 // 128
        b = (t0 + n + 127) // 128
        return arr[a:b]

    def mm(self, out, lhsT, rhs, start, stop, reads, writes):
        return self.S.op("pe", lambda h: h.matmul(out, lhsT, rhs, start=start, stop=stop), reads, writes)

    def tr(self, out, in_, ident, reads, writes):
        if in_.dtype == BF16:
            return self.S.op("pe", lambda h: h.transpose(out, in_, ident), reads, writes)
        return self.S.op("pe", lambda h: h.matmul(out, in_, ident, start=True, stop=True), reads, writes)

    def build(self):
        cfg, nc, S = self.cfg, self.nc, self.S
        T = cfg.T
        dram = {}

        only = getattr(cfg, "only", None)

        def din(name, shape):
            if only is not None and name not in only:
                return None
            dram[name] = nc.dram_tensor(name, list(shape), F32, kind="ExternalInput").ap()
            return dram[name]

        def dout(name, shape):
            if only is not None and name not in only:
                return None
            dram[name] = nc.dram_tensor(name, list(shape), F32, kind="ExternalOutput").ap()
            return dram[name]

        self.dram = dram
        ns = cfg.nseq
        din("xp", [cfg.seq, D])
        din("xs", [cfg.ts_real, D])
        din("st_mC", [2, ns, 4, 64, 128])
        din("st_mn", [2, ns, 4, 64])
        din("st_mm", [2, ns, 4])
        din("st_gS", [2, ns, 4, 64, 128])
        din("st_sh", [2, ns, 32, 64, 128])
        din("st_cv", [2, ns, 3, 3072])
        din("ab_w_in", [2, D, AB_IN])
        din("ab_ig_bias", [2, 4])
        din("ab_fg_bias", [2, 4])
        din("ab_ml_norm", [2, 512])
        din("ab_gla_wa2", [2, 16, 256])
        din("ab_gla_ba", [2, 256])
        din("ab_gla_norm", [2, 512])
        din("ab_w_out", [2, D, D])
        din("ssd_w_in", [2, D, SSD_IN])
        din("ssd_conv_w", [2, 4, 3072])
        din("ssd_conv_b", [2, 3072])
        din("ssd_dt_bias", [2, 32])
        din("ssd_a_log", [2, 32])
        din("ssd_d", [2, 32])
        din("ssd_norm", [2, 2048])
        din("ssd_w_out", [2, 2048, D])
        din("mlp_w1", [DEPTH, D, DFF])
        din("mlp_w2", [DEPTH, DFF, D])
        din("ln_mix_g", [DEPTH, D])
        din("ln_mix_b", [DEPTH, D])
        din("ln_mlp_g", [DEPTH, D])
        din("ln_mlp_b", [DEPTH, D])
        dout("yp", [cfg.seq, D])
        dout("ys", [cfg.ts_real, D])
        dout("p_C", [2, 4, 64, 128])
        dout("p_n", [2, 4, 64])
        dout("p_m", [2, 4])
        dout("p_S", [2, 4, 64, 128])
        dout("p_h", [2, 32, 64, 128])
        dout("p_cv", [2, 3, 3072])
        dout("s_C", [2, ns, 4, 64, 128])
        dout("s_n", [2, ns, 4, 64])
        dout("s_m", [2, ns, 4])
        dout("s_S", [2, ns, 4, 64, 128])
        dout("s_h", [2, ns, 32, 64, 128])
        dout("s_cv", [2, ns, 3, 3072])

        with ExitStack() as st:
            self.st = st
            self.xres = self.sb(st, "xres", [128, KD, T], F32)
            self.xbf = self.sb(st, "xbf", [128, KD, T], BF16)
            ntile = len(cfg.tiles)
            self.Bxres = [Buf("xres%d" % i) for i in range(ntile)]
            self.Bxbf = [Buf("xbf%d" % i) for i in range(ntile)]
            WCAP = 16512
            self.wbuf = [self.sb(st, "wbuf%d" % i, [128, WCAP], BF16) for i in range(2)]
            self.Bw = [[Buf("w%d_%d" % (i, j), persist=True) for j in range(24)] for i in range(2)]
            self.wslot = 0
            self.identf = self.sb(st, "identf", [128, 128], F32)
            self.ident = self.sb(st, "ident", [128, 128], BF16)
            self.ones_bf = self.sb(st, "ones_bf", [128, 128], BF16)
            self.lnp = self.sb(st, "lnp", [128, 4 * DEPTH, KD], F32)
            self.Bconst = Buf("const")
            self.pb = [st.enter_context(nc.psum_tensor("pb%d" % i, [128, 512], F32)) for i in range(7)]
            self.pbB = [Buf("pb%d" % i, excl=True) for i in range(7)]
            self.pbf = st.enter_context(nc.psum_tensor("pbf", [128, 1024], BF16))
            self.BpbfB = Buf("pbf", excl=True)

            self.setup_consts()
            self.wq = self.weight_specs()
            self.wq_i = 0
            self.wq_ready = self.load_w(self.wq[0]) if self.wq else None
            self.load_x()
            for l in cfg.layers:
                if "mix" in cfg.parts:
                    if l % 2 == 0:
                        self.ab_layer(l)
                    else:
                        self.ssd_layer(l)
                if "mlp" in cfg.parts:
                    self.mlp_layer(l)
            self.store_y()
            S.finalize(st)
            with nc.Block() as block:
                S.run_block(block)
        return nc

    def setup_consts(self):
        S, d = self.S, self.dram
        identf, ident, ones_bf = self.identf, self.ident, self.ones_bf
        Bc = self.Bconst
        S.op("pool", lambda h: h.memset(identf[:], 0.0), writes=[Bc])
        S.op("pool", lambda h: h.affine_select(out=identf[:], in_=identf[:], pattern=[[-1, 128]],
                                               compare_op=ALU.not_equal, fill=1.0, base=0,
                                               channel_multiplier=1), reads=[Bc], writes=[Bc])
        S.op("dve", lambda h: h.tensor_copy(ident[:], identf[:]), reads=[Bc], writes=[Bc])
        S.op("dve", lambda h: h.memset(ones_bf[:], 1.0), writes=[Bc])

    def load_x(self):
        cfg, S, d = self.cfg, self.S, self.dram
        with ExitStack() as st:
            xin = [self.sb(st, "xin%d" % i, [128, D], F32) for i in range(2)]
            Bxin = [Buf() for _ in range(2)]
            lnraw = self.sb(st, "lnraw", [128, 128], F32)
            Blr = [Buf() for _ in range(4)]
            for k, nm in enumerate(("ln_mix_g", "ln_mix_b", "ln_mlp_g", "ln_mlp_b")):
                S.op("sp", lambda h, k=k, nm=nm: h.dma_start(out=lnraw[k * 32:(k + 1) * 32, :], in_=d[nm].rearrange("l (c p) -> (l c) p", p=128)),
                     writes=[Blr[k]], dma=True)
            pl, Bpl = self.psbank()
            self.mm(pl[:, 0:128], lnraw[:, :], self.identf[:, :], True, True, Blr + [self.Bconst], [Bpl])
            S.op("dve", lambda h, pl=pl: h.tensor_copy(self.lnp[:, :, :].rearrange("p a c -> p (a c)"), pl[:, 0:128]), reads=[Bpl], writes=[self.Bconst])
            S.op("pool", lambda h: h.memset(self.xres[:, :, cfg.seq:cfg.T], 0.0), writes=[self.Bxres[-1]])
            S.op("pool", lambda h: h.memset(self.xbf[:, :, cfg.seq:cfg.T], 0.0), writes=[self.Bxbf[-1]])
            for ti, (t0, n) in enumerate(cfg.tiles):
                if t0 >= cfg.seq:
                    n = cfg.ts_real
                src = d["xp"][t0:t0 + n, :] if t0 < cfg.seq else d["xs"][:, :]
                xi, Bx = xin[ti % 2], Bxin[ti % 2]
                S.op("sp", lambda h, xi=xi, src=src, n=n: h.dma_start(out=xi[0:n, :], in_=src), writes=[Bx], dma=True)
                for half in range(2):
                    pt, Bp = self.psbank()
                    for c4 in range(4):
                        c = half * 4 + c4
                        self.tr(pt[:, c4 * 128:c4 * 128 + n], xi[0:n, c * 128:(c + 1) * 128], self.identf[0:n, 0:n],
                                [Bx, self.Bconst], [Bp])
                    pv = pt[:, :].rearrange("p (c t) -> p c t", t=128)[:, :, 0:n]
                    xr = self.xres[:, half * 4:half * 4 + 4, t0:t0 + n]
                    xb = self.xbf[:, half * 4:half * 4 + 4, t0:t0 + n]
                    S.op("dve", lambda h, xr=xr, pv=pv: h.tensor_copy(xr, pv), reads=[Bp], writes=[self.Bxres[ti]])
                    S.op("act", lambda h, xb=xb, pv=pv: h.activation(out=xb, in_=pv, func=AF.Copy), reads=[Bp],
                         writes=[self.Bxbf[ti]])
        S.barrier()

    def store_y(self):
        cfg, S, d = self.cfg, self.S, self.dram
        S.barrier()
        with ExitStack() as st:
            yo = [self.sb(st, "yo%d" % i, [128, D], F32) for i in range(2)]
            Byo = [Buf() for _ in range(2)]
            for ti, (t0, n) in enumerate(cfg.tiles):
                if t0 >= cfg.seq:
                    n = cfg.ts_real
                dst = d["yp"][t0:t0 + n, :] if t0 < cfg.seq else d["ys"][:, :]
                y, By = yo[ti % 2], Byo[ti % 2]
                for half in range(2):
                    pt, Bp = self.psbank()
                    for c4 in range(4):
                        c = half * 4 + c4
                        self.tr(pt[0:n, c4 * 128:(c4 + 1) * 128], self.xres[:, c, t0:t0 + n], self.identf[:, :],
                                [self.Bxres[ti], self.Bconst], [Bp])
                    eng = "dve" if half == 0 else "act"
                    if eng == "dve":
                        S.op("dve", lambda h, y=y, pt=pt, half=half, n=n: h.tensor_copy(y[0:n, half * 512:(half + 1) * 512], pt[0:n, :]),
                             reads=[Bp], writes=[By])
                    else:
                        S.op("act", lambda h, y=y, pt=pt, half=half, n=n: h.activation(out=y[0:n, half * 512:(half + 1) * 512], in_=pt[0:n, :], func=AF.Copy),
                             reads=[Bp], writes=[By])
                S.op("sp", lambda h, y=y, dst=dst, n=n: h.dma_start(out=dst, in_=y[0:n, :]), reads=[By], dma=True)

    def weight_specs(self):
        cfg, d = self.cfg, self.dram
        skip = getattr(cfg, "skip", ())
        out = []
        for l in cfg.layers:
            j = l // 2
            if "mix" in cfg.parts:
                if l % 2 == 0:
                    w_in, w_out = d["ab_w_in"][j], d["ab_w_out"][j]
                    if "ml" not in skip:
                        out.append([(w_in[:, 0:1544], 8, 1544), (w_out[0:512, :], 4, 1024)])
                    if "gla" not in skip:
                        out.append([(w_in[:, 1544:3096], 8, 1552), (w_out[512:1024, :], 4, 1024)])
                else:
                    w_in, w_out = d["ssd_w_in"][j], d["ssd_w_out"][j]
                    for g in getattr(cfg, "ssd_groups", (0, 1, 2, 3)):
                        out.append([(w_in[:, g * 512:(g + 1) * 512], 8, 512),
                                    (w_in[:, 2048 + g * 512:2048 + (g + 1) * 512], 8, 512),
                                    (w_in[:, 4096 + g * 128:4096 + (g + 1) * 128], 8, 128),
                                    (w_in[:, 4608 + g * 128:4608 + (g + 1) * 128], 8, 128),
                                    (w_in[:, 5120 + g * 8:5120 + (g + 1) * 8], 8, 8),
                                    (w_out[g * 512:(g + 1) * 512, :], 4, 1024)])
            if "mlp" in cfg.parts:
                w1, w2 = d["mlp_w1"][l], d["mlp_w2"][l]
                for q in range(4):
                    out.append([(w1[:, q * 1024:(q + 1) * 1024], 8, 1024), (w2[q * 1024:(q + 1) * 1024, :], 8, 1024)])
        return out

    def take_w(self):
        if not hasattr(self, "wq"):
            self.wq = self.weight_specs()
            self.wq_i = 0
            self.wq_ready = self.load_w(self.wq[0]) if self.wq else None
        cur = self.wq_ready
        self.wq_i += 1
        self.wq_ready = self.load_w(self.wq[self.wq_i]) if self.wq_i < len(self.wq) else None
        self.cur_slot = cur[2]
        return cur[0], cur[1]

    def load_w(self, pieces):
        S = self.S
        slot = self.wslot
        self.wslot ^= 1
        wb = self.wbuf[slot]
        old_users = []
        for B_ in self.Bw[slot]:
            old_users.extend(B_.r)
            if B_.w is not None:
                old_users.append(B_.w)
        off = 0
        views, bufs = [], []
        j = 0
        for (src, k, cols) in pieces:
            v = wb[:, off:off + k * cols].rearrange("p (k c) -> p k c", c=cols)
            bl = []
            if cols < 512:
                B = self.Bw[slot][j % 24]
                j += 1
                S.op("pool", lambda h, v=v, src=src: h.dma_start(out=v, in_=src.rearrange("(k p) c -> p k c", p=128)), writes=[B], dma=True, nobar=True, extra=old_users)
                views.append(v)
                bufs.append([B] * k)
                off += k * cols
                continue
            for kk in range(k):
                B = self.Bw[slot][j % 24]
                j += 1
                S.op("pool", lambda h, v=v, src=src, kk=kk: h.dma_start(out=v[:, kk, :], in_=src[kk * 128:(kk + 1) * 128, :]),
                     writes=[B], dma=True, nobar=True, extra=old_users)
                bl.append(B)
            views.append(v)
            bufs.append(bl)
            off += k * cols
        assert off <= 16512, off
        return views, bufs, slot

    def layer_norm(self, kind, l, t0, n, sc):
        if n > 256:
            for o in range(0, n, 256):
                self.layer_norm(kind, l, t0 + o, min(256, n - o), sc)
            return
        S = self.S
        Bxr = self.tile_bufs(self.Bxres, t0, n)
        Bxb = self.tile_bufs(self.Bxbf, t0, n)
        u = self.xres[:, :, t0:t0 + n]
        ub, usq, mean, var, rstd, nmr, tmp = sc["ub"], sc["usq"], sc["mean"], sc["var"], sc["rstd"], sc["nmr"], sc["tmp"]
        Bub, Busq, Bst, Btmp = sc["Bub"], sc["Busq"], sc["Bst"], sc["Btmp"]
        S.op("act", lambda h: h.activation(out=ub[:, :, 0:n], in_=u, func=AF.Copy), reads=Bxr, writes=[Bub])
        S.op("act", lambda h: h.activation(out=usq[:, :, 0:n], in_=u, func=AF.Square), reads=Bxr, writes=[Busq])
        p1, B1 = self.psbank()
        for c in range(KD):
            self.mm(p1[:, 0:n], self.ones_bf[:, :], ub[:, c, 0:n], c == 0, c == KD - 1, [Bub, self.Bconst], [B1])
        p2, B2 = self.psbank()
        for c in range(KD):
            self.mm(p2[:, 0:n], self.ones_bf[:, :], usq[:, c, 0:n], c == 0, c == KD - 1, [Busq, self.Bconst], [B2])
        S.op("dve", lambda h: h.tensor_scalar(mean[:, 0:n], p1[:, 0:n], 1.0 / D, None, ALU.mult), reads=[B1], writes=[Bst])
        S.op("dve", lambda h: h.tensor_tensor(var[:, 0:n], mean[:, 0:n], mean[:, 0:n], ALU.mult), reads=[Bst], writes=[Bst])
        S.op("dve", lambda h: h.scalar_tensor_tensor(var[:, 0:n], p2[:, 0:n], 1.0 / D, var[:, 0:n], ALU.mult, ALU.subtract),
             reads=[B2, Bst], writes=[Bst])
        S.op("dve", lambda h: h.tensor_scalar(var[:, 0:n], var[:, 0:n], LN_EPS, None, ALU.add), reads=[Bst], writes=[Bst])
        S.op("act", lambda h: h.activation(out=rstd[:, 0:n], in_=var[:, 0:n], func=AF.Ln), reads=[Bst], writes=[Bst])
        S.op("act", lambda h: h.activation(out=rstd[:, 0:n], in_=rstd[:, 0:n], func=AF.Exp, scale=-0.5), reads=[Bst], writes=[Bst])
        S.op("dve", lambda h: h.scalar_tensor_tensor(nmr[:, 0:n], mean[:, 0:n], -1.0, rstd[:, 0:n], ALU.mult, ALU.mult),
             reads=[Bst], writes=[Bst])
        gi = (0 if kind == "mix" else 2) * DEPTH + l
        bi = gi + DEPTH
        for c in range(KD):
            tc_ = tmp[:, c % 2, 0:n]
            Bt = Btmp[c % 2]
            eng = "dve"
            S.op(eng, lambda h, tc_=tc_, c=c: h.tensor_tensor(tc_, self.xres[:, c, t0:t0 + n], rstd[:, 0:n], ALU.mult),
                 reads=Bxr + [Bst], writes=[Bt])
            S.op(eng, lambda h, tc_=tc_: h.tensor_tensor(tc_, tc_, nmr[:, 0:n], ALU.add), reads=[Bt, Bst], writes=[Bt])
            g_ap = self.lnp[:, gi, c:c + 1]
            b_ap = self.lnp[:, bi, c:c + 1]
            S.op("act", lambda h, tc_=tc_, c=c, g_ap=g_ap, b_ap=b_ap: h.activation(out=self.xres[:, c, t0:t0 + n], in_=tc_, func=AF.Identity,
                                                                                   bias=b_ap, scale=g_ap),
                 reads=[Bt, self.Bconst], writes=Bxr)
            S.op("act", lambda h, tc_=tc_, c=c, g_ap=g_ap, b_ap=b_ap: h.activation(out=self.xbf[:, c, t0:t0 + n], in_=tc_, func=AF.Identity,
                                                                                   bias=b_ap, scale=g_ap),
                 reads=[Bt, self.Bconst], writes=Bxb)

    def ln_scratch(self, st):
        sc = {}
        sc["ub"] = self.sb(st, "ln_ub", [128, KD, 256], BF16)
        sc["usq"] = self.sb(st, "ln_usq", [128, KD, 256], BF16)
        for nm in ("mean", "var", "rstd", "nmr"):
            sc[nm] = self.sb(st, "ln_" + nm, [128, 256], F32)
        sc["tmp"] = self.sb(st, "ln_tmp", [128, 2, 256], F32)
        sc["Bub"], sc["Busq"], sc["Bst"] = Buf(), Buf(), Buf()
        sc["Btmp"] = [Buf(), Buf()]
        return sc

    def accum_u(self, first, ps, c, t0, n):
        S = self.S
        pt, Bp = ps
        Bxr = self.tile_bufs(self.Bxres, t0, n)
        xr = self.xres[:, c, t0:t0 + n]
        if first:
            S.op("dve", lambda h: h.scalar_tensor_tensor(xr, xr, DN_ALPHA, pt[:, 0:n], ALU.mult, ALU.add),
                 reads=Bxr + [Bp], writes=Bxr)
        else:
            S.op("dve", lambda h: h.tensor_tensor(xr, xr, pt[:, 0:n], ALU.add), reads=Bxr + [Bp], writes=Bxr)

    def mlp_layer(self, l):
        cfg, S, d = self.cfg, self.S, self.dram
        self.nrot = 7
        self.rr = 0
        with ExitStack() as st:
            hq = [self.sb(st, "hq%d" % i, [128, 8, 512], BF16) for i in range(2)]
            Bhq = [Buf(), Buf()]
            rl = [self.sb(st, "rl%d" % i, [128, 512], BF16) for i in range(2)]
            Brl = [Buf(), Buf()]
            sc = self.ln_scratch(st)
            w1 = d["mlp_w1"][l]
            w2 = d["mlp_w2"][l]
            gi = 0
            ri = 0
            for q in range(4):
                (w1v, w2v), (B1, B2) = self.take_w()
                def w1_part(t0, n, hh, Bh, w1v=w1v, B1=B1):
                    nonlocal ri
                    Bxb = self.tile_bufs(self.Bxbf, t0, n)
                    for hc in range(8):
                        ps = self.psbank()
                        for kc in range(KD):
                            self.mm(ps[0][:, 0:n], w1v[:, kc, hc * 128:(hc + 1) * 128], self.xbf[:, kc, t0:t0 + n],
                                    kc == 0, kc == KD - 1, Bxb + [B1[kc]], [ps[1]])
                        r, Br = rl[ri % 2], Brl[ri % 2]
                        ri += 1
                        S.op("act", lambda h, r=r, ps=ps, n=n: h.activation(out=r[:, 0:n], in_=ps[0][:, 0:n], func=AF.Relu),
                             reads=[ps[1]], writes=[Br])
                        S.op("dve", lambda h, r=r, hh=hh, hc=hc, n=n: h.tensor_tensor(hh[:, hc, 0:n], r[:, 0:n], r[:, 0:n], ALU.mult),
                             reads=[Br], writes=[Bh])

                def w2_part(t0, n, hh, Bh, q=q, w2v=w2v, B2=B2):
                    for fc in range(KD):
                        ps = self.psbank()
                        for hc in range(8):
                            self.mm(ps[0][:, 0:n], w2v[:, hc, fc * 128:(fc + 1) * 128], hh[:, hc, 0:n],
                                    hc == 0, hc == 7, [Bh, B2[hc]], [ps[1]])
                        self.accum_u(q == 0, ps, fc, t0, n)
                    if q == 3:
                        self.layer_norm("mlp", l, t0, n, sc)

                prev = None
                for (t0, n) in cfg.groups:
                    hh, Bh = hq[gi % 2], Bhq[gi % 2]
                    gi += 1
                    w1_part(t0, n, hh, Bh)
                    if prev is not None:
                        w2_part(*prev)
                    prev = (t0, n, hh, Bh)
                w2_part(*prev)
        S.barrier()
        self.nrot = 4
        self.rr = 0

    def mixer_consts(self, st, small=False, need01=True):
        S = self.S
        K_ = {}
        B = Buf("mixconst")
        K_["B"] = B
        ns, sl = 128 // self.cfg.slen, self.cfg.slen
        ts = 128
        mneg_p = self.sb(st, "mneg_p", [128, 128], BF16)
        m01_p = self.sb(st, "m01_p", [128, 128], BF16) if need01 else None
        mneg_s = self.sb(st, "mneg_s", [128, 128], BF16)
        m01_s = self.sb(st, "m01_s", [128, 128], BF16) if need01 else None
        tokmask = self.sb(st, "tokmask", [128, 16], BF16)
        QW = 64 if small else 128
        qmask = self.sb(st, "qmask", [128, 16, QW], BF16)
        onesf = self.sb(st, "onesf", [128, 128], F32)
        rst_s = self.sb(st, "rst_s", [128, 128], F32)
        with ExitStack() as st2:
            t1 = self.sb(st2, "mc_t1", [128, 128], F32)
            t2 = self.sb(st2, "mc_t2", [128, 128], F32)
            t3 = self.sb(st2, "mc_t3", [128, 16, QW], F32)
            Bt = Buf()
            S.op("dve", lambda h: h.memset(onesf[:], 1.0), writes=[B])
            S.op("pool", lambda h: h.memset(t1[:], 1.0), writes=[Bt])
            S.op("pool", lambda h: h.affine_select(out=t1[:], in_=t1[:], pattern=[[1, 128]], compare_op=ALU.is_ge, fill=0.0,
                                                   base=0, channel_multiplier=-1), reads=[Bt], writes=[Bt])
            if need01:
                S.op("dve", lambda h: h.tensor_copy(m01_p[:], t1[:]), reads=[Bt], writes=[B])
            S.op("dve", lambda h: h.tensor_scalar(mneg_p[:], t1[:], -1.0, -NEG, ALU.add, ALU.mult), reads=[Bt], writes=[B])
            S.op("pool", lambda h: h.memset(t2[:, 0:ns], 1.0), writes=[Bt])
            S.op("pool", lambda h: h.affine_select(out=t2[:, 0:ns], in_=t2[:, 0:ns], pattern=[[-sl, ns]], compare_op=ALU.is_ge, fill=0.0,
                                                   base=0, channel_multiplier=1), reads=[Bt], writes=[Bt])
            S.op("pool", lambda h: h.affine_select(out=t2[:, 0:ns], in_=t2[:, 0:ns], pattern=[[sl, ns]], compare_op=ALU.is_ge, fill=0.0,
                                                   base=sl - 1, channel_multiplier=-1), reads=[Bt], writes=[Bt])
            S.op("dve", lambda h: h.tensor_copy(tokmask[:], t2[:, 0:16]), reads=[Bt], writes=[B])
            if need01:
                S.op("dve", lambda h: h.memset(m01_s[:], 0.0), writes=[B])
            S.op("dve", lambda h: h.memset(mneg_s[:], 0.0), writes=[B])
            bd = t2[0:ts, 0:ns].unsqueeze(2).broadcast_to([ts, ns, sl])
            S.op("dve", lambda h: h.tensor_tensor(t1[0:ts, 0:ts].rearrange("p (j t) -> p j t", t=sl), t1[0:ts, 0:ts].rearrange("p (j t) -> p j t", t=sl), bd, ALU.mult),
                 reads=[Bt], writes=[Bt])
            if need01:
                S.op("dve", lambda h: h.tensor_copy(m01_s[0:ts, 0:ts], t1[0:ts, 0:ts]), reads=[Bt], writes=[B])
            S.op("dve", lambda h: h.tensor_scalar(mneg_s[0:ts, 0:ts], t1[0:ts, 0:ts], -1.0, -NEG, ALU.add, ALU.mult), reads=[Bt], writes=[B])
            S.op("pool", lambda h: h.memset(t3[:], 1.0), writes=[Bt])
            S.op("pool", lambda h: h.affine_select(out=t3[:], in_=t3[:], pattern=[[-sl, 16], [1, QW]], compare_op=ALU.is_ge, fill=0.0,
                                                   base=0, channel_multiplier=0), reads=[Bt], writes=[Bt])
            S.op("pool", lambda h: h.affine_select(out=t3[:], in_=t3[:], pattern=[[sl, 16], [-1, QW]], compare_op=ALU.is_ge, fill=0.0,
                                                   base=sl - 1, channel_multiplier=0), reads=[Bt], writes=[Bt])
            S.op("dve", lambda h: h.tensor_copy(qmask[:], t3[:]), reads=[Bt], writes=[B])
            S.op("dve", lambda h: h.memset(rst_s[:], 1.0), writes=[B])
            S.op("dve", lambda h: h.memset(rst_s[:, :].rearrange("p (j t) -> p j t", t=sl)[:, :, 0:1], 0.0), reads=[B], writes=[B])
            S.barrier()
        K_.update(mneg_p=mneg_p, m01_p=m01_p, mneg_s=mneg_s, m01_s=m01_s, tokmask=tokmask, qmask=qmask, onesf=onesf, rst_s=rst_s)
        return K_

    def chunks(self):
        cfg = self.cfg
        out = [(t0, 128, 1, 128, False) for t0 in range(0, cfg.seq, 128)]
        out.append((cfg.seq, cfg.ts, cfg.nseq, cfg.slen, True))
        return out

    def out_proj(self, wo, Bwo, nrow, srcT, Bsrc, t0, n, first):
        for fc in range(KD):
            ps = self.psbank()
            for r in range(nrow):
                self.mm(ps[0][:, 0:n], wo[:, r, fc * 128:(fc + 1) * 128], srcT[:, r, 0:n], r == 0, r == nrow - 1,
                        [Bsrc, Bwo[r]], [ps[1]])
            self.accum_u(first, ps, fc, t0, n)

    def ab_layer(self, l):
        j = l // 2
        skip = getattr(self.cfg, "skip", ())
        first = True
        if "ml" not in skip:
            self.ml_phase(l, j, first)
            first = False
        if "gla" not in skip:
            self.gla_phase(l, j, first)
        with ExitStack() as st:
            sc = self.ln_scratch(st)
            for (t0, n) in self.cfg.groups:
                self.layer_norm("mix", l, t0, n, sc)
        self.S.barrier()

    def ml_phase(self, l, j, first):
        cfg, S, d = self.cfg, self.S, self.dram
        NS = cfg.nseq
        with ExitStack() as st:
            MC = self.mixer_consts(st, need01=False)
            Bmc = MC["B"]
            onesf = MC["onesf"]
            w_in = d["ab_w_in"][j]
            w_out = d["ab_w_out"][j]
            (wv, wo), (Bwi, Bwo) = self.take_w()
            gb = self.sb(st, "ml_gb", [4, 2], F32)
            gnorm = self.sb(st, "ml_gnorm", [128, 512], BF16)
            NB = 128 // cfg.slen
            m0T = self.sb(st, "ml_m0T", [4, NB], F32)
            zc = self.sb(st, "ml_zc", [4, 1], F32)
            Bp = Buf("mlparams")
            Bp2 = Buf("mlparams2")
            Bp3 = Buf("mlparams3")
            Bp4 = Buf("mlparams4")
            S.op("sp", lambda h: h.dma_start(out=gb[:, 0:1], in_=d["ab_ig_bias"][j].rearrange("(h o) -> h o", o=1)), writes=[Bp], dma=True)
            S.op("sp", lambda h: h.dma_start(out=gb[:, 1:2], in_=d["ab_fg_bias"][j].rearrange("(h o) -> h o", o=1)), writes=[Bp2], dma=True)
            S.op("dve", lambda h: h.tensor_scalar(gb[:, 1:2], gb[:, 1:2], -1.0, None, ALU.mult), reads=[Bp2], writes=[Bp2])
            S.op("pool", lambda h: h.dma_start(out=gnorm[:], in_=d["ab_ml_norm"][j].partition_broadcast(128)), writes=[Bp3], dma=True)
            S.op("dve", lambda h: h.memset(m0T[:], 0.0), writes=[Bp4])
            S.op("sp", lambda h: h.dma_start(out=m0T[:, 0:NS], in_=d["st_mm"][j].rearrange("s h -> h s"), allow_slow_non_contiguous=True),
                 reads=[Bp4], writes=[Bp4], dma=True)
            S.op("dve", lambda h: h.memset(zc[:], 0.0), writes=[Bp])
            Bpar = [Bp, Bp2, Bp3, Bp4]
            NSLOT = 9
            IG, SP, CSP, A_, M_, NEGM, WE, ENM, TMP = range(9)
            G1 = self.sb(st, "ml_G", [4, NSLOT, 128], F32)
            G = [G1, G1]
            BG1 = Buf()
            BG = [BG1, BG1]
            carry = self.sb(st, "ml_carry", [4, 2], F32)
            Bcarry = Buf()
            RW = self.sb(st, "ml_RW", [4, 160], F32)
            RWD = self.sb(st, "ml_RWD", [4, 4, 160], F32)
            negMD = self.sb(st, "ml_negMD", [4, 4, 128], F32)
            blk = self.sb(st, "ml_blk", [4, 2, NB], F32)
            BRW, BRWD, BnMD, Bblk = Buf(), Buf(), Buf(), Buf()
            cols = self.sb(st, "ml_cols", [128, 12], F32)
            Bcols = Buf()
            qT = self.sb(st, "ml_qT", [128, 2, 128], BF16)
            kT = self.sb(st, "ml_kT", [128, 2, 128], BF16)
            BqT, BkT = Buf(), Buf()
            ktok = self.sb(st, "ml_ktok", [128, 256], BF16)
            vext = self.sb(st, "ml_vext", [128, 4, 129], BF16)
            gs = self.sb(st, "ml_gs", [128, 512], BF16)
            Bktok, Bvext, Bgs = Buf(), Buf(), Buf()
            E = [self.sb(st, "ml_E%d" % i, [128, 128], BF16) for i in range(2)]
            PT = [self.sb(st, "ml_PT%d" % i, [128, 128], BF16) for i in range(2)]
            BE, BPT = [Buf(), Buf()], [Buf(), Buf()]
            qsT = self.sb(st, "ml_qsT", [128, 128], BF16)
            qsx = [self.sb(st, "ml_qsx%d" % i, [128, 128], BF16) for i in range(2)]
            Bqsxj = [Buf(), Buf()]
            Cbj = [self.sb(st, "ml_Cbj%d" % i, [128, 129], BF16) for i in range(2)]
            BCbj = [Buf(), Buf()]
            kwxj = [self.sb(st, "ml_kwxj%d" % i, [128, 64], BF16) for i in range(2)]
            Bkwxj = [Buf(), Buf()]
            dec = self.sb(st, "ml_dec", [128, NS], F32)
            BqsT, Bqsx, Bdec = Buf(), Buf(), Buf()
            kw = self.sb(st, "ml_kw", [128, 64], BF16)
            Bkw, Bkwx = Buf(), Buf()
            htok = self.sb(st, "ml_htok", [128, 4, 128], F32)
            stats = self.sb(st, "ml_stats", [128, 4, 8], F32)
            mv = self.sb(st, "ml_mv", [128, 4, 2], F32)
            rstd = self.sb(st, "ml_rstd", [128, 4], F32)
            dn = self.sb(st, "ml_dn", [128, 4], F32)
            Bhtok, Bstats, Bmv, Brstd, Bdn = Buf(), Buf(), Buf(), Buf(), Buf()
            sgt = htok[:, :, :].rearrange("p h v -> p (h v)")
            Bsgt = Bhtok
            mltok = self.sb(st, "ml_mltok", [128, 512], BF16)
            Bmltok = Buf()
            mlT = self.sb(st, "ml_mlT", [128, 4, 512], BF16)
            BmlT = Buf()
            Cfp = self.sb(st, "ml_Cfp", [128, 2, 129], F32)
            Cbp = self.sb(st, "ml_Cbp", [128, 2, 129], BF16)
            Cfs = self.sb(st, "ml_Cfs", [128, NS, 129], F32)
            BCfp, BCbp = [Buf(), Buf()], [Buf(), Buf()]
            BCfs = Buf()
            S.op("pool", lambda h: h.memset(Cfp[:], 0.0), writes=BCfp)
            S.op("pool", lambda h: h.memset(Cbp[:], 0.0), writes=BCbp)
            S.op("pool", lambda h: h.memset(vext[:, :, 128:129], 1.0), writes=[Bvext])
            S.op("pool", lambda h: h.memset(RW[:], 0.0), writes=[BRW])
            id4 = self.identf[0:4, 0:4]
            chunks = self.chunks()
            nprompt = len(chunks) - 1
            grp_start = 0
            for ci, (t0, L, nseq, blen, samp) in enumerate(chunks):
                nblk = L // blen
                Bxb = self.tile_bufs(self.Bxbf, t0, L)
                Gc, Gp = G[ci % 2], G[(ci + 1) % 2]
                BGc, BGp = BG[ci % 2], BG[(ci + 1) % 2]
                xk = lambda kc: self.xbf[:, kc, t0:t0 + L]
                pg = self.psbank()
                for kc in range(KD):
                    self.mm(pg[0][0:4, 0:L], wv[:, kc, 1536:1540], xk(kc), kc == 0, kc == KD - 1, Bxb + [Bwi[kc]], [pg[1]])
                for kc in range(KD):
                    self.mm(pg[0][0:4, 128:128 + L], wv[:, kc, 1540:1544], xk(kc), kc == 0, kc == KD - 1, Bxb + [Bwi[kc]], [pg[1]])
                S.op("act", lambda h, Gc=Gc, pg=pg, L=L: h.activation(out=Gc[:, IG, 0:L], in_=pg[0][0:4, 0:L], func=AF.Identity, bias=gb[:, 0:1], scale=1.0),
                     reads=[pg[1]] + Bpar, writes=[BGc])
                S.op("act", lambda h, Gc=Gc, pg=pg, L=L: h.activation(out=Gc[:, TMP, 0:L], in_=pg[0][0:4, 128:128 + L], func=AF.Exp, bias=gb[:, 1:2], scale=-1.0),
                     reads=[pg[1]] + Bpar, writes=[BGc])
                S.op("act", lambda h, Gc=Gc, L=L: h.activation(out=Gc[:, SP, 0:L], in_=Gc[:, TMP, 0:L], func=AF.Ln, bias=1.0, scale=1.0),
                     reads=[BGc], writes=[BGc])
                pq = self.psbank()
                for i4 in range(4):
                    for kc in range(KD):
                        self.mm(pq[0][:, i4 * 128:i4 * 128 + L], wv[:, kc, i4 * 128:(i4 + 1) * 128], xk(kc), kc == 0, kc == KD - 1,
                                Bxb + [Bwi[kc]], [pq[1]])
                pqv = pq[0][:, :].rearrange("p (c t) -> p c t", t=128)
                S.op("act", lambda h, pqv=pqv, L=L: h.activation(out=qT[:, :, 0:L], in_=pqv[:, 0:2, 0:L], func=AF.Copy), reads=[pq[1]], writes=[BqT])
                S.op("act", lambda h, pqv=pqv, L=L: h.mul(kT[:, :, 0:L], pqv[:, 2:4, 0:L], 0.125), reads=[pq[1]], writes=[BkT])
                pk = self.psbank()
                for kc in range(KD):
                    self.mm(pk[0][0:L, 0:256], self.xbf[:, kc, t0:t0 + L], wv[:, kc, 256:512], kc == 0, kc == KD - 1, Bxb + [Bwi[kc]], [pk[1]])
                S.op("act", lambda h, pk=pk, L=L: h.mul(ktok[0:L, :], pk[0][0:L, 0:256], 0.125), reads=[pk[1]], writes=[Bktok])
                pv_ = self.psbank()
                for kc in range(KD):
                    self.mm(pv_[0][0:L, :], self.xbf[:, kc, t0:t0 + L], wv[:, kc, 512:1024], kc == 0, kc == KD - 1, Bxb + [Bwi[kc]], [pv_[1]])
                S.op("dve", lambda h, pv_=pv_, L=L: h.tensor_copy(vext[0:L, :, 0:128], pv_[0][0:L, :].rearrange("p (h v) -> p h v", v=128)),
                     reads=[pv_[1]], writes=[Bvext])
                po = self.psbank()
                for kc in range(KD):
                    self.mm(po[0][0:L, :], self.xbf[:, kc, t0:t0 + L], wv[:, kc, 1024:1536], kc == 0, kc == KD - 1, Bxb + [Bwi[kc]], [po[1]])
                S.op("act", lambda h, po=po, L=L: h.activation(out=sgt[0:L, :], in_=po[0][0:L, :], func=AF.Exp, scale=-1.0), reads=[po[1]], writes=[Bsgt])
                S.op("act", lambda h, L=L: h.activation(out=sgt[0:L, :], in_=sgt[0:L, :], func=AF.Ln, bias=1.0, scale=1.0), reads=[Bsgt], writes=[Bsgt])
                S.op("act", lambda h, L=L: h.activation(out=gs[0:L, :], in_=sgt[0:L, :], func=AF.Exp, scale=-1.0), reads=[Bsgt], writes=[Bgs])
                S.op("dve", lambda h, L=L: h.tensor_tensor(gs[0:L, :], gs[0:L, :], gnorm[0:L, :], ALU.mult), reads=[Bgs] + Bpar, writes=[Bgs])
                if not samp:
                    ini_c = 0.0 if ci == 0 else carry[:, 0:1]
                    ini_m = 0.0 if ci == 0 else carry[:, 1:2]
                    S.op("dve", lambda h, Gc=Gc, ini_c=ini_c, L=L: h.tensor_tensor_scan(Gc[:, CSP, 0:L], onesf[0:4, 0:L], Gc[:, SP, 0:L], ini_c, ALU.mult, ALU.add),
                         reads=[BGc, Bcarry, Bmc], writes=[BGc])
                    S.op("dve", lambda h, Gc=Gc, L=L: h.tensor_tensor(Gc[:, A_, 0:L], Gc[:, IG, 0:L], Gc[:, CSP, 0:L], ALU.add), reads=[BGc], writes=[BGc])
                    S.op("dve", lambda h, Gc=Gc, ini_m=ini_m, L=L: h.tensor_tensor_scan(Gc[:, M_, 0:L], Gc[:, A_, 0:L], Gc[:, A_, 0:L], ini_m, ALU.max, ALU.max),
                         reads=[BGc, Bcarry], writes=[BGc])
                    Mprev = zc[:, 0:1] if ci == 0 else carry[:, 1:2]
                    Me = Gc[:, M_, L - 1:L]
                    cspe = Gc[:, CSP, L - 1:L]
                else:
                    v3 = lambda slot, Gc=Gc: Gc[:, slot, 0:L].rearrange("p (s t) -> p s t", t=blen)
                    S.op("dve", lambda h, v3=v3: h.tensor_copy(v3(CSP)[:, :, 0:1], v3(SP)[:, :, 0:1]), reads=[BGc], writes=[BGc])
                    for t in range(1, blen):
                        S.op("dve", lambda h, v3=v3, t=t: h.tensor_tensor(v3(CSP)[:, :, t:t + 1], v3(CSP)[:, :, t - 1:t], v3(SP)[:, :, t:t + 1], ALU.add),
                             reads=[BGc], writes=[BGc])
                    S.op("dve", lambda h, Gc=Gc, L=L: h.tensor_tensor(Gc[:, A_, 0:L], Gc[:, IG, 0:L], Gc[:, CSP, 0:L], ALU.add), reads=[BGc], writes=[BGc])
                    S.op("dve", lambda h, v3=v3: h.tensor_tensor(v3(M_)[:, :, 0:1], v3(A_)[:, :, 0:1], m0T[:, :].unsqueeze(2), ALU.max),
                         reads=[BGc] + Bpar, writes=[BGc])
                    for t in range(1, blen):
                        S.op("dve", lambda h, v3=v3, t=t: h.tensor_tensor(v3(M_)[:, :, t:t + 1], v3(M_)[:, :, t - 1:t], v3(A_)[:, :, t:t + 1], ALU.max),
                             reads=[BGc], writes=[BGc])
                    Mprev = m0T[:, 0:nblk]
                    Me = v3(M_)[:, :, blen - 1]
                    cspe = v3(CSP)[:, :, blen - 1]
                b3 = lambda ap, L=L, nblk=nblk, blen=blen: ap.unsqueeze(2).broadcast_to([4, nblk, blen])
                g3 = lambda slot, Gc=Gc, L=L, blen=blen: Gc[:, slot, 0:L].rearrange("p (s t) -> p s t", t=blen)
                rdg = [BGc, Bcarry] + Bpar
                S.op("dve", lambda h, g3=g3, b3=b3, Mprev=Mprev: h.tensor_tensor(g3(TMP), b3(Mprev), g3(M_), ALU.subtract), reads=rdg, writes=[BGc])
                S.op("act", lambda h, Gc=Gc, L=L: h.activation(out=RW[:, 0:L], in_=Gc[:, TMP, 0:L], func=AF.Exp), reads=[BGc], writes=[BRW])
                S.op("dve", lambda h, g3=g3, b3=b3, Me=Me: h.tensor_tensor(g3(TMP), g3(A_), b3(Me), ALU.subtract), reads=rdg, writes=[BGc])
                S.op("act", lambda h, Gc=Gc, L=L: h.activation(out=Gc[:, WE, 0:L], in_=Gc[:, TMP, 0:L], func=AF.Exp), reads=[BGc], writes=[BGc])
                S.op("dve", lambda h, Gc=Gc, L=L: h.tensor_tensor(Gc[:, TMP, 0:L], Gc[:, CSP, 0:L], Gc[:, M_, 0:L], ALU.subtract), reads=[BGc], writes=[BGc])
                S.op("act", lambda h, Gc=Gc, L=L: h.activation(out=Gc[:, ENM, 0:L], in_=Gc[:, TMP, 0:L], func=AF.Exp), reads=[BGc], writes=[BGc])
                S.op("dve", lambda h, Gc=Gc, L=L: h.tensor_scalar(Gc[:, NEGM, 0:L], Gc[:, M_, 0:L], -1.0, None, ALU.mult), reads=[BGc], writes=[BGc])
                S.op("dve", lambda h, Mprev=Mprev, Me=Me, nblk=nblk: h.tensor_tensor(blk[:, 0, 0:nblk], Mprev, Me, ALU.subtract), reads=rdg, writes=[Bblk])
                S.op("act", lambda h, nblk=nblk: h.activation(out=RW[:, 128:128 + nblk], in_=blk[:, 0, 0:nblk], func=AF.Exp), reads=[Bblk], writes=[BRW])
                S.op("dve", lambda h, Me=Me, cspe=cspe, nblk=nblk: h.tensor_tensor(blk[:, 1, 0:nblk], Me, cspe, ALU.subtract), reads=rdg, writes=[Bblk])
                if samp:
                    S.op("sp", lambda h: h.dma_start(out=d["s_m"][j].rearrange("s h -> h s"), in_=blk[:, 1, 0:NS], allow_slow_non_contiguous=True),
                         reads=[Bblk], dma=True)
                elif ci == nprompt - 1:
                    S.op("sp", lambda h: h.dma_start(out=d["p_m"][j].rearrange("(h o) -> h o", o=1), in_=blk[:, 1, 0:1]), reads=[Bblk], dma=True)
                S.op("dve", lambda h, Gc=Gc, L=L: h.tensor_tensor(negMD[:, :, 0:L], Gc[:, NEGM, 0:L].unsqueeze(1).broadcast_to([4, 4, L]),
                                                                 id4.unsqueeze(2).broadcast_to([4, 4, L]), ALU.mult),
                     reads=[BGc, self.Bconst], writes=[BnMD])
                S.op("dve", lambda h: h.tensor_tensor(RWD[:, :, :], RW[:, :].unsqueeze(1).broadcast_to([4, 4, 160]),
                                                      id4.unsqueeze(2).broadcast_to([4, 4, 160]), ALU.mult),
                     reads=[BRW, self.Bconst], writes=[BRWD])
                pc = self.psbank()
                for qi, slot in enumerate((A_, WE, ENM)):
                    self.mm(pc[0][0:L, qi * 4:qi * 4 + 4], Gc[:, slot, 0:L], id4, True, True, [BGc, self.Bconst], [pc[1]])
                S.op("dve", lambda h, pc=pc, L=L: h.tensor_copy(cols[0:L, :], pc[0][0:L, 0:12]), reads=[pc[1]], writes=[Bcols])
                mneg = MC["mneg_s"] if samp else MC["mneg_p"]
                for p in range(2):
                    if samp:
                        BCf, BCb = BCfs, None
                        srcC = d["st_mC"][j][:, 2 * p:2 * p + 2, :, :].rearrange("s hh d v -> (hh d) s v")
                        srcn = d["st_mn"][j][:, 2 * p:2 * p + 2, :].rearrange("s hh d -> (hh d) s")
                        for q4 in range(0, NS, 4):
                            S.op("sp", lambda h, srcC=srcC, q4=q4: h.dma_start(out=Cfs[:, q4:q4 + 4, 0:128], in_=srcC[:, q4:q4 + 4, :]), writes=[BCfs], dma=True)
                        for q4 in range(0, NS, 4):
                            S.op("sp", lambda h, srcn=srcn, q4=q4: h.dma_start(out=Cfs[:, q4:q4 + 4, 128:129], in_=srcn[:, q4:q4 + 4].unsqueeze(2), allow_slow_non_contiguous=True),
                                 reads=[BCfs], writes=[BCfs], dma=True)
                        Cfv, Cbv = Cfs[:, :, :], None
                    else:
                        BCf, BCb = BCfp[p], BCbp[p]
                        Cfv = Cfp[:, p:p + 1, :]
                        Cbv = Cbp[:, p:p + 1, :]
                    pw = self.psbank()
                    for hh in range(2):
                        self.mm(pw[0][hh * 64:(hh + 1) * 64, 0:160], onesf[0:4, 0:64], RWD[:, 2 * p + hh, :], True, True, [BRWD, Bmc], [pw[1]])
                    S.op("dve", lambda h, pw=pw, p=p, L=L: h.tensor_tensor(qsT[:, 0:L], qT[:, p, 0:L], pw[0][:, 0:L], ALU.mult), reads=[pw[1], BqT], writes=[BqsT])
                    S.op("act", lambda h, pw=pw, nseq=nseq: h.activation(out=dec[:, 0:nseq], in_=pw[0][:, 128:128 + nseq], func=AF.Copy), reads=[pw[1]], writes=[Bdec])
                    PN = [(self.pb[4], self.pbB[4]), (self.pb[5], self.pbB[5])]
                    pab = []
                    for hh in range(2):
                        hd = 2 * p + hh
                        o = hh * 64
                        pa = self.psbank()
                        self.mm(pa[0][0:L, 0:L], kT[o:o + 64, p, 0:L], qT[o:o + 64, p, 0:L], True, True, [BkT, BqT], [pa[1]])
                        pb_ = self.psbank()
                        self.mm(pb_[0][0:L, 0:L], onesf[0:4, 0:L], negMD[:, hd, 0:L], True, False, [BnMD, Bmc], [pb_[1]])
                        self.mm(pb_[0][0:L, 0:L], self.ident[0:L, 0:L], mneg[0:L, 0:L], False, True, [Bmc, self.Bconst], [pb_[1]])
                        pab.append((pa, pb_))
                    for hh in range(2):
                        hd = 2 * p + hh
                        o = hh * 64
                        e_, Be_ = E[hd % 2], BE[hd % 2]
                        pt_, Bpt_ = PT[hd % 2], BPT[hd % 2]
                        pa, pb_ = pab[hh]
                        S.op("act", lambda h, e_=e_, pb_=pb_, hd=hd, L=L: h.activation(out=e_[0:L, 0:L], in_=pb_[0][0:L, 0:L], func=AF.Exp, bias=cols[0:L, hd:hd + 1], scale=1.0),
                             reads=[pb_[1], Bcols], writes=[Be_])
                        S.op("dve", lambda h, e_=e_, pt_=pt_, pa=pa, L=L: h.tensor_tensor(pt_[0:L, 0:L], e_[0:L, 0:L], pa[0][0:L, 0:L], ALU.mult),
                             reads=[Be_, pa[1]], writes=[Bpt_])
                        self.mm(PN[hh][0][0:L, 0:129], pt_[0:L, 0:L], vext[0:L, hd, :], True, False, [Bpt_, Bvext], [PN[hh][1]])
                    for jj in range(nseq):
                        if samp:
                            qx_, Bqx_ = qsx[jj % 2], Bqsxj[jj % 2]
                            cb_, Bcb_ = Cbj[jj % 2], BCbj[jj % 2]
                            S.op("dve", lambda h, qx_=qx_, jj=jj, L=L: h.tensor_tensor(qx_[:, 0:L], qsT[:, 0:L], MC["qmask"][:, jj, 0:L], ALU.mult),
                                 reads=[BqsT, Bmc], writes=[Bqx_])
                            S.op("act", lambda h, cb_=cb_, jj=jj: h.activation(out=cb_[:, :], in_=Cfs[:, jj, :], func=AF.Copy), reads=[BCfs], writes=[Bcb_])
                        for hh in range(2):
                            o = hh * 64
                            if samp:
                                self.mm(PN[hh][0][0:L, 0:129], qx_[o:o + 64, 0:L], cb_[o:o + 64, :], False, jj == nseq - 1, [Bqx_, Bcb_], [PN[hh][1]])
                            else:
                                self.mm(PN[hh][0][0:L, 0:129], qsT[o:o + 64, 0:L], Cbv[o:o + 64, jj, :], False, jj == nseq - 1, [BqsT, BCb], [PN[hh][1]])
                    for hh in range(2):
                        hd = 2 * p + hh
                        o = hh * 64
                        pn = PN[hh]
                        S.op("act", lambda h, pn=pn, hd=hd, L=L: h.activation(out=dn[0:L, hd:hd + 1], in_=pn[0][0:L, 128:129], func=AF.Abs),
                             reads=[pn[1]], writes=[Bdn])
                        S.op("dve", lambda h, hd=hd, L=L: h.tensor_tensor(dn[0:L, hd:hd + 1], dn[0:L, hd:hd + 1], cols[0:L, 8 + hd:9 + hd], ALU.max),
                             reads=[Bdn, Bcols], writes=[Bdn])
                        S.op("dve", lambda h, hd=hd, L=L: h.reciprocal(dn[0:L, hd:hd + 1], dn[0:L, hd:hd + 1]), reads=[Bdn], writes=[Bdn])
                        S.op("dve", lambda h, pn=pn, hd=hd, L=L: h.tensor_scalar(htok[0:L, hd, :], pn[0][0:L, 0:128], dn[0:L, hd:hd + 1], None, ALU.mult),
                             reads=[pn[1], Bdn], writes=[Bhtok])
                        S.op("dve", lambda h, hd=hd, L=L: h.bn_stats(stats[0:L, hd, 0:6], htok[0:L, hd, :]), reads=[Bhtok], writes=[Bstats])
                        S.op("dve", lambda h, hd=hd, L=L: h.bn_aggr(mv[0:L, hd, :], stats[0:L, hd, 0:6]), reads=[Bstats], writes=[Bmv])
                        S.op("dve", lambda h, hd=hd, L=L: h.tensor_scalar(kw[0:L, :], ktok[0:L, hd * 64:(hd + 1) * 64], cols[0:L, 4 + hd:5 + hd], None, ALU.mult),
                             reads=[Bktok, Bcols], writes=[Bkw])
                        for r0 in range(0, nseq, 4):
                            nr = min(4, nseq - r0)
                            pu = self.psbank()
                            for jj in range(r0, r0 + nr):
                                if samp:
                                    kx_, Bkx_ = kwxj[jj % 2], Bkwxj[jj % 2]
                                    S.op("dve", lambda h, kx_=kx_, jj=jj, L=L: h.tensor_scalar(kx_[0:L, :], kw[0:L, :], MC["tokmask"][0:L, jj:jj + 1], None, ALU.mult),
                                         reads=[Bkw, Bmc], writes=[Bkx_])
                                    self.mm(pu[0][o:o + 64, (jj - r0) * 128:(jj - r0 + 1) * 128], kx_[0:L, :], vext[0:L, hd, 0:128], True, True, [Bkx_, Bvext], [pu[1]])
                                else:
                                    self.mm(pu[0][o:o + 64, (jj - r0) * 128:(jj - r0 + 1) * 128], kw[0:L, :], vext[0:L, hd, 0:128], True, True, [Bkw, Bvext], [pu[1]])
                            cf = Cfv[o:o + 64, r0:r0 + nr, 0:128]
                            S.op("dve", lambda h, cf=cf, r0=r0, nr=nr, o=o: h.tensor_tensor(cf, cf, dec[o:o + 64, r0:r0 + nr].unsqueeze(2).broadcast_to([64, nr, 128]), ALU.mult),
                                 reads=[BCf, Bdec], writes=[BCf])
                            S.op("dve", lambda h, cf=cf, pu=pu, nr=nr, o=o: h.tensor_tensor(cf, cf, pu[0][o:o + 64, 0:nr * 128].rearrange("p (j v) -> p j v", v=128), ALU.add),
                                 reads=[BCf, pu[1]], writes=[BCf])
                        pn2 = self.psbank()
                        rhs_n = MC["tokmask"][0:L, 0:nseq] if samp else self.ones_bf[0:L, 0:1]
                        self.mm(pn2[0][o:o + 64, 0:nseq], kw[0:L, :], rhs_n, True, True, [Bkw, Bmc, self.Bconst], [pn2[1]])
                        cn = Cfv[o:o + 64, 0:nseq, 128:129]
                        S.op("dve", lambda h, cn=cn, nseq=nseq, o=o: h.tensor_tensor(cn, cn, dec[o:o + 64, 0:nseq].unsqueeze(2), ALU.mult), reads=[BCf, Bdec], writes=[BCf])
                        S.op("dve", lambda h, cn=cn, pn2=pn2, nseq=nseq, o=o: h.tensor_tensor(cn, cn, pn2[0][o:o + 64, 0:nseq].unsqueeze(2), ALU.add),
                             reads=[BCf, pn2[1]], writes=[BCf])
                    if not samp:
                        S.op("act", lambda h, Cfv=Cfv, Cbv=Cbv: h.activation(out=Cbv, in_=Cfv, func=AF.Copy), reads=[BCf], writes=[BCb])
                    if samp:
                        dstC = d["s_C"][j][:, 2 * p:2 * p + 2, :, :].rearrange("s hh d v -> (hh d) s v")
                        dstn = d["s_n"][j][:, 2 * p:2 * p + 2, :].rearrange("s hh d -> (hh d) s")
                        for q4 in range(0, NS, 4):
                            S.op("sp", lambda h, dstC=dstC, q4=q4: h.dma_start(out=dstC[:, q4:q4 + 4, :], in_=Cfs[:, q4:q4 + 4, 0:128]), reads=[BCfs], dma=True)
                        for q4 in range(0, NS, 4):
                            S.op("sp", lambda h, dstn=dstn, q4=q4: h.dma_start(out=dstn[:, q4:q4 + 4].unsqueeze(2), in_=Cfs[:, q4:q4 + 4, 128:129], allow_slow_non_contiguous=True),
                                 reads=[BCfs], dma=True)
                    elif ci == nprompt - 1:
                        dstC = d["p_C"][j][2 * p:2 * p + 2, :, :].rearrange("hh d v -> (hh d) v")
                        dstn = d["p_n"][j][2 * p:2 * p + 2, :].rearrange("hh (d o) -> (hh d) o", o=1)
                        S.op("sp", lambda h, dstC=dstC, p=p: h.dma_start(out=dstC, in_=Cfp[:, p, 0:128]), reads=[BCfp[p]], dma=True)
                        S.op("sp", lambda h, dstn=dstn, p=p: h.dma_start(out=dstn, in_=Cfp[:, p, 128:129]), reads=[BCfp[p]], dma=True)
                S.op("dve", lambda h, L=L: h.tensor_scalar(rstd[0:L, :], mv[0:L, :, 1], LN_EPS, None, ALU.add), reads=[Bmv], writes=[Brstd])
                S.op("act", lambda h, L=L: h.activation(out=rstd[0:L, :], in_=rstd[0:L, :], func=AF.Ln), reads=[Brstd], writes=[Brstd])
                S.op("act", lambda h, L=L: h.activation(out=rstd[0:L, :], in_=rstd[0:L, :], func=AF.Exp, scale=-0.5), reads=[Brstd], writes=[Brstd])
                for hd in range(4):
                    S.op("dve", lambda h, hd=hd, L=L: h.tensor_scalar(htok[0:L, hd, :], htok[0:L, hd, :], mv[0:L, hd, 0:1], rstd[0:L, hd:hd + 1], ALU.subtract, ALU.mult),
                         reads=[Bhtok, Bmv, Brstd], writes=[Bhtok])
                S.op("dve", lambda h, L=L: h.tensor_tensor(mltok[0:L, :], htok[0:L, :, :].rearrange("p h v -> p (h v)"), gs[0:L, :], ALU.mult),
                     reads=[Bhtok, Bgs], writes=[Bmltok])
                S.op("dve", lambda h, Gc=Gc, L=L: h.tensor_copy(carry[:, 0:1], Gc[:, CSP, L - 1:L]), reads=[BGc], writes=[Bcarry])
                S.op("dve", lambda h, Gc=Gc, L=L: h.tensor_copy(carry[:, 1:2], Gc[:, M_, L - 1:L]), reads=[BGc], writes=[Bcarry])
                gs0 = (t0 // 512) * 512 if t0 < cfg.seq else t0
                toff = t0 - gs0
                for hd in range(4):
                    self.tr(self.pbf[:, hd * 128:hd * 128 + L], mltok[0:L, hd * 128:(hd + 1) * 128], self.ident[0:L, 0:L], [Bmltok, self.Bconst], [self.BpbfB])
                S.op("act", lambda h, L=L, toff=toff: h.activation(out=mlT[:, :, toff:toff + L], in_=self.pbf[:, 0:512].rearrange("p (h t) -> p h t", t=128)[:, :, 0:L], func=AF.Copy),
                     reads=[self.BpbfB], writes=[BmlT])
                gend = t0 + L
                if t0 >= cfg.seq or gend % 512 == 0 or gend == cfg.seq:
                    self.out_proj(wo, Bwo, 4, mlT, BmlT, gs0, gend - gs0, first)
        S.barrier()

    def gla_phase(self, l, j, first):
        cfg, S, d = self.cfg, self.S, self.dram
        NS = cfg.nseq
        with ExitStack() as st:
            MC = self.mixer_consts(st)
            Bmc = MC["B"]
            onesf = MC["onesf"]
            w_in = d["ab_w_in"][j]
            w_out = d["ab_w_out"][j]
            (wv, wo), (Bwi, Bwo) = self.take_w()
            wa2 = self.sb(st, "gl_wa2", [16, 256], BF16)
            nba = self.sb(st, "gl_nba", [128, 2], F32)
            gnorm = self.sb(st, "gl_gnorm", [128, 512], BF16)
            Bq1, Bq2, Bq3 = Buf(), Buf(), Buf()
            S.op("pool", lambda h: h.dma_start(out=wa2[:], in_=d["ab_gla_wa2"][j]), writes=[Bq1], dma=True)
            S.op("sp", lambda h: h.dma_start(out=nba[:], in_=d["ab_gla_ba"][j].rearrange("(c p) -> p c", p=128), allow_slow_non_contiguous=True),
                 writes=[Bq2], dma=True)
            S.op("dve", lambda h: h.tensor_scalar(nba[:], nba[:], -1.0, None, ALU.mult), reads=[Bq2], writes=[Bq2])
            S.op("pool", lambda h: h.dma_start(out=gnorm[:], in_=d["ab_gla_norm"][j].partition_broadcast(128)), writes=[Bq3], dma=True)
            Bpar = [Bq1, Bq2, Bq3]
            gaT = self.sb(st, "gl_gaT", [16, 128], BF16)
            BgaT = Buf()
            spT = self.sb(st, "gl_spT", [128, 2, 128], F32)
            spc = self.sb(st, "gl_spc", [128, 2, 128], F32)
            eA = self.sb(st, "gl_eA", [128, 2, 128], F32)
            enA = self.sb(st, "gl_enA", [128, 2, 128], F32)
            BspT, Bspc, BeA, BenA = Buf(), Buf(), Buf(), Buf()
            qT = self.sb(st, "gl_qT", [128, 2, 128], BF16)
            kT = self.sb(st, "gl_kT", [128, 2, 128], BF16)
            BqT, BkT = Buf(), Buf()
            ktl = self.sb(st, "gl_ktl", [128, 128], BF16)
            kxj = [self.sb(st, "gl_kxj%d" % i, [128, 128], BF16) for i in range(2)]
            qxj = [self.sb(st, "gl_qxj%d" % i, [128, 128], BF16) for i in range(2)]
            Sbj = [self.sb(st, "gl_Sbj%d" % i, [128, 128], BF16) for i in range(2)]
            Bkxj, Bqxj, BSbj = [Buf(), Buf()], [Buf(), Buf()], [Buf(), Buf()]
            Bktl = Buf()
            vtok = self.sb(st, "gl_vtok", [128, 4, 128], BF16)
            gs = self.sb(st, "gl_gs", [128, 512], BF16)
            Bvtok, Bgs = Buf(), Buf()
            PT = [self.sb(st, "gl_PT%d" % i, [128, 128], BF16) for i in range(2)]
            BPT = [Buf(), Buf()]
            otok = self.sb(st, "gl_otok", [128, 4, 128], F32)
            stats = self.sb(st, "gl_stats", [128, 4, 8], F32)
            mv = self.sb(st, "gl_mv", [128, 4, 2], F32)
            rstd = self.sb(st, "gl_rstd", [128, 4], F32)
            Botok, Bstats, Bmv, Brstd = Buf(), Buf(), Buf(), Buf()
            gltok = self.sb(st, "gl_gltok", [128, 512], BF16)
            Bgltok = Buf()
            glT = self.sb(st, "gl_glT", [128, 4, 512], BF16)
            BglT = Buf()
            Sfp = self.sb(st, "gl_Sfp", [128, 2, 128], F32)
            Sbp = self.sb(st, "gl_Sbp", [128, 2, 128], BF16)
            Sfs = self.sb(st, "gl_Sfs", [128, NS, 128], F32)
            BSfp, BSbp = [Buf(), Buf()], [Buf(), Buf()]
            BSfs = Buf()
            S.op("pool", lambda h: h.memset(Sfp[:], 0.0), writes=BSfp)
            S.op("pool", lambda h: h.memset(Sbp[:], 0.0), writes=BSbp)
            chunks = self.chunks()
            nprompt = len(chunks) - 1
            grp_start = 0
            for ci, (t0, L, nseq, blen, samp) in enumerate(chunks):
                nblk = L // blen
                Bxb = self.tile_bufs(self.Bxbf, t0, L)
                xk = lambda kc: self.xbf[:, kc, t0:t0 + L]
                pg = self.psbank()
                for kc in range(KD):
                    self.mm(pg[0][0:16, 0:L], wv[:, kc, 1536:1552], xk(kc), kc == 0, kc == KD - 1, Bxb + [Bwi[kc]], [pg[1]])
                S.op("act", lambda h, pg=pg, L=L: h.activation(out=gaT[:, 0:L], in_=pg[0][0:16, 0:L], func=AF.Copy), reads=[pg[1]], writes=[BgaT])
                pz = self.psbank()
                for p in range(2):
                    self.mm(pz[0][:, p * 128:p * 128 + L], wa2[:, p * 128:(p + 1) * 128], gaT[:, 0:L], True, True, [BgaT] + Bpar, [pz[1]])
                for p in range(2):
                    S.op("act", lambda h, pz=pz, p=p, L=L: h.activation(out=spT[:, p, 0:L], in_=pz[0][:, p * 128:p * 128 + L], func=AF.Exp, bias=nba[:, p:p + 1], scale=-1.0),
                         reads=[pz[1]] + Bpar, writes=[BspT])
                S.op("act", lambda h, L=L: h.activation(out=spT[:, :, 0:L], in_=spT[:, :, 0:L], func=AF.Ln, bias=1.0, scale=1.0), reads=[BspT], writes=[BspT])
                for p in range(2):
                    d0 = MC["rst_s"][:, 0:L] if samp else onesf[:, 0:L]
                    S.op("dve", lambda h, p=p, d0=d0, L=L: h.tensor_tensor_scan(spc[:, p, 0:L], d0, spT[:, p, 0:L], 0.0, ALU.mult, ALU.add),
                         reads=[BspT, Bmc], writes=[Bspc])
                S.op("act", lambda h, L=L: h.activation(out=eA[:, :, 0:L], in_=spc[:, :, 0:L], func=AF.Exp, scale=-1.0 / 16.0), reads=[Bspc], writes=[BeA])
                S.op("act", lambda h, L=L: h.activation(out=enA[:, :, 0:L], in_=spc[:, :, 0:L], func=AF.Exp, scale=1.0 / 16.0), reads=[Bspc], writes=[BenA])
                pq = self.psbank()
                for i4 in range(4):
                    for kc in range(KD):
                        self.mm(pq[0][:, i4 * 128:i4 * 128 + L], wv[:, kc, i4 * 128:(i4 + 1) * 128], xk(kc), kc == 0, kc == KD - 1,
                                Bxb + [Bwi[kc]], [pq[1]])
                pqv = pq[0][:, :].rearrange("p (c t) -> p c t", t=128)
                S.op("dve", lambda h, pqv=pqv, L=L: h.scalar_tensor_tensor(qT[:, :, 0:L], pqv[:, 0:2, 0:L], 0.125, eA[:, :, 0:L], ALU.mult, ALU.mult),
                     reads=[pq[1], BeA], writes=[BqT])
                S.op("dve", lambda h, pqv=pqv, L=L: h.tensor_tensor(kT[:, :, 0:L], pqv[:, 2:4, 0:L], enA[:, :, 0:L], ALU.mult),
                     reads=[pq[1], BenA], writes=[BkT])
                pv_ = self.psbank()
                for kc in range(KD):
                    self.mm(pv_[0][0:L, :], self.xbf[:, kc, t0:t0 + L], wv[:, kc, 512:1024], kc == 0, kc == KD - 1, Bxb + [Bwi[kc]], [pv_[1]])
                S.op("act", lambda h, pv_=pv_, L=L: h.activation(out=vtok[0:L, :, :], in_=pv_[0][0:L, :].rearrange("p (h v) -> p h v", v=128), func=AF.Copy),
                     reads=[pv_[1]], writes=[Bvtok])
                po = self.psbank()
                for kc in range(KD):
                    self.mm(po[0][0:L, :], self.xbf[:, kc, t0:t0 + L], wv[:, kc, 1024:1536], kc == 0, kc == KD - 1, Bxb + [Bwi[kc]], [po[1]])
                S.op("act", lambda h, po=po, L=L: h.activation(out=gs[0:L, :], in_=po[0][0:L, :], func=AF.Silu), reads=[po[1]], writes=[Bgs])
                S.op("dve", lambda h, L=L: h.tensor_tensor(gs[0:L, :], gs[0:L, :], gnorm[0:L, :], ALU.mult), reads=[Bgs] + Bpar, writes=[Bgs])
                m01 = MC["m01_s"] if samp else MC["m01_p"]
                for p in range(2):
                    if samp:
                        BSf, BSb = BSfs, None
                        srcS = d["st_gS"][j][:, 2 * p:2 * p + 2, :, :].rearrange("s hh d v -> (hh d) s v")
                        for q4 in range(0, NS, 4):
                            S.op("sp", lambda h, srcS=srcS, q4=q4: h.dma_start(out=Sfs[:, q4:q4 + 4, :], in_=srcS[:, q4:q4 + 4, :]), writes=[BSfs], dma=True)
                        Sfv, Sbv = Sfs[:, :, :], None
                    else:
                        BSf, BSb = BSfp[p], BSbp[p]
                        Sfv = Sfp[:, p:p + 1, :]
                        Sbv = Sbp[:, p:p + 1, :]
                    self.tr(self.pbf[0:L, 0:128], kT[:, p, 0:L], self.ident[:, :], [BkT, self.Bconst], [self.BpbfB])
                    S.op("act", lambda h, L=L: h.activation(out=ktl[0:L, :], in_=self.pbf[0:L, 0:128], func=AF.Copy), reads=[self.BpbfB], writes=[Bktl])
                    eAL = eA[:, p, 0:L].rearrange("p (s t) -> p s t", t=blen)[:, :, blen - 1]
                    PN = [(self.pb[4], self.pbB[4]), (self.pb[5], self.pbB[5])]
                    pas = []
                    for hh in range(2):
                        o = hh * 64
                        pa = self.psbank()
                        self.mm(pa[0][0:L, 0:L], kT[o:o + 64, p, 0:L], qT[o:o + 64, p, 0:L], True, True, [BkT, BqT], [pa[1]])
                        pas.append(pa)
                    for hh in range(2):
                        hd = 2 * p + hh
                        o = hh * 64
                        pt_, Bpt_ = PT[hd % 2], BPT[hd % 2]
                        pa = pas[hh]
                        S.op("dve", lambda h, pt_=pt_, pa=pa, L=L, m01=m01: h.tensor_tensor(pt_[0:L, 0:L], pa[0][0:L, 0:L], m01[0:L, 0:L], ALU.mult),
                             reads=[pa[1], Bmc], writes=[Bpt_])
                        self.mm(PN[hh][0][0:L, 0:128], pt_[0:L, 0:L], vtok[0:L, hd, :], True, False, [Bpt_, Bvtok], [PN[hh][1]])
                    for jj in range(nseq):
                        if samp:
                            qx_, Bqx_ = qxj[jj % 2], Bqxj[jj % 2]
                            sb_, Bsb_ = Sbj[jj % 2], BSbj[jj % 2]
                            S.op("dve", lambda h, qx_=qx_, jj=jj, p=p, L=L: h.tensor_tensor(qx_[:, 0:L], qT[:, p, 0:L], MC["qmask"][:, jj, 0:L], ALU.mult),
                                 reads=[BqT, Bmc], writes=[Bqx_])
                            S.op("act", lambda h, sb_=sb_, jj=jj: h.activation(out=sb_[:, :], in_=Sfs[:, jj, :], func=AF.Copy), reads=[BSfs], writes=[Bsb_])
                        for hh in range(2):
                            o = hh * 64
                            if samp:
                                self.mm(PN[hh][0][0:L, 0:128], qx_[o:o + 64, 0:L], sb_[o:o + 64, :], False, jj == nseq - 1, [Bqx_, Bsb_], [PN[hh][1]])
                            else:
                                self.mm(PN[hh][0][0:L, 0:128], qT[o:o + 64, p, 0:L], Sbv[o:o + 64, jj, :], False, jj == nseq - 1, [BqT, BSb], [PN[hh][1]])
                    for hh in range(2):
                        hd = 2 * p + hh
                        o = hh * 64
                        pn = PN[hh]
                        S.op("act", lambda h, pn=pn, hd=hd, L=L: h.activation(out=otok[0:L, hd, :], in_=pn[0][0:L, 0:128], func=AF.Copy), reads=[pn[1]], writes=[Botok])
                        S.op("dve", lambda h, hd=hd, L=L: h.bn_stats(stats[0:L, hd, 0:6], otok[0:L, hd, :]), reads=[Botok], writes=[Bstats])
                        S.op("dve", lambda h, hd=hd, L=L: h.bn_aggr(mv[0:L, hd, :], stats[0:L, hd, 0:6]), reads=[Bstats], writes=[Bmv])
                        for r0 in range(0, nseq, 4):
                            nr = min(4, nseq - r0)
                            pu = self.psbank()
                            for jj in range(r0, r0 + nr):
                                if samp:
                                    kx_, Bkx_ = kxj[jj % 2], Bkxj[jj % 2]
                                    S.op("dve", lambda h, kx_=kx_, jj=jj, o=o, L=L: h.tensor_scalar(kx_[0:L, 0:64], ktl[0:L, o:o + 64], MC["tokmask"][0:L, jj:jj + 1], None, ALU.mult),
                                         reads=[Bktl, Bmc], writes=[Bkx_])
                                    self.mm(pu[0][o:o + 64, (jj - r0) * 128:(jj - r0 + 1) * 128], kx_[0:L, 0:64], vtok[0:L, hd, :], True, True, [Bkx_, Bvtok], [pu[1]])
                                else:
                                    self.mm(pu[0][o:o + 64, (jj - r0) * 128:(jj - r0 + 1) * 128], ktl[0:L, o:o + 64], vtok[0:L, hd, :], True, True, [Bktl, Bvtok], [pu[1]])
                            sf = Sfv[o:o + 64, r0:r0 + nr, :]
                            S.op("dve", lambda h, sf=sf, pu=pu, nr=nr, o=o: h.tensor_tensor(sf, sf, pu[0][o:o + 64, 0:nr * 128].rearrange("p (j v) -> p j v", v=128), ALU.add),
                                 reads=[BSf, pu[1]], writes=[BSf])
                            S.op("dve", lambda h, sf=sf, eAL=eAL, r0=r0, nr=nr, o=o: h.tensor_tensor(sf, sf, eAL[o:o + 64, r0:r0 + nr].unsqueeze(2).broadcast_to([64, nr, 128]), ALU.mult),
                                 reads=[BSf, BeA], writes=[BSf])
                    if not samp:
                        S.op("act", lambda h, Sfv=Sfv, Sbv=Sbv: h.activation(out=Sbv, in_=Sfv, func=AF.Copy), reads=[BSf], writes=[BSb])
                    if samp:
                        dstS = d["s_S"][j][:, 2 * p:2 * p + 2, :, :].rearrange("s hh d v -> (hh d) s v")
                        for q4 in range(0, NS, 4):
                            S.op("sp", lambda h, dstS=dstS, q4=q4: h.dma_start(out=dstS[:, q4:q4 + 4, :], in_=Sfs[:, q4:q4 + 4, :]), reads=[BSfs], dma=True)
                    elif ci == nprompt - 1:
                        dstS = d["p_S"][j][2 * p:2 * p + 2, :, :].rearrange("hh d v -> (hh d) v")
                        S.op("sp", lambda h, dstS=dstS, p=p: h.dma_start(out=dstS, in_=Sfp[:, p, :]), reads=[BSfp[p]], dma=True)
                S.op("dve", lambda h, L=L: h.tensor_scalar(rstd[0:L, :], mv[0:L, :, 1], LN_EPS, None, ALU.add), reads=[Bmv], writes=[Brstd])
                S.op("act", lambda h, L=L: h.activation(out=rstd[0:L, :], in_=rstd[0:L, :], func=AF.Ln), reads=[Brstd], writes=[Brstd])
                S.op("act", lambda h, L=L: h.activation(out=rstd[0:L, :], in_=rstd[0:L, :], func=AF.Exp, scale=-0.5), reads=[Brstd], writes=[Brstd])
                for hd in range(4):
                    S.op("dve", lambda h, hd=hd, L=L: h.tensor_scalar(otok[0:L, hd, :], otok[0:L, hd, :], mv[0:L, hd, 0:1], rstd[0:L, hd:hd + 1], ALU.subtract, ALU.mult),
                         reads=[Botok, Bmv, Brstd], writes=[Botok])
                S.op("dve", lambda h, L=L: h.tensor_tensor(gltok[0:L, :], otok[0:L, :, :].rearrange("p h v -> p (h v)"), gs[0:L, :], ALU.mult),
                     reads=[Botok, Bgs], writes=[Bgltok])
                gs0 = (t0 // 512) * 512 if t0 < cfg.seq else t0
                toff = t0 - gs0
                for hd in range(4):
                    self.tr(self.pbf[:, hd * 128:hd * 128 + L], gltok[0:L, hd * 128:(hd + 1) * 128], self.ident[0:L, 0:L], [Bgltok, self.Bconst], [self.BpbfB])
                S.op("act", lambda h, L=L, toff=toff: h.activation(out=glT[:, :, toff:toff + L], in_=self.pbf[:, 0:512].rearrange("p (h t) -> p h t", t=128)[:, :, 0:L], func=AF.Copy),
                     reads=[self.BpbfB], writes=[BglT])
                gend = t0 + L
                if getattr(cfg, "dbg_stop", None) == "gla_c0" and ci == 0:
                    S.cut = S.count
                if t0 >= cfg.seq or gend % 512 == 0 or gend == cfg.seq:
                    self.out_proj(wo, Bwo, 4, glT, BglT, gs0, gend - gs0, first)
        S.barrier()

    def ssd_layer(self, l):
        cfg, S, d = self.cfg, self.S, self.dram
        j = l // 2
        with ExitStack() as st:
            cw = self.sb(st, "sd_cw", [128, 24, 4], F32)
            cb = self.sb(st, "sd_cb", [128, 24], F32)
            Bcw = [Buf() for _ in range(5)]
            cwraw = self.sb(st, "sd_cwraw", [128, 128], F32)
            Braw = [Buf(), Buf(), Buf()]
            S.op("pool", lambda h: h.memset(cwraw[:], 0.0), writes=[Braw[2]])
            S.op("sp", lambda h: h.dma_start(out=cwraw[0:96, :], in_=d["ssd_conv_w"][j].rearrange("w (c p) -> (w c) p", p=128)), reads=[Braw[2]], writes=[Braw[0]], dma=True)
            S.op("sp", lambda h: h.dma_start(out=cwraw[96:120, :], in_=d["ssd_conv_b"][j].rearrange("(c p) -> c p", p=128)), reads=[Braw[2]], writes=[Braw[1]], dma=True)
            pcw, Bpcw = self.psbank()
            self.mm(pcw[:, 0:128], cwraw[:, :], self.identf[:, :], True, True, Braw + [self.Bconst], [Bpcw])
            S.op("dve", lambda h, pcw=pcw: h.tensor_copy(cw[:, :, :], pcw[:, 0:96].rearrange("p (w c) -> p c w", w=4)), reads=[Bpcw], writes=[Bcw[0]])
            S.op("dve", lambda h, pcw=pcw: h.tensor_copy(cb[:, :], pcw[:, 96:120]), reads=[Bpcw], writes=[Bcw[4]])
            groups = getattr(cfg, "ssd_groups", (0, 1, 2, 3))
            for gi, g in enumerate(groups):
                self.ssd_group(l, j, g, gi == 0, cw, cb, Bcw)
            sc = self.ln_scratch(st)
            for (t0, n) in cfg.groups:
                self.layer_norm("mix", l, t0, n, sc)
        S.barrier()

    def ssd_group(self, l, j, g, first, cw, cb, Bcw):
        cfg, S, d = self.cfg, self.S, self.dram
        NS = cfg.nseq
        with ExitStack() as st:
            MC = self.mixer_consts(st, small=True)
            Bmc = MC["B"]
            onesf = MC["onesf"]
            w_in = d["ssd_w_in"][j]
            w_out = d["ssd_w_out"][j]
            (wz, wx, wB, wC, wdt, wo), (Bz, Bwx, BwB, BwC, Bwdt, Bwo) = self.take_w()
            CH = [g * 512 + cc * 128 for cc in range(4)] + [2048 + g * 128, 2560 + g * 128]
            CHI = [c // 128 for c in CH]
            COLS = [(g * 512, 0, 512), (2048 + g * 128, 512, 128), (2560 + g * 128, 640, 128)]
            par8 = self.sb(st, "sd_par8", [8, 2], F32)
            Dbc = self.sb(st, "sd_Dbc", [128, 8], F32)
            normg = self.sb(st, "sd_normg", [128, 512], BF16)
            Bp = [Buf() for _ in range(4)]
            S.op("sp", lambda h: h.dma_start(out=par8[:, 0:1], in_=d["ssd_dt_bias"][j][g * 8:(g + 1) * 8].rearrange("(h o) -> h o", o=1)), writes=[Bp[0]], dma=True)
            S.op("sp", lambda h: h.dma_start(out=par8[:, 1:2], in_=d["ssd_a_log"][j][g * 8:(g + 1) * 8].rearrange("(h o) -> h o", o=1)), writes=[Bp[1]], dma=True)
            S.op("act", lambda h: h.activation(out=par8[:, 1:2], in_=par8[:, 1:2], func=AF.Exp), reads=[Bp[1]], writes=[Bp[1]])
            S.op("dve", lambda h: h.tensor_scalar(par8[:, 1:2], par8[:, 1:2], -1.0, None, ALU.mult), reads=[Bp[1]], writes=[Bp[1]])
            S.op("sp", lambda h: h.dma_start(out=Dbc[:], in_=d["ssd_d"][j][g * 8:(g + 1) * 8].partition_broadcast(128)), writes=[Bp[2]], dma=True)
            S.op("pool", lambda h: h.dma_start(out=normg[:], in_=d["ssd_norm"][j][g * 512:(g + 1) * 512].partition_broadcast(128)), writes=[Bp[3]], dma=True)
            id8 = self.identf[0:8, 0:8]
            DT, CS, NCS, ECS, WL, TMP = range(6)
            G8 = self.sb(st, "sd_G8", [8, 6, 128], F32)
            BG8 = Buf()
            edL = self.sb(st, "sd_edL", [8, 32], F32)
            edLD = self.sb(st, "sd_edLD", [8, 8, 16], F32)
            csD = self.sb(st, "sd_csD", [8, 4, 128], F32)
            BedL, BedLD, BcsD = Buf(), Buf(), Buf()
            cols2 = [self.sb(st, "sd_cols%d" % i, [128, 32], F32) for i in range(2)]
            Bcols2 = [Buf(), Buf()]
            decb = self.sb(st, "sd_decb", [128, 8, 16], F32)
            Bdecb = Buf()
            ext = self.sb(st, "sd_ext", [128, 6, 232], F32)
            Bext = Buf()
            nct = self.sb(st, "sd_nct", [128, 6, 48], F32)
            Bnct = Buf()
            acc = self.sb(st, "sd_acc", [128, 1, 128], F32)
            Bacc1 = Buf()
            Bacc = [Bacc1, Bacc1]
            xc = self.sb(st, "sd_xc", [128, 6, 128], BF16)
            Bxc = Buf()
            cvt = self.sb(st, "sd_cvt", [48, 768], F32)
            Bcvt = Buf()
            zs2 = [self.sb(st, "sd_zs%d" % i, [128, 512], BF16) for i in range(2)]
            Bzs2 = [Buf(), Buf()]
            xD2 = [self.sb(st, "sd_xD%d" % i, [128, 512], BF16) for i in range(2)]
            BxD2 = [Buf(), Buf()]
            xtok = self.sb(st, "sd_xtok", [128, 640], BF16)
            xdt = self.sb(st, "sd_xdt", [128, 512], BF16)
            xw = self.sb(st, "sd_xw", [128, 512], BF16)
            Bxtok, Bxdt, Bxw = Buf(), Buf(), Buf()
            CBT = self.sb(st, "sd_CBT", [128, 128], BF16)
            BCBT = Buf()
            E = [self.sb(st, "sd_E%d" % i, [128, 128], BF16) for i in range(2)]
            PT = [self.sb(st, "sd_PT%d" % i, [128, 128], BF16) for i in range(2)]
            BE, BPT = [Buf(), Buf()], [Buf(), Buf()]
            Cxj = self.sb(st, "sd_Cxj", [128, 128], BF16)
            Bxj = self.sb(st, "sd_Bxj", [128, 128], BF16)
            BCxj, BBxj = Buf(), Buf()
            ytok = self.sb(st, "sd_ytok", [128, 512], F32)
            yn = self.sb(st, "sd_yn", [128, 512], BF16)
            ss = self.sb(st, "sd_ss", [128, 1], F32)
            Bytok, Byn, Bss = Buf(), Buf(), Buf()
            ynT = self.wbuf[self.cur_slot][:, 14400:14400 + 2048].rearrange("p (h t) -> p h t", t=512)
            BynT = Buf()
            hnat = ytok[:, :].rearrange("p (r n) -> p r n", n=128)
            Bhnat = Bytok
            hTf = self.sb(st, "sd_hTf", [128, 512], F32)
            hTb = self.sb(st, "sd_hTb", [128, 512], BF16)
            BhTf, BhTb = Buf(), Buf()
            PY, BPY = self.pb[4], self.pbB[4]
            PI, BPI = self.pb[5], self.pbB[5]
            S.op("pool", lambda h: h.memset(hTf[:], 0.0), writes=[BhTf])
            S.op("pool", lambda h: h.memset(hTb[:], 0.0), writes=[BhTb])
            S.op("pool", lambda h: h.memset(ext[:], 0.0), writes=[Bext])
            S.op("pool", lambda h: h.memset(Cxj[:], 0.0), writes=[BCxj])
            chunks = self.chunks()
            nprompt = len(chunks) - 1
            pending = None
            for ci, (t0, L, nseq, blen, samp) in enumerate(chunks):
                nblk = L // blen
                W = 3 + blen
                Bxb = self.tile_bufs(self.Bxbf, t0, L)
                xk = lambda kc: self.xbf[:, kc, t0:t0 + L]
                extv = ext[:, :, 0:nblk * W].rearrange("p c (b w) -> p c b w", w=W)
                g8 = lambda slot, blen=blen, L=L: G8[:, slot, 0:L].rearrange("p (b t) -> p b t", t=blen)
                cols, Bcols = cols2[ci % 2], Bcols2[ci % 2]
                zs, Bzs = zs2[ci % 2], Bzs2[ci % 2]
                xD, BxD = xD2[ci % 2], BxD2[ci % 2]
                if samp:
                    S.op("pool", lambda h: h.memset(ext[:], 0.0), writes=[Bext])
                    for (c0, lc, n) in COLS:
                        S.op("sp", lambda h, c0=c0, lc=lc, n=n: h.dma_start(out=cvt[:, lc:lc + n], in_=d["st_cv"][j].rearrange("s w c -> (s w) c")[:, c0:c0 + n]),
                             writes=[Bcvt], dma=True)
                    pcv = self.psbank()
                    for cc in range(6):
                        self.mm(pcv[0][:, cc * 48:(cc + 1) * 48], cvt[0:48, cc * 128:(cc + 1) * 128], self.identf[0:48, 0:48], True, True, [Bcvt, self.Bconst], [pcv[1]])
                    S.op("dve", lambda h, pcv=pcv, extv=extv: h.tensor_copy(extv[:, :, 0:16, 0:3], pcv[0][:, 0:288].rearrange("p (c s w) -> p c s w", s=16, w=3)),
                         reads=[pcv[1]], writes=[Bext])
                elif ci > 0:
                    S.op("dve", lambda h, extv=extv, blen=blen: h.tensor_copy(extv[:, :, 0, 0:3], extv[:, :, 0, blen:blen + 3]), reads=[Bext], writes=[Bext])
                pd = self.psbank()
                for kc in range(KD):
                    self.mm(pd[0][0:8, 0:L], wdt[:, kc, :], xk(kc), kc == 0, kc == KD - 1, Bxb + [Bwdt[kc]], [pd[1]])
                px = self.psbank()
                for cc in range(4):
                    for kc in range(KD):
                        self.mm(px[0][:, cc * 128:cc * 128 + L], wx[:, kc, cc * 128:(cc + 1) * 128], xk(kc), kc == 0, kc == KD - 1, Bxb + [Bwx[kc]], [px[1]])
                pbc = self.psbank()
                for kc in range(KD):
                    self.mm(pbc[0][:, 0:L], wB[:, kc, :], xk(kc), kc == 0, kc == KD - 1, Bxb + [BwB[kc]], [pbc[1]])
                for kc in range(KD):
                    self.mm(pbc[0][:, 128:128 + L], wC[:, kc, :], xk(kc), kc == 0, kc == KD - 1, Bxb + [BwC[kc]], [pbc[1]])
                pz = (self.pb[6], self.pbB[6])
                for kc in range(KD):
                    self.mm(pz[0][0:L, :], self.xbf[:, kc, t0:t0 + L], wz[:, kc, :], kc == 0, kc == KD - 1, Bxb + [Bz[kc]], [pz[1]])
                S.op("act", lambda h, pd=pd, L=L: h.activation(out=G8[:, TMP, 0:L], in_=pd[0][0:8, 0:L], func=AF.Exp, bias=par8[:, 0:1], scale=1.0),
                     reads=[pd[1], Bp[0]], writes=[BG8])
                S.op("act", lambda h, L=L: h.activation(out=G8[:, DT, 0:L], in_=G8[:, TMP, 0:L], func=AF.Ln, bias=1.0, scale=1.0), reads=[BG8], writes=[BG8])
                S.op("act", lambda h, px=px, extv=extv, blen=blen: h.activation(out=extv[:, 0:4, :, 3:3 + blen], in_=px[0][:, :].rearrange("p (c b t) -> p c b t", c=4, t=blen), func=AF.Copy),
                     reads=[px[1]], writes=[Bext])
                S.op("act", lambda h, pbc=pbc, extv=extv, blen=blen: h.activation(out=extv[:, 4:6, :, 3:3 + blen], in_=pbc[0][:, 0:256].rearrange("p (c b t) -> p c b t", c=2, t=blen), func=AF.Copy),
                     reads=[pbc[1]], writes=[Bext])
                S.op("dve", lambda h, L=L: h.tensor_scalar(G8[:, TMP, 0:L], G8[:, DT, 0:L], par8[:, 1:2], None, ALU.mult), reads=[BG8, Bp[1]], writes=[BG8])
                d0 = MC["rst_s"][0:8, 0:L] if samp else onesf[0:8, 0:L]
                S.op("dve", lambda h, d0=d0, L=L: h.tensor_tensor_scan(G8[:, CS, 0:L], d0, G8[:, TMP, 0:L], 0.0, ALU.mult, ALU.add), reads=[BG8, Bmc], writes=[BG8])
                csL = g8(CS)[:, :, blen - 1]
                S.op("dve", lambda h, g8=g8, csL=csL, nblk=nblk, blen=blen: h.tensor_tensor(g8(TMP), csL.unsqueeze(2).broadcast_to([8, nblk, blen]), g8(CS), ALU.subtract),
                     reads=[BG8], writes=[BG8])
                S.op("act", lambda h, L=L: h.activation(out=G8[:, WL, 0:L], in_=G8[:, TMP, 0:L], func=AF.Exp), reads=[BG8], writes=[BG8])
                S.op("dve", lambda h, L=L: h.tensor_tensor(G8[:, WL, 0:L], G8[:, WL, 0:L], G8[:, DT, 0:L], ALU.mult), reads=[BG8], writes=[BG8])
                S.op("dve", lambda h, L=L: h.tensor_scalar(G8[:, NCS, 0:L], G8[:, CS, 0:L], -1.0, None, ALU.mult), reads=[BG8], writes=[BG8])
                S.op("act", lambda h, L=L: h.activation(out=G8[:, ECS, 0:L], in_=G8[:, CS, 0:L], func=AF.Exp), reads=[BG8], writes=[BG8])
                S.op("act", lambda h, csL=csL, nblk=nblk: h.activation(out=edL[:, 0:nblk], in_=csL, func=AF.Exp), reads=[BG8], writes=[BedL])
                pc = self.psbank()
                for qi, slot in enumerate((NCS, ECS, DT, WL)):
                    self.mm(pc[0][0:L, qi * 8:qi * 8 + 8], G8[:, slot, 0:L], id8, True, True, [BG8, self.Bconst], [pc[1]])
                S.op("dve", lambda h, pc=pc, L=L, cols=cols: h.tensor_copy(cols[0:L, :], pc[0][0:L, 0:32]), reads=[pc[1]], writes=[Bcols])
                S.op("dve", lambda h, nseq=nseq: h.tensor_tensor(edLD[:, :, 0:nseq], edL[:, 0:nseq].unsqueeze(1).broadcast_to([8, 8, nseq]),
                                                                 id8.unsqueeze(2).broadcast_to([8, 8, nseq]), ALU.mult),
                     reads=[BedL, self.Bconst], writes=[BedLD])
                pdc = self.psbank()
                if nseq == 16:
                    self.mm(pdc[0][:, 0:128], onesf[0:8, 0:128], edLD[:, :, :].rearrange("p h s -> p (h s)"), True, True, [BedLD, Bmc], [pdc[1]])
                    S.op("act", lambda h, pdc=pdc: h.activation(out=decb[:, :, 0:16], in_=pdc[0][:, 0:128].rearrange("p (h s) -> p h s", s=16), func=AF.Copy),
                         reads=[pdc[1]], writes=[Bdecb])
                else:
                    assert nseq == 1
                    self.mm(pdc[0][:, 0:8], onesf[0:8, 0:128], edLD[:, :, 0], True, True, [BedLD, Bmc], [pdc[1]])
                    S.op("act", lambda h, pdc=pdc: h.activation(out=decb[:, :, 0:1], in_=pdc[0][:, 0:8].unsqueeze(2), func=AF.Copy),
                         reads=[pdc[1]], writes=[Bdecb])
                if samp or ci == nprompt - 1:
                    nrow = 48 if samp else 3
                    if samp:
                        S.op("dve", lambda h, extv=extv: h.tensor_copy(nct[:, :, :].rearrange("p c (s w) -> p c s w", w=3), extv[:, :, 0:16, 4:7]), reads=[Bext], writes=[Bnct])
                    else:
                        S.op("dve", lambda h, extv=extv, blen=blen: h.tensor_copy(nct[:, :, 0:3], extv[:, :, 0, blen:blen + 3]), reads=[Bext], writes=[Bnct])
                    for half, (c_lo, c_hi) in enumerate(((0, 4), (4, 6))):
                        pco = self.psbank()
                        for cc in range(c_lo, c_hi):
                            lhs = nct[:, cc, 0:nrow]
                            self.mm(pco[0][0:nrow, (cc - c_lo) * 128:(cc - c_lo + 1) * 128], lhs, self.identf[:, :], True, True, [Bnct, self.Bconst], [pco[1]])
                        ncol = (c_hi - c_lo) * 128
                        S.op("dve", lambda h, pco=pco, c_lo=c_lo, ncol=ncol, nrow=nrow: h.tensor_copy(cvt[0:nrow, c_lo * 128:c_lo * 128 + ncol], pco[0][0:nrow, 0:ncol]),
                             reads=[pco[1]], writes=[Bcvt])
                    for (c0, lc, n) in COLS:
                        if samp:
                            dst = d["s_cv"][j].rearrange("s w c -> (s w) c")[:, c0:c0 + n]
                        else:
                            dst = d["p_cv"][j][:, c0:c0 + n]
                        S.op("sp", lambda h, dst=dst, lc=lc, n=n, nrow=nrow: h.dma_start(out=dst, in_=cvt[0:nrow, lc:lc + n]), reads=[Bcvt], dma=True)
                for cc in range(6):
                    eng = "dve"
                    a_ = acc[:, 0, 0:L].rearrange("p (b t) -> p b t", t=blen)
                    Ba_ = Bacc[cc % 2]
                    ci_ = CHI[cc]
                    S.op(eng, lambda h, a_=a_, cc=cc, ci_=ci_, extv=extv, blen=blen: h.tensor_scalar(a_, extv[:, cc, :, 0:blen], cw[:, ci_, 0:1], cb[:, ci_:ci_ + 1], ALU.mult, ALU.add),
                         reads=[Bext] + Bcw, writes=[Ba_])
                    for w in range(1, 4):
                        if eng == "dve":
                            S.op(eng, lambda h, a_=a_, cc=cc, ci_=ci_, w=w, extv=extv, blen=blen: h.scalar_tensor_tensor(a_, extv[:, cc, :, w:w + blen], cw[:, ci_, w:w + 1], a_, ALU.mult, ALU.add),
                                 reads=[Bext, Ba_] + Bcw, writes=[Ba_])
                        else:
                            t_ = ctmp[:, 0:L].rearrange("p (b t) -> p b t", t=blen)
                            S.op(eng, lambda h, t_=t_, cc=cc, ci_=ci_, w=w, extv=extv, blen=blen: h.tensor_scalar(t_, extv[:, cc, :, w:w + blen], cw[:, ci_, w:w + 1], None, ALU.mult),
                                 reads=[Bext] + Bcw, writes=[Bctmp])
                            S.op(eng, lambda h, a_=a_, t_=t_: h.tensor_tensor(a_, a_, t_, ALU.add), reads=[Ba_, Bctmp], writes=[Ba_])
                    S.op("act", lambda h, cc=cc, L=L: h.activation(out=xc[:, cc, 0:L], in_=acc[:, 0, 0:L], func=AF.Silu), reads=[Ba_], writes=[Bxc])
                S.op("act", lambda h, pz=pz, L=L, zs=zs: h.activation(out=zs[0:L, :], in_=pz[0][0:L, :], func=AF.Silu), reads=[pz[1]], writes=[Bzs])
                for cc in range(5):
                    self.tr(self.pbf[0:L, cc * 128:(cc + 1) * 128], xc[:, cc, 0:L], self.ident[:, :], [Bxc, self.Bconst], [self.BpbfB])
                S.op("act", lambda h, L=L: h.activation(out=xtok[0:L, :], in_=self.pbf[0:L, 0:640], func=AF.Copy), reads=[self.BpbfB], writes=[Bxtok])
                x3 = lambda t, L=L: t[0:L, 0:512].rearrange("p (h e) -> p h e", e=64)
                c3 = lambda k, L=L, cols=cols: cols[0:L, k * 8:(k + 1) * 8].unsqueeze(2).broadcast_to([L, 8, 64])
                S.op("dve", lambda h, x3=x3, c3=c3: h.tensor_tensor(x3(xdt), x3(xtok), c3(2), ALU.mult), reads=[Bxtok, Bcols], writes=[Bxdt])
                S.op("dve", lambda h, x3=x3, c3=c3: h.tensor_tensor(x3(xw), x3(xtok), c3(3), ALU.mult), reads=[Bxtok, Bcols], writes=[Bxw])
                pcb = self.psbank()
                self.mm(pcb[0][0:L, 0:L], xc[:, 4, 0:L], xc[:, 5, 0:L], True, True, [Bxc], [pcb[1]])
                S.op("act", lambda h, pcb=pcb, L=L: h.activation(out=CBT[0:L, 0:L], in_=pcb[0][0:L, 0:L], func=AF.Copy), reads=[pcb[1]], writes=[BCBT])
                S.op("dve", lambda h, x3=x3, L=L, xD=xD: h.tensor_tensor(x3(xD), x3(xtok), Dbc[0:L, :].unsqueeze(2).broadcast_to([L, 8, 64]), ALU.mult),
                     reads=[Bxtok, Bp[2]], writes=[BxD])
                if pending is not None:
                    pending()
                mneg = MC["mneg_s"] if samp else MC["mneg_p"]
                def half_front(hf, L=L, mneg=mneg):
                    S.op("dve", lambda h, hf=hf, L=L: h.tensor_tensor(csD[:, :, 0:L], G8[:, CS, 0:L].unsqueeze(1).broadcast_to([8, 4, L]),
                                                                     self.identf[0:8, 4 * hf:4 * hf + 4].unsqueeze(2).broadcast_to([8, 4, L]), ALU.mult),
                         reads=[BG8, self.Bconst], writes=[BcsD])
                    pbq = self.psbank()
                    self.mm(pbq[0][0:L, 0:512], onesf[0:8, 0:L], csD[:, :, :].rearrange("p a t -> p (a t)"), True, False, [BcsD, Bmc], [pbq[1]])
                    for q in range(4):
                        self.mm(pbq[0][0:L, q * 128:q * 128 + L], self.ident[0:L, 0:L], mneg[0:L, 0:L], False, q == 3, [Bmc, self.Bconst], [pbq[1]])
                    return pbq
                assert L == 128
                pbh = {0: half_front(0)}
                for hh in range(8):
                    e_, Be_ = E[hh % 2], BE[hh % 2]
                    pt_, Bpt_ = PT[hh % 2], BPT[hh % 2]
                    pbq = pbh[hh // 4]
                    q = hh % 4
                    S.op("act", lambda h, e_=e_, pbq=pbq, hh=hh, q=q, L=L, cols=cols: h.activation(out=e_[0:L, 0:L], in_=pbq[0][0:L, q * 128:q * 128 + L], func=AF.Exp, bias=cols[0:L, hh:hh + 1], scale=1.0),
                         reads=[pbq[1], Bcols], writes=[Be_])
                    S.op("dve", lambda h, e_=e_, pt_=pt_, L=L: h.tensor_tensor(pt_[0:L, 0:L], e_[0:L, 0:L], CBT[0:L, 0:L], ALU.mult),
                         reads=[Be_, BCBT], writes=[Bpt_])
                    if hh == 0:
                        pbh[1] = half_front(1)
                    self.mm(PY[0:L, hh * 64:(hh + 1) * 64], pt_[0:L, 0:L], xdt[0:L, hh * 64:(hh + 1) * 64], True, True, [Bpt_, Bxdt], [BPY])
                d3 = lambda jj: decb[:, :, jj:jj + 1].broadcast_to([128, 8, 64])
                h3 = hTf[:, :].rearrange("p (h e) -> p h e", e=64)
                if not samp:
                    self.mm(PI[0:L, :], xc[:, 5, 0:L], hTb[:, :], True, True, [Bxc, BhTb], [BPI])
                    pu = self.psbank()
                    self.mm(pu[0][:, :], xtok[0:L, 512:640], xw[0:L, :], True, True, [Bxtok, Bxw], [pu[1]])
                    S.op("dve", lambda h, d3=d3: h.tensor_tensor(h3, h3, d3(0), ALU.mult), reads=[BhTf, Bdecb], writes=[BhTf])
                    S.op("dve", lambda h, pu=pu: h.tensor_tensor(hTf[:, :], hTf[:, :], pu[0][:, :], ALU.add), reads=[BhTf, pu[1]], writes=[BhTf])
                    S.op("act", lambda h: h.activation(out=hTb[:, :], in_=hTf[:, :], func=AF.Copy), reads=[BhTf], writes=[BhTb])
                    if ci == nprompt - 1:
                        pt2 = self.psbank()
                        for pr in range(4):
                            self.mm(pt2[0][:, pr * 128:(pr + 1) * 128], hTf[:, pr * 128:(pr + 1) * 128], self.identf[:, :], True, True, [BhTf, self.Bconst], [pt2[1]])
                        S.op("act", lambda h, pt2=pt2: h.activation(out=hnat[:, :, :], in_=pt2[0][:, :].rearrange("p (r n) -> p r n", n=128), func=AF.Copy), reads=[pt2[1]], writes=[Bhnat])
                        dsth = d["p_h"][j][g * 8:(g + 1) * 8].rearrange("(pr hh) p n -> (hh p) pr n", hh=2)
                        S.op("sp", lambda h, dsth=dsth: h.dma_start(out=dsth, in_=hnat[:, :, :]), reads=[Bhnat], dma=True)
                else:
                    extf = ext[:, :, :].rearrange("p c w -> p (c w)")
                    hin = [extf[:, k * 512:(k + 1) * 512].rearrange("p (r n) -> p r n", n=128) for k in range(2)]
                    Bhin = [Buf(), Buf()]
                    S.op("dve", lambda h: h.memset(extf[:, 1024:1026], 0.0), writes=[Bext] + Bhin)
                    def load_state(jj):
                        srch = d["st_sh"][j][jj, g * 8:(g + 1) * 8].rearrange("(pr hh) p n -> (hh p) pr n", hh=2)
                        S.op("sp", lambda h, srch=srch, k=jj % 2: h.dma_start(out=hin[k], in_=srch), writes=[Bhin[jj % 2]], dma=True)
                    load_state(0)
                    for jj in range(nseq):
                        if jj + 1 < nseq:
                            load_state(jj + 1)
                        hin_, Bhin_ = hin[jj % 2], Bhin[jj % 2]
                        pt1 = self.psbank()
                        for pr in range(4):
                            self.mm(pt1[0][:, pr * 128:(pr + 1) * 128], hin_[:, pr, :], self.identf[:, :], True, True, [Bhin_, self.Bconst], [pt1[1]])
                        S.op("act", lambda h, pt1=pt1: h.activation(out=hTb[:, :], in_=pt1[0][:, :], func=AF.Copy), reads=[pt1[1]], writes=[BhTb])
                        S.op("dve", lambda h, jj=jj: h.tensor_tensor(Cxj[:, 0:64], xc[:, 5, 0:64], MC["qmask"][:, jj, 0:64], ALU.mult), reads=[Bxc, Bmc], writes=[BCxj])
                        self.mm(PI[0:L, :], Cxj[:, 0:L], hTb[:, :], jj == 0, jj == nseq - 1, [BCxj, BhTb], [BPI])
                        S.op("dve", lambda h, jj=jj, L=L: h.tensor_scalar(Bxj[0:L, :], xtok[0:L, 512:640], MC["tokmask"][0:L, jj:jj + 1], None, ALU.mult),
                             reads=[Bxtok, Bmc], writes=[BBxj])
                        pu = self.psbank()
                        self.mm(pu[0][:, :], Bxj[0:L, :], xw[0:L, :], True, True, [BBxj, Bxw], [pu[1]])
                        S.op("dve", lambda h, d3=d3, jj=jj, pt1=pt1: h.tensor_tensor(h3, pt1[0][:, :].rearrange("p (h e) -> p h e", e=64), d3(jj), ALU.mult),
                             reads=[pt1[1], Bdecb], writes=[BhTf])
                        S.op("dve", lambda h, pu=pu: h.tensor_tensor(hTf[:, :], hTf[:, :], pu[0][:, :], ALU.add), reads=[BhTf, pu[1]], writes=[BhTf])
                        pt2 = self.psbank()
                        for pr in range(4):
                            self.mm(pt2[0][:, pr * 128:(pr + 1) * 128], hTf[:, pr * 128:(pr + 1) * 128], self.identf[:, :], True, True, [BhTf, self.Bconst], [pt2[1]])
                        S.op("act", lambda h, pt2=pt2: h.activation(out=hnat[:, :, :], in_=pt2[0][:, :].rearrange("p (r n) -> p r n", n=128), func=AF.Copy), reads=[pt2[1]], writes=[Bhnat])
                        dsth = d["s_h"][j][jj, g * 8:(g + 1) * 8].rearrange("(pr hh) p n -> (hh p) pr n", hh=2)
                        S.op("sp", lambda h, dsth=dsth: h.dma_start(out=dsth, in_=hnat[:, :, :]), reads=[Bhnat], dma=True)
                def tail(t0=t0, L=L, cols=cols, Bcols=Bcols, zs=zs, Bzs=Bzs, xD=xD, BxD=BxD, c3=c3):
                    y3 = ytok[0:L, :].rearrange("p (h e) -> p h e", e=64)
                    S.op("dve", lambda h, y3=y3, c3=c3, L=L: h.tensor_tensor(y3, PI[0:L, :].rearrange("p (h e) -> p h e", e=64), c3(1), ALU.mult), reads=[BPI, Bcols], writes=[Bytok])
                    S.op("dve", lambda h, L=L: h.tensor_tensor(ytok[0:L, :], ytok[0:L, :], PY[0:L, :], ALU.add), reads=[Bytok, BPY], writes=[Bytok])
                    S.op("dve", lambda h, L=L, xD=xD: h.tensor_tensor(ytok[0:L, :], ytok[0:L, :], xD[0:L, :], ALU.add), reads=[Bytok, BxD], writes=[Bytok])
                    S.op("dve", lambda h, L=L, zs=zs: h.tensor_tensor(ytok[0:L, :], ytok[0:L, :], zs[0:L, :], ALU.mult), reads=[Bytok, Bzs], writes=[Bytok])
                    S.op("act", lambda h, L=L: h.activation(out=yn[0:L, :], in_=ytok[0:L, :], func=AF.Square, accum_out=ss[0:L, 0:1]), reads=[Bytok], writes=[Byn, Bss])
                    S.op("dve", lambda h, L=L: h.tensor_scalar(ss[0:L, :], ss[0:L, :], 1.0 / 512.0, LN_EPS, ALU.mult, ALU.add), reads=[Bss], writes=[Bss])
                    S.op("act", lambda h, L=L: h.activation(out=ss[0:L, :], in_=ss[0:L, :], func=AF.Ln), reads=[Bss], writes=[Bss])
                    S.op("act", lambda h, L=L: h.activation(out=ss[0:L, :], in_=ss[0:L, :], func=AF.Exp, scale=-0.5), reads=[Bss], writes=[Bss])
                    S.op("dve", lambda h, L=L: h.scalar_tensor_tensor(yn[0:L, :], ytok[0:L, :], ss[0:L, 0:1], normg[0:L, :], ALU.mult, ALU.mult),
                         reads=[Bytok, Bss, Bp[3], Byn], writes=[Byn])
                    for cc in range(4):
                        self.tr(self.pbf[:, cc * 128:cc * 128 + L], yn[0:L, cc * 128:(cc + 1) * 128], self.ident[0:L, 0:L], [Byn, self.Bconst], [self.BpbfB])
                    gs0 = (t0 // 512) * 512 if t0 < cfg.seq else t0
                    toff = t0 - gs0
                    S.op("act", lambda h, L=L, toff=toff: h.activation(out=ynT[:, :, toff:toff + L], in_=self.pbf[:, 0:512].rearrange("p (h t) -> p h t", t=128)[:, :, 0:L], func=AF.Copy),
                         reads=[self.BpbfB], writes=[BynT])
                    gend = t0 + L
                    if t0 >= cfg.seq or gend % 512 == 0 or gend == cfg.seq:
                        self.out_proj(wo, Bwo, 4, ynT, BynT, gs0, gend - gs0, first)
                pending = tail
            pending()
        S.barrier()


_W_NAMES = ("ab_w_in", "ab_ig_bias", "ab_fg_bias", "ab_ml_norm", "ab_gla_wa2", "ab_gla_ba", "ab_gla_norm",
            "ab_w_out", "ssd_w_in", "ssd_conv_w", "ssd_conv_b", "ssd_dt_bias", "ssd_a_log", "ssd_d", "ssd_norm",
            "ssd_w_out", "mlp_w1", "mlp_w2", "ln_mix_g", "ln_mix_b", "ln_mlp_g", "ln_mlp_b")


def run(cfg, inputs, ncores=8):
    prog = Prog(cfg)
    nc = prog.build()
    f = lambda a: np.ascontiguousarray(np.asarray(a, dtype=np.float32))
    W = {k: f(inputs[k]) for k in _W_NAMES}
    ns = cfg.nseq
    in_maps = []
    for c in range(ncores):
        m = dict(W)
        m["xp"] = f(inputs["x_prompt"][c])
        m["xs"] = f(inputs["x_sample"][c * ns:(c + 1) * ns]).reshape(cfg.ts_real, D)
        m["st_mC"] = f(inputs["state_mlstm_C"][:, c * ns:(c + 1) * ns])
        m["st_mn"] = f(inputs["state_mlstm_n"][:, c * ns:(c + 1) * ns])
        m["st_mm"] = f(inputs["state_mlstm_m"][:, c * ns:(c + 1) * ns])
        m["st_gS"] = f(inputs["state_gla_S"][:, c * ns:(c + 1) * ns])
        m["st_sh"] = f(inputs["state_ssd_h"][:, c * ns:(c + 1) * ns])
        m["st_cv"] = f(inputs["state_ssd_conv"][:, c * ns:(c + 1) * ns])
        in_maps.append(m)
    res = run_bass_kernel_spmd(nc, in_maps, core_ids=list(range(ncores)))
    R = res.results
    cat = lambda k, ax: np.concatenate([np.expand_dims(r[k], ax) if False else r[k] for r in R], axis=ax)
    y_prompt = np.stack([r["yp"] for r in R], 0)
    y_sample = np.concatenate([r["ys"].reshape(ns, cfg.slen, D) for r in R], 0)
    outs = [y_prompt, y_sample]
    for k in ("p_C", "p_n", "p_m", "p_S", "p_h", "p_cv"):
        outs.append(np.stack([r[k] for r in R], 1))
    for k in ("s_C", "s_n", "s_m", "s_S", "s_h", "s_cv"):
        outs.append(np.concatenate([r[k] for r in R], 1))
    return tuple(np.ascontiguousarray(o.astype(np.float32)) for o in outs)


def kernel(**inputs):
    return run(Cfg(), inputs)
```

```python
from contextlib import ExitStack
import numpy as np
import concourse.bass as bass
import concourse.mybir as mybir
from concourse.bass_utils import run_bass_kernel_spmd

F32 = mybir.dt.float32
BF16 = mybir.dt.bfloat16
AF = mybir.ActivationFunctionType
ALU = mybir.AluOpType
AX = mybir.AxisListType

ENGS = ("pe", "act", "dve", "pool", "sp")
NDMASEM = 8

D = 1024
KD = 8
DEPTH = 4
DFF = 4096
LN_EPS = 1e-5
DN_ALPHA = (2 * DEPTH) ** 0.25
AB_IN = 3096
SSD_IN = 5152
NEG = -30000.0


class Buf:
    __slots__ = ("name", "w", "r", "slot", "excl", "persist")

    def __init__(self, name="", excl=False, persist=False):
        self.name = name
        self.excl = excl
        self.slot = None
        self.persist = persist
        self.w = None
        self.r = []


class Slot:
    __slots__ = ("sem", "nd")

    def __init__(self):
        self.sem = None
        self.nd = 0


class Op:
    __slots__ = ("eng", "emit", "deps", "sig", "dma", "sem", "val", "chan")

    def __init__(self, eng, emit, dma):
        self.eng = eng
        self.emit = emit
        self.dma = dma
        self.deps = []
        self.sig = False
        self.sem = None
        self.val = 0
        self.chan = None


class Sched:
    def __init__(self, nc):
        self.nc = nc
        self.ops = {e: [] for e in ENGS}
        self.slots = []
        self.free = {e: [] for e in ENGS}
        self.live = []
        self.since_barrier = []

    def op(self, eng, emit, reads=(), writes=(), dma=False, extra=(), nobar=False):
        o = Op(eng, emit, dma)
        self.count = getattr(self, "count", 0) + 1
        if self.count > getattr(self, "cut", 1 << 60):
            return o
        deps = {}
        for b in reads:
            if b.w is not None:
                deps[id(b.w)] = (b.w, True)
            if b.excl:
                for r in b.r:
                    if id(r) not in deps:
                        deps[id(r)] = (r, False)
        for b in writes:
            if b.w is not None and id(b.w) not in deps:
                deps[id(b.w)] = (b.w, False)
            for r in b.r:
                if id(r) not in deps:
                    deps[id(r)] = (r, False)
        for d in extra:
            deps[id(d)] = (d, True)
        for d, raw in deps.values():
            if d is o:
                continue
            if d.eng == eng and not d.dma and not dma:
                if eng == "pe" or (not raw and eng != "pool"):
                    continue
            o.deps.append(d)
        for b in reads:
            b.r.append(o)
        for b in writes:
            b.w = o
            b.r = []
        self.ops[eng].append(o)
        if dma:
            ch = writes[0] if writes else reads[0]
            if ch.slot is None:
                if self.free[eng]:
                    ch.slot = self.free[eng].pop()
                else:
                    ch.slot = Slot()
                    self.slots.append(ch.slot)
                if not ch.persist:
                    self.live.append((ch, eng))
            ch.slot.nd += 1
            o.chan = ch.slot
            o.val = 16 * ch.slot.nd
            if not nobar:
                self.since_barrier.append(o)
        return o

    def barrier(self):
        last = [self.ops[e][-1] for e in ENGS if self.ops[e] and not self.ops[e][-1].dma]
        pend = list(self.since_barrier)
        self.since_barrier = []
        for e in ("pe", "act", "dve", "pool", "sp"):
            self.op(e, lambda h: h.nop(), extra=last + pend)
        for b, e in self.live:
            self.free[e].append(b.slot)
            b.slot = None
        self.live = []

    def finalize(self, stack):
        nc = self.nc
        for e in ENGS:
            for o in self.ops[e]:
                for d in o.deps:
                    d.sig = True
        self.esem = {}
        for e in ENGS:
            self.esem[e] = stack.enter_context(nc.semaphore("s_" + e))
        for i, sl in enumerate(self.slots):
            sl.sem = stack.enter_context(nc.semaphore("d%d" % i))
        for e in ENGS:
            cnt = 0
            for o in self.ops[e]:
                if o.dma:
                    o.sem = o.chan.sem
                elif o.sig:
                    cnt += 1
                    o.sem = self.esem[e]
                    o.val = cnt

    def emit_engine(self, e, h):
        waited = {}
        for o in self.ops[e]:
            need = {}
            for d in o.deps:
                k = d.sem.num
                if need.get(k, (None, 0))[1] < d.val:
                    need[k] = (d.sem, d.val)
            ws = []
            for k, (s, v) in need.items():
                if waited.get(k, 0) >= v:
                    continue
                waited[k] = v
                ws.append((s, v))
            for (s, v) in ws[1:]:
                h.wait_ge(s, v)
            ins = o.emit(h)
            if ws:
                ins._wait_ge(ws[0][0], ws[0][1])
            if o.dma:
                ins.then_inc(o.sem, 16)
            elif o.sig:
                ins.then_inc(o.sem, 1)
        if e == "sp":
            for sl in self.slots:
                if waited.get(sl.sem.num, 0) < 16 * sl.nd:
                    h.wait_ge(sl.sem, 16 * sl.nd)

    def run_block(self, block):
        s = self

        @block.tensor
        def _(h):
            s.emit_engine("pe", h)

        @block.scalar
        def _(h):
            s.emit_engine("act", h)

        @block.vector
        def _(h):
            s.emit_engine("dve", h)

        @block.gpsimd
        def _(h):
            s.emit_engine("pool", h)

        @block.sync
        def _(h):
            s.emit_engine("sp", h)


class Cfg:
    def __init__(self, seq=2048, nseq=16, slen=4, layers=(0, 1, 2, 3), parts=("mix", "mlp")):
        self.seq = seq
        self.nseq = nseq
        self.slen = slen
        self.ts_real = nseq * slen
        self.ts = 128
        self.T = seq + self.ts
        self.layers = tuple(layers)
        self.parts = tuple(parts)
        self.groups = [(i, min(512, seq - i)) for i in range(0, seq, 512)] + [(seq, self.ts)]
        self.tiles = [(i, 128) for i in range(0, seq, 128)] + [(seq, self.ts)]


class Prog:
    def __init__(self, cfg):
        self.cfg = cfg
        self.nc = bass.Bass("TRN2", target_bir_lowering=False)
        self.S = Sched(self.nc)
        self.S.cut = getattr(cfg, "cut", 1 << 60)
        self.rr = 0

    def sb(self, st, name, shape, dt):
        self.uid = getattr(self, "uid", 0) + 1
        return st.enter_context(self.nc.sbuf_tensor("%s_%d" % (name, self.uid), shape, dt))

    def psbank(self):
        rot = getattr(self, "rot", None)
        i = rot[self.rr % len(rot)] if rot else self.rr % getattr(self, "nrot", 4)
        self.rr += 1
        return self.pb[i], self.pbB[i]

    def tile_bufs(self, arr, t0, n):
        a = t0 // 128
        b = (t0 + n + 127) // 128
        return arr[a:b]

    def mm(self, out, lhsT, rhs, start, stop, reads, writes):
        return self.S.op("pe", lambda h: h.matmul(out, lhsT, rhs, start=start, stop=stop), reads, writes)

    def tr(self, out, in_, ident, reads, writes):
        if in_.dtype == BF16:
            return self.S.op("pe", lambda h: h.transpose(out, in_, ident), reads, writes)
        return self.S.op("pe", lambda h: h.matmul(out, in_, ident, start=True, stop=True), reads, writes)

    def build(self):
        cfg, nc, S = self.cfg, self.nc, self.S
        T = cfg.T
        dram = {}

        only = getattr(cfg, "only", None)

        def din(name, shape):
            if only is not None and name not in only:
                return None
            dram[name] = nc.dram_tensor(name, list(shape), F32, kind="ExternalInput").ap()
            return dram[name]

        def dout(name, shape):
            if only is not None and name not in only:
                return None
            dram[name] = nc.dram_tensor(name, list(shape), F32, kind="ExternalOutput").ap()
            return dram[name]

        self.dram = dram
        ns = cfg.nseq
        din("xp", [cfg.seq, D])
        din("xs", [cfg.ts_real, D])
        din("st_mC", [2, ns, 4, 64, 128])
        din("st_mn", [2, ns, 4, 64])
        din("st_mm", [2, ns, 4])
        din("st_gS", [2, ns, 4, 64, 128])
        din("st_sh", [2, ns, 32, 64, 128])
        din("st_cv", [2, ns, 3, 3072])
        din("ab_w_in", [2, D, AB_IN])
        din("ab_ig_bias", [2, 4])
        din("ab_fg_bias", [2, 4])
        din("ab_ml_norm", [2, 512])
        din("ab_gla_wa2", [2, 16, 256])
        din("ab_gla_ba", [2, 256])
        din("ab_gla_norm", [2, 512])
        din("ab_w_out", [2, D, D])
        din("ssd_w_in", [2, D, SSD_IN])
        din("ssd_conv_w", [2, 4, 3072])
        din("ssd_conv_b", [2, 3072])
        din("ssd_dt_bias", [2, 32])
        din("ssd_a_log", [2, 32])
        din("ssd_d", [2, 32])
        din("ssd_norm", [2, 2048])
        din("ssd_w_out", [2, 2048, D])
        din("mlp_w1", [DEPTH, D, DFF])
        din("mlp_w2", [DEPTH, DFF, D])
        din("ln_mix_g", [DEPTH, D])
        din("ln_mix_b", [DEPTH, D])
        din("ln_mlp_g", [DEPTH, D])
        din("ln_mlp_b", [DEPTH, D])
        dout("yp", [cfg.seq, D])
        dout("ys", [cfg.ts_real, D])
        dout("p_C", [2, 4, 64, 128])
        dout("p_n", [2, 4, 64])
        dout("p_m", [2, 4])
        dout("p_S", [2, 4, 64, 128])
        dout("p_h", [2, 32, 64, 128])
        dout("p_cv", [2, 3, 3072])
        dout("s_C", [2, ns, 4, 64, 128])
        dout("s_n", [2, ns, 4, 64])
        dout("s_m", [2, ns, 4])
        dout("s_S", [2, ns, 4, 64, 128])
        dout("s_h", [2, ns, 32, 64, 128])
        dout("s_cv", [2, ns, 3, 3072])

        with ExitStack() as st:
            self.st = st
            self.xres = self.sb(st, "xres", [128, KD, T], F32)
            self.xbf = self.sb(st, "xbf", [128, KD, T], BF16)
            ntile = len(cfg.tiles)
            self.Bxres = [Buf("xres%d" % i) for i in range(ntile)]
            self.Bxbf = [Buf("xbf%d" % i) for i in range(ntile)]
            WCAP = 16512
            self.wbuf = [self.sb(st, "wbuf%d" % i, [128, WCAP], BF16) for i in range(2)]
            self.Bw = [[Buf("w%d_%d" % (i, j), persist=True) for j in range(24)] for i in range(2)]
            self.wslot = 0
            self.identf = self.sb(st, "identf", [128, 128], F32)
            self.ident = self.sb(st, "ident", [128, 128], BF16)
            self.ones_bf = self.sb(st, "ones_bf", [128, 128], BF16)
            self.lnp = self.sb(st, "lnp", [128, 4 * DEPTH, KD], F32)
            self.Bconst = Buf("const")
            self.pb = [st.enter_context(nc.psum_tensor("pb%d" % i, [128, 512], F32)) for i in range(7)]
            self.pbB = [Buf("pb%d" % i, excl=True) for i in range(7)]
            self.pbf = st.enter_context(nc.psum_tensor("pbf", [128, 1024], BF16))
            self.BpbfB = Buf("pbf", excl=True)

            self.setup_consts()
            self.wq = self.weight_specs()
            self.wq_i = 0
            self.wq_ready = self.load_w(self.wq[0]) if self.wq else None
            self.load_x()
            for l in cfg.layers:
                if "mix" in cfg.parts:
                    if l % 2 == 0:
                        self.ab_layer(l)
                    else:
                        self.ssd_layer(l)
                if "mlp" in cfg.parts:
                    self.mlp_layer(l)
            self.store_y()
            S.finalize(st)
            with nc.Block() as block:
                S.run_block(block)
        return nc

    def setup_consts(self):
        S, d = self.S, self.dram
        identf, ident, ones_bf = self.identf, self.ident, self.ones_bf
        Bc = self.Bconst
        S.op("pool", lambda h: h.memset(identf[:], 0.0), writes=[Bc])
        S.op("pool", lambda h: h.affine_select(out=identf[:], in_=identf[:], pattern=[[-1, 128]],
                                               compare_op=ALU.not_equal, fill=1.0, base=0,
                                               channel_multiplier=1), reads=[Bc], writes=[Bc])
        S.op("dve", lambda h: h.tensor_copy(ident[:], identf[:]), reads=[Bc], writes=[Bc])
        S.op("dve", lambda h: h.memset(ones_bf[:], 1.0), writes=[Bc])

    def load_x(self):
        cfg, S, d = self.cfg, self.S, self.dram
        with ExitStack() as st:
            xin = [self.sb(st, "xin%d" % i, [128, D], F32) for i in range(2)]
            Bxin = [Buf() for _ in range(2)]
            lnraw = self.sb(st, "lnraw", [128, 128], F32)
            Blr = [Buf() for _ in range(4)]
            for k, nm in enumerate(("ln_mix_g", "ln_mix_b", "ln_mlp_g", "ln_mlp_b")):
                S.op("sp", lambda h, k=k, nm=nm: h.dma_start(out=lnraw[k * 32:(k + 1) * 32, :], in_=d[nm].rearrange("l (c p) -> (l c) p", p=128)),
                     writes=[Blr[k]], dma=True)
            pl, Bpl = self.psbank()
            self.mm(pl[:, 0:128], lnraw[:, :], self.identf[:, :], True, True, Blr + [self.Bconst], [Bpl])
            S.op("dve", lambda h, pl=pl: h.tensor_copy(self.lnp[:, :, :].rearrange("p a c -> p (a c)"), pl[:, 0:128]), reads=[Bpl], writes=[self.Bconst])
            S.op("pool", lambda h: h.memset(self.xres[:, :, cfg.seq:cfg.T], 0.0), writes=[self.Bxres[-1]])
            S.op("pool", lambda h: h.memset(self.xbf[:, :, cfg.seq:cfg.T], 0.0), writes=[self.Bxbf[-1]])
            for ti, (t0, n) in enumerate(cfg.tiles):
                if t0 >= cfg.seq:
                    n = cfg.ts_real
                src = d["xp"][t0:t0 + n, :] if t0 < cfg.seq else d["xs"][:, :]
                xi, Bx = xin[ti % 2], Bxin[ti % 2]
                S.op("sp", lambda h, xi=xi, src=src, n=n: h.dma_start(out=xi[0:n, :], in_=src), writes=[Bx], dma=True)
                for half in range(2):
                    pt, Bp = self.psbank()
                    for c4 in range(4):
                        c = half * 4 + c4
                        self.tr(pt[:, c4 * 128:c4 * 128 + n], xi[0:n, c * 128:(c + 1) * 128], self.identf[0:n, 0:n],
                                [Bx, self.Bconst], [Bp])
                    pv = pt[:, :].rearrange("p (c t) -> p c t", t=128)[:, :, 0:n]
                    xr = self.xres[:, half * 4:half * 4 + 4, t0:t0 + n]
                    xb = self.xbf[:, half * 4:half * 4 + 4, t0:t0 + n]
                    S.op("dve", lambda h, xr=xr, pv=pv: h.tensor_copy(xr, pv), reads=[Bp], writes=[self.Bxres[ti]])
                    S.op("act", lambda h, xb=xb, pv=pv: h.activation(out=xb, in_=pv, func=AF.Copy), reads=[Bp],
                         writes=[self.Bxbf[ti]])
        S.barrier()

    def store_y(self):
        cfg, S, d = self.cfg, self.S, self.dram
        S.barrier()
        with ExitStack() as st:
            yo = [self.sb(st, "yo%d" % i, [128, D], F32) for i in range(2)]
            Byo = [Buf() for _ in range(2)]
            for ti, (t0, n) in enumerate(cfg.tiles):
                if t0 >= cfg.seq:
                    n = cfg.ts_real
                dst = d["yp"][t0:t0 + n, :] if t0 < cfg.seq else d["ys"][:, :]
                y, By = yo[ti % 2], Byo[ti % 2]
                for half in range(2):
                    pt, Bp = self.psbank()
                    for c4 in range(4):
                        c = half * 4 + c4
                        self.tr(pt[0:n, c4 * 128:(c4 + 1) * 128], self.xres[:, c, t0:t0 + n], self.identf[:, :],
                                [self.Bxres[ti], self.Bconst], [Bp])
                    eng = "dve" if half == 0 else "act"
                    if eng == "dve":
                        S.op("dve", lambda h, y=y, pt=pt, half=half, n=n: h.tensor_copy(y[0:n, half * 512:(half + 1) * 512], pt[0:n, :]),
                             reads=[Bp], writes=[By])
                    else:
                        S.op("act", lambda h, y=y, pt=pt, half=half, n=n: h.activation(out=y[0:n, half * 512:(half + 1) * 512], in_=pt[0:n, :], func=AF.Copy),
                             reads=[Bp], writes=[By])
                S.op("sp", lambda h, y=y, dst=dst, n=n: h.dma_start(out=dst, in_=y[0:n, :]), reads=[By], dma=True)

    def weight_specs(self):
        cfg, d = self.cfg, self.dram
        skip = getattr(cfg, "skip", ())
        out = []
        for l in cfg.layers:
            j = l // 2
            if "mix" in cfg.parts:
                if l % 2 == 0:
                    w_in, w_out = d["ab_w_in"][j], d["ab_w_out"][j]
                    if "ml" not in skip:
                        out.append([(w_in[:, 0:1544], 8, 1544), (w_out[0:512, :], 4, 1024)])
                    if "gla" not in skip:
                        out.append([(w_in[:, 1544:3096], 8, 1552), (w_out[512:1024, :], 4, 1024)])
                else:
                    w_in, w_out = d["ssd_w_in"][j], d["ssd_w_out"][j]
                    for g in getattr(cfg, "ssd_groups", (0, 1, 2, 3)):
                        out.append([(w_in[:, g * 512:(g + 1) * 512], 8, 512),
                                    (w_in[:, 2048 + g * 512:2048 + (g + 1) * 512], 8, 512),
                                    (w_in[:, 4096 + g * 128:4096 + (g + 1) * 128], 8, 128),
                                    (w_in[:, 4608 + g * 128:4608 + (g + 1) * 128], 8, 128),
                                    (w_in[:, 5120 + g * 8:5120 + (g + 1) * 8], 8, 8),
                                    (w_out[g * 512:(g + 1) * 512, :], 4, 1024)])
            if "mlp" in cfg.parts:
                w1, w2 = d["mlp_w1"][l], d["mlp_w2"][l]
                for q in range(4):
                    out.append([(w1[:, q * 1024:(q + 1) * 1024], 8, 1024), (w2[q * 1024:(q + 1) * 1024, :], 8, 1024)])
        return out

    def take_w(self):
        if not hasattr(self, "wq"):
            self.wq = self.weight_specs()
            self.wq_i = 0
            self.wq_ready = self.load_w(self.wq[0]) if self.wq else None
        cur = self.wq_ready
        self.wq_i += 1
        self.wq_ready = self.load_w(self.wq[self.wq_i]) if self.wq_i < len(self.wq) else None
        self.cur_slot = cur[2]
        return cur[0], cur[1]

    def load_w(self, pieces):
        S = self.S
        slot = self.wslot
        self.wslot ^= 1
        wb = self.wbuf[slot]
        old_users = []
        for B_ in self.Bw[slot]:
            old_users.extend(B_.r)
            if B_.w is not None:
                old_users.append(B_.w)
        off = 0
        views, bufs = [], []
        j = 0
        for (src, k, cols) in pieces:
            v = wb[:, off:off + k * cols].rearrange("p (k c) -> p k c", c=cols)
            bl = []
            if cols < 512:
                B = self.Bw[slot][j % 24]
                j += 1
                S.op("pool", lambda h, v=v, src=src: h.dma_start(out=v, in_=src.rearrange("(k p) c -> p k c", p=128)), writes=[B], dma=True, nobar=True, extra=old_users)
                views.append(v)
                bufs.append([B] * k)
                off += k * cols
                continue
            for kk in range(k):
                B = self.Bw[slot][j % 24]
                j += 1
                S.op("pool", lambda h, v=v, src=src, kk=kk: h.dma_start(out=v[:, kk, :], in_=src[kk * 128:(kk + 1) * 128, :]),
                     writes=[B], dma=True, nobar=True, extra=old_users)
                bl.append(B)
            views.append(v)
            bufs.append(bl)
            off += k * cols
        assert off <= 16512, off
        return views, bufs, slot

    def layer_norm(self, kind, l, t0, n, sc):
        if n > 256:
            for o in range(0, n, 256):
                self.layer_norm(kind, l, t0 + o, min(256, n - o), sc)
            return
        S = self.S
        Bxr = self.tile_bufs(self.Bxres, t0, n)
        Bxb = self.tile_bufs(self.Bxbf, t0, n)
        u = self.xres[:, :, t0:t0 + n]
        ub, usq, mean, var, rstd, nmr, tmp = sc["ub"], sc["usq"], sc["mean"], sc["var"], sc["rstd"], sc["nmr"], sc["tmp"]
        Bub, Busq, Bst, Btmp = sc["Bub"], sc["Busq"], sc["Bst"], sc["Btmp"]
        S.op("act", lambda h: h.activation(out=ub[:, :, 0:n], in_=u, func=AF.Copy), reads=Bxr, writes=[Bub])
        S.op("act", lambda h: h.activation(out=usq[:, :, 0:n], in_=u, func=AF.Square), reads=Bxr, writes=[Busq])
        p1, B1 = self.psbank()
        for c in range(KD):
            self.mm(p1[:, 0:n], self.ones_bf[:, :], ub[:, c, 0:n], c == 0, c == KD - 1, [Bub, self.Bconst], [B1])
        p2, B2 = self.psbank()
        for c in range(KD):
            self.mm(p2[:, 0:n], self.ones_bf[:, :], usq[:, c, 0:n], c == 0, c == KD - 1, [Busq, self.Bconst], [B2])
        S.op("dve", lambda h: h.tensor_scalar(mean[:, 0:n], p1[:, 0:n], 1.0 / D, None, ALU.mult), reads=[B1], writes=[Bst])
        S.op("dve", lambda h: h.tensor_tensor(var[:, 0:n], mean[:, 0:n], mean[:, 0:n], ALU.mult), reads=[Bst], writes=[Bst])
        S.op("dve", lambda h: h.scalar_tensor_tensor(var[:, 0:n], p2[:, 0:n], 1.0 / D, var[:, 0:n], ALU.mult, ALU.subtract),
             reads=[B2, Bst], writes=[Bst])
        S.op("dve", lambda h: h.tensor_scalar(var[:, 0:n], var[:, 0:n], LN_EPS, None, ALU.add), reads=[Bst], writes=[Bst])
        S.op("act", lambda h: h.activation(out=rstd[:, 0:n], in_=var[:, 0:n], func=AF.Ln), reads=[Bst], writes=[Bst])
        S.op("act", lambda h: h.activation(out=rstd[:, 0:n], in_=rstd[:, 0:n], func=AF.Exp, scale=-0.5), reads=[Bst], writes=[Bst])
        S.op("dve", lambda h: h.scalar_tensor_tensor(nmr[:, 0:n], mean[:, 0:n], -1.0, rstd[:, 0:n], ALU.mult, ALU.mult),
             reads=[Bst], writes=[Bst])
        gi = (0 if kind == "mix" else 2) * DEPTH + l
        bi = gi + DEPTH
        for c in range(KD):
            tc_ = tmp[:, c % 2, 0:n]
            Bt = Btmp[c % 2]
            eng = "dve"
            S.op(eng, lambda h, tc_=tc_, c=c: h.tensor_tensor(tc_, self.xres[:, c, t0:t0 + n], rstd[:, 0:n], ALU.mult),
                 reads=Bxr + [Bst], writes=[Bt])
            S.op(eng, lambda h, tc_=tc_: h.tensor_tensor(tc_, tc_, nmr[:, 0:n], ALU.add), reads=[Bt, Bst], writes=[Bt])
            g_ap = self.lnp[:, gi, c:c + 1]
            b_ap = self.lnp[:, bi, c:c + 1]
            S.op("act", lambda h, tc_=tc_, c=c, g_ap=g_ap, b_ap=b_ap: h.activation(out=self.xres[:, c, t0:t0 + n], in_=tc_, func=AF.Identity,
                                                                                   bias=b_ap, scale=g_ap),
                 reads=[Bt, self.Bconst], writes=Bxr)
            S.op("act", lambda h, tc_=tc_, c=c, g_ap=g_ap, b_ap=b_ap: h.activation(out=self.xbf[:, c, t0:t0 + n], in_=tc_, func=AF.Identity,
                                                                                   bias=b_ap, scale=g_ap),
                 reads=[Bt, self.Bconst], writes=Bxb)

    def ln_scratch(self, st):
        sc = {}
        sc["ub"] = self.sb(st, "ln_ub", [128, KD, 256], BF16)
        sc["usq"] = self.sb(st, "ln_usq", [128, KD, 256], BF16)
        for nm in ("mean", "var", "rstd", "nmr"):
            sc[nm] = self.sb(st, "ln_" + nm, [128, 256], F32)
        sc["tmp"] = self.sb(st, "ln_tmp", [128, 2, 256], F32)
        sc["Bub"], sc["Busq"], sc["Bst"] = Buf(), Buf(), Buf()
        sc["Btmp"] = [Buf(), Buf()]
        return sc

    def accum_u(self, first, ps, c, t0, n):
        S = self.S
        pt, Bp = ps
        Bxr = self.tile_bufs(self.Bxres, t0, n)
        xr = self.xres[:, c, t0:t0 + n]
        if first:
            S.op("dve", lambda h: h.scalar_tensor_tensor(xr, xr, DN_ALPHA, pt[:, 0:n], ALU.mult, ALU.add),
                 reads=Bxr + [Bp], writes=Bxr)
        else:
            S.op("dve", lambda h: h.tensor_tensor(xr, xr, pt[:, 0:n], ALU.add), reads=Bxr + [Bp], writes=Bxr)

    def mlp_layer(self, l):
        cfg, S, d = self.cfg, self.S, self.dram
        self.nrot = 7
        self.rr = 0
        with ExitStack() as st:
            hq = [self.sb(st, "hq%d" % i, [128, 8, 512], BF16) for i in range(2)]
            Bhq = [Buf(), Buf()]
            rl = [self.sb(st, "rl%d" % i, [128, 512], BF16) for i in range(2)]
            Brl = [Buf(), Buf()]
            sc = self.ln_scratch(st)
            w1 = d["mlp_w1"][l]
            w2 = d["mlp_w2"][l]
            gi = 0
            ri = 0
            for q in range(4):
                (w1v, w2v), (B1, B2) = self.take_w()
                def w1_part(t0, n, hh, Bh, w1v=w1v, B1=B1):
                    nonlocal ri
                    Bxb = self.tile_bufs(self.Bxbf, t0, n)
                    for hc in range(8):
                        ps = self.psbank()
                        for kc in range(KD):
                            self.mm(ps[0][:, 0:n], w1v[:, kc, hc * 128:(hc + 1) * 128], self.xbf[:, kc, t0:t0 + n],
                                    kc == 0, kc == KD - 1, Bxb + [B1[kc]], [ps[1]])
                        r, Br = rl[ri % 2], Brl[ri % 2]
                        ri += 1
                        S.op("act", lambda h, r=r, ps=ps, n=n: h.activation(out=r[:, 0:n], in_=ps[0][:, 0:n], func=AF.Relu),
                             reads=[ps[1]], writes=[Br])
                        S.op("dve", lambda h, r=r, hh=hh, hc=hc, n=n: h.tensor_tensor(hh[:, hc, 0:n], r[:, 0:n], r[:, 0:n], ALU.mult),
                             reads=[Br], writes=[Bh])

                def w2_part(t0, n, hh, Bh, q=q, w2v=w2v, B2=B2):
                    for fc in range(KD):
                        ps = self.psbank()
                        for hc in range(8):
                            self.mm(ps[0][:, 0:n], w2v[:, hc, fc * 128:(fc + 1) * 128], hh[:, hc, 0:n],
                                    hc == 0, hc == 7, [Bh, B2[hc]], [ps[1]])
                        self.accum_u(q == 0, ps, fc, t0, n)
                    if q == 3:
                        self.layer_norm("mlp", l, t0, n, sc)

                prev = None
                for (t0, n) in cfg.groups:
                    hh, Bh = hq[gi % 2], Bhq[gi % 2]
                    gi += 1
                    w1_part(t0, n, hh, Bh)
                    if prev is not None:
                        w2_part(*prev)
                    prev = (t0, n, hh, Bh)
                w2_part(*prev)
        S.barrier()
        self.nrot = 4
        self.rr = 0

    def mixer_consts(self, st, small=False, need01=True):
        S = self.S
        K_ = {}
        B = Buf("mixconst")
        K_["B"] = B
        ns, sl = 128 // self.cfg.slen, self.cfg.slen
        ts = 128
        mneg_p = self.sb(st, "mneg_p", [128, 128], BF16)
        m01_p = self.sb(st, "m01_p", [128, 128], BF16) if need01 else None
        mneg_s = self.sb(st, "mneg_s", [128, 128], BF16)
        m01_s = self.sb(st, "m01_s", [128, 128], BF16) if need01 else None
        tokmask = self.sb(st, "tokmask", [128, 16], BF16)
        QW = 64 if small else 128
        qmask = self.sb(st, "qmask", [128, 16, QW], BF16)
        onesf = self.sb(st, "onesf", [128, 128], F32)
        rst_s = self.sb(st, "rst_s", [128, 128], F32)
        with ExitStack() as st2:
            t1 = self.sb(st2, "mc_t1", [128, 128], F32)
            t2 = self.sb(st2, "mc_t2", [128, 128], F32)
            t3 = self.sb(st2, "mc_t3", [128, 16, QW], F32)
            Bt = Buf()
            S.op("dve", lambda h: h.memset(onesf[:], 1.0), writes=[B])
            S.op("pool", lambda h: h.memset(t1[:], 1.0), writes=[Bt])
            S.op("pool", lambda h: h.affine_select(out=t1[:], in_=t1[:], pattern=[[1, 128]], compare_op=ALU.is_ge, fill=0.0,
                                                   base=0, channel_multiplier=-1), reads=[Bt], writes=[Bt])
            if need01:
                S.op("dve", lambda h: h.tensor_copy(m01_p[:], t1[:]), reads=[Bt], writes=[B])
            S.op("dve", lambda h: h.tensor_scalar(mneg_p[:], t1[:], -1.0, -NEG, ALU.add, ALU.mult), reads=[Bt], writes=[B])
            S.op("pool", lambda h: h.memset(t2[:, 0:ns], 1.0), writes=[Bt])
            S.op("pool", lambda h: h.affine_select(out=t2[:, 0:ns], in_=t2[:, 0:ns], pattern=[[-sl, ns]], compare_op=ALU.is_ge, fill=0.0,
                                                   base=0, channel_multiplier=1), reads=[Bt], writes=[Bt])
            S.op("pool", lambda h: h.affine_select(out=t2[:, 0:ns], in_=t2[:, 0:ns], pattern=[[sl, ns]], compare_op=ALU.is_ge, fill=0.0,
                                                   base=sl - 1, channel_multiplier=-1), reads=[Bt], writes=[Bt])
            S.op("dve", lambda h: h.tensor_copy(tokmask[:], t2[:, 0:16]), reads=[Bt], writes=[B])
            if need01:
                S.op("dve", lambda h: h.memset(m01_s[:], 0.0), writes=[B])
            S.op("dve", lambda h: h.memset(mneg_s[:], 0.0), writes=[B])
            bd = t2[0:ts, 0:ns].unsqueeze(2).broadcast_to([ts, ns, sl])
            S.op("dve", lambda h: h.tensor_tensor(t1[0:ts, 0:ts].rearrange("p (j t) -> p j t", t=sl), t1[0:ts, 0:ts].rearrange("p (j t) -> p j t", t=sl), bd, ALU.mult),
                 reads=[Bt], writes=[Bt])
            if need01:
                S.op("dve", lambda h: h.tensor_copy(m01_s[0:ts, 0:ts], t1[0:ts, 0:ts]), reads=[Bt], writes=[B])
            S.op("dve", lambda h: h.tensor_scalar(mneg_s[0:ts, 0:ts], t1[0:ts, 0:ts], -1.0, -NEG, ALU.add, ALU.mult), reads=[Bt], writes=[B])
            S.op("pool", lambda h: h.memset(t3[:], 1.0), writes=[Bt])
            S.op("pool", lambda h: h.affine_select(out=t3[:], in_=t3[:], pattern=[[-sl, 16], [1, QW]], compare_op=ALU.is_ge, fill=0.0,
                                                   base=0, channel_multiplier=0), reads=[Bt], writes=[Bt])
            S.op("pool", lambda h: h.affine_select(out=t3[:], in_=t3[:], pattern=[[sl, 16], [-1, QW]], compare_op=ALU.is_ge, fill=0.0,
                                                   base=sl - 1, channel_multiplier=0), reads=[Bt], writes=[Bt])
            S.op("dve", lambda h: h.tensor_copy(qmask[:], t3[:]), reads=[Bt], writes=[B])
            S.op("dve", lambda h: h.memset(rst_s[:], 1.0), writes=[B])
            S.op("dve", lambda h: h.memset(rst_s[:, :].rearrange("p (j t) -> p j t", t=sl)[:, :, 0:1], 0.0), reads=[B], writes=[B])
            S.barrier()
        K_.update(mneg_p=mneg_p, m01_p=m01_p, mneg_s=mneg_s, m01_s=m01_s, tokmask=tokmask, qmask=qmask, onesf=onesf, rst_s=rst_s)
        return K_

    def chunks(self):
        cfg = self.cfg
        out = [(t0, 128, 1, 128, False) for t0 in range(0, cfg.seq, 128)]
        out.append((cfg.seq, cfg.ts, cfg.nseq, cfg.slen, True))
        return out

    def out_proj(self, wo, Bwo, nrow, srcT, Bsrc, t0, n, first):
        for fc in range(KD):
            ps = self.psbank()
            for r in range(nrow):
                self.mm(ps[0][:, 0:n], wo[:, r, fc * 128:(fc + 1) * 128], srcT[:, r, 0:n], r == 0, r == nrow - 1,
                        [Bsrc, Bwo[r]], [ps[1]])
            self.accum_u(first, ps, fc, t0, n)

    def ab_layer(self, l):
        j = l // 2
        skip = getattr(self.cfg, "skip", ())
        first = True
        if "ml" not in skip:
            self.ml_phase(l, j, first)
            first = False
        if "gla" not in skip:
            self.gla_phase(l, j, first)
        self.rot = None
        self.rr = 0
        with ExitStack() as st:
            sc = self.ln_scratch(st)
            for (t0, n) in self.cfg.groups:
                self.layer_norm("mix", l, t0, n, sc)
        self.S.barrier()

    def ml_phase(self, l, j, first):
        cfg, S, d = self.cfg, self.S, self.dram
        self.rot = [0, 1, 2, 3, 6]
        self.rr = 0
        NS = cfg.nseq
        with ExitStack() as st:
            MC = self.mixer_consts(st, need01=False)
            Bmc = MC["B"]
            onesf = MC["onesf"]
            w_in = d["ab_w_in"][j]
            w_out = d["ab_w_out"][j]
            (wv, wo), (Bwi, Bwo) = self.take_w()
            gb = self.sb(st, "ml_gb", [4, 2], F32)
            gnorm = self.sb(st, "ml_gnorm", [128, 512], BF16)
            NB = 128 // cfg.slen
            m0T = self.sb(st, "ml_m0T", [4, NB], F32)
            zc = self.sb(st, "ml_zc", [4, 1], F32)
            Bp = Buf("mlparams")
            Bp2 = Buf("mlparams2")
            Bp3 = Buf("mlparams3")
            Bp4 = Buf("mlparams4")
            S.op("sp", lambda h: h.dma_start(out=gb[:, 0:1], in_=d["ab_ig_bias"][j].rearrange("(h o) -> h o", o=1)), writes=[Bp], dma=True)
            S.op("sp", lambda h: h.dma_start(out=gb[:, 1:2], in_=d["ab_fg_bias"][j].rearrange("(h o) -> h o", o=1)), writes=[Bp2], dma=True)
            S.op("dve", lambda h: h.tensor_scalar(gb[:, 1:2], gb[:, 1:2], -1.0, None, ALU.mult), reads=[Bp2], writes=[Bp2])
            S.op("pool", lambda h: h.dma_start(out=gnorm[:], in_=d["ab_ml_norm"][j].partition_broadcast(128)), writes=[Bp3], dma=True)
            S.op("dve", lambda h: h.memset(m0T[:], 0.0), writes=[Bp4])
            S.op("sp", lambda h: h.dma_start(out=m0T[:, 0:NS], in_=d["st_mm"][j].rearrange("s h -> h s"), allow_slow_non_contiguous=True),
                 reads=[Bp4], writes=[Bp4], dma=True)
            S.op("dve", lambda h: h.memset(zc[:], 0.0), writes=[Bp])
            Bpar = [Bp, Bp2, Bp3, Bp4]
            NSLOT = 9
            IG, SP, CSP, A_, M_, NEGM, WE, ENM, TMP = range(9)
            G1 = self.sb(st, "ml_G", [4, NSLOT, 128], F32)
            G = [G1, G1]
            BG1 = Buf()
            BG = [BG1, BG1]
            carry = self.sb(st, "ml_carry", [4, 2], F32)
            Bcarry = Buf()
            RW = self.sb(st, "ml_RW", [4, 160], F32)
            RWD = self.sb(st, "ml_RWD", [4, 4, 160], F32)
            negMD = self.sb(st, "ml_negMD", [4, 4, 128], F32)
            blk = self.sb(st, "ml_blk", [4, 2, NB], F32)
            BRW, BRWD, BnMD, Bblk = Buf(), Buf(), Buf(), Buf()
            cols = self.sb(st, "ml_cols", [128, 12], F32)
            Bcols = Buf()
            qT = self.sb(st, "ml_qT", [128, 2, 128], BF16)
            kT = self.sb(st, "ml_kT", [128, 2, 128], BF16)
            BqT, BkT = Buf(), Buf()
            ktok = self.sb(st, "ml_ktok", [128, 256], BF16)
            vext = self.sb(st, "ml_vext", [128, 4, 129], BF16)
            gs = self.sb(st, "ml_gs", [128, 512], BF16)
            Bktok, Bvext, Bgs = Buf(), Buf(), Buf()
            E = [self.sb(st, "ml_E%d" % i, [128, 128], BF16) for i in range(2)]
            PT = [self.sb(st, "ml_PT%d" % i, [128, 128], BF16) for i in range(2)]
            BE, BPT = [Buf(), Buf()], [Buf(), Buf()]
            qsT = self.sb(st, "ml_qsT", [128, 128], BF16)
            qsx = [self.sb(st, "ml_qsx%d" % i, [128, 128], BF16) for i in range(2)]
            Bqsxj = [Buf(), Buf()]
            Cbj = [self.sb(st, "ml_Cbj%d" % i, [128, 129], BF16) for i in range(2)]
            BCbj = [Buf(), Buf()]
            kwxj = [self.sb(st, "ml_kwxj%d" % i, [128, 64], BF16) for i in range(2)]
            Bkwxj = [Buf(), Buf()]
            dec = self.sb(st, "ml_dec", [128, NS], F32)
            BqsT, Bqsx, Bdec = Buf(), Buf(), Buf()
            kw = self.sb(st, "ml_kw", [128, 64], BF16)
            Bkw, Bkwx = Buf(), Buf()
            htok = self.sb(st, "ml_htok", [128, 4, 128], F32)
            stats = self.sb(st, "ml_stats", [128, 4, 8], F32)
            mv = self.sb(st, "ml_mv", [128, 4, 2], F32)
            rstd = self.sb(st, "ml_rstd", [128, 4], F32)
            dn = self.sb(st, "ml_dn", [128, 4], F32)
            Bhtok, Bstats, Bmv, Brstd, Bdn = Buf(), Buf(), Buf(), Buf(), Buf()
            sgt = htok[:, :, :].rearrange("p h v -> p (h v)")
            Bsgt = Bhtok
            mltok = self.sb(st, "ml_mltok", [128, 512], BF16)
            Bmltok = Buf()
            mlT = self.sb(st, "ml_mlT", [128, 4, 512], BF16)
            BmlT = Buf()
            Cfp = self.sb(st, "ml_Cfp", [128, 2, 129], F32)
            Cbp = self.sb(st, "ml_Cbp", [128, 2, 129], BF16)
            Cfs = self.sb(st, "ml_Cfs", [128, NS, 129], F32)
            BCfp, BCbp = [Buf(), Buf()], [Buf(), Buf()]
            BCfs = Buf()
            S.op("pool", lambda h: h.memset(Cfp[:], 0.0), writes=BCfp)
            S.op("pool", lambda h: h.memset(Cbp[:], 0.0), writes=BCbp)
            S.op("pool", lambda h: h.memset(vext[:, :, 128:129], 1.0), writes=[Bvext])
            S.op("pool", lambda h: h.memset(RW[:], 0.0), writes=[BRW])
            id4 = self.identf[0:4, 0:4]
            chunks = self.chunks()
            nprompt = len(chunks) - 1
            grp_start = 0
            for ci, (t0, L, nseq, blen, samp) in enumerate(chunks):
                nblk = L // blen
                Bxb = self.tile_bufs(self.Bxbf, t0, L)
                Gc, Gp = G[ci % 2], G[(ci + 1) % 2]
                BGc, BGp = BG[ci % 2], BG[(ci + 1) % 2]
                xk = lambda kc: self.xbf[:, kc, t0:t0 + L]
                pg = self.psbank()
                for kc in range(KD):
                    self.mm(pg[0][0:4, 0:L], wv[:, kc, 1536:1540], xk(kc), kc == 0, kc == KD - 1, Bxb + [Bwi[kc]], [pg[1]])
                for kc in range(KD):
                    self.mm(pg[0][0:4, 128:128 + L], wv[:, kc, 1540:1544], xk(kc), kc == 0, kc == KD - 1, Bxb + [Bwi[kc]], [pg[1]])
                S.op("act", lambda h, Gc=Gc, pg=pg, L=L: h.activation(out=Gc[:, IG, 0:L], in_=pg[0][0:4, 0:L], func=AF.Identity, bias=gb[:, 0:1], scale=1.0),
                     reads=[pg[1]] + Bpar, writes=[BGc])
                S.op("act", lambda h, Gc=Gc, pg=pg, L=L: h.activation(out=Gc[:, TMP, 0:L], in_=pg[0][0:4, 128:128 + L], func=AF.Exp, bias=gb[:, 1:2], scale=-1.0),
                     reads=[pg[1]] + Bpar, writes=[BGc])
                S.op("act", lambda h, Gc=Gc, L=L: h.activation(out=Gc[:, SP, 0:L], in_=Gc[:, TMP, 0:L], func=AF.Ln, bias=1.0, scale=1.0),
                     reads=[BGc], writes=[BGc])
                pq = self.psbank()
                for i4 in range(4):
                    for kc in range(KD):
                        self.mm(pq[0][:, i4 * 128:i4 * 128 + L], wv[:, kc, i4 * 128:(i4 + 1) * 128], xk(kc), kc == 0, kc == KD - 1,
                                Bxb + [Bwi[kc]], [pq[1]])
                pqv = pq[0][:, :].rearrange("p (c t) -> p c t", t=128)
                S.op("act", lambda h, pqv=pqv, L=L: h.activation(out=qT[:, :, 0:L], in_=pqv[:, 0:2, 0:L], func=AF.Copy), reads=[pq[1]], writes=[BqT])
                S.op("act", lambda h, pqv=pqv, L=L: h.mul(kT[:, :, 0:L], pqv[:, 2:4, 0:L], 0.125), reads=[pq[1]], writes=[BkT])
                pk = self.psbank()
                for kc in range(KD):
                    self.mm(pk[0][0:L, 0:256], self.xbf[:, kc, t0:t0 + L], wv[:, kc, 256:512], kc == 0, kc == KD - 1, Bxb + [Bwi[kc]], [pk[1]])
                S.op("act", lambda h, pk=pk, L=L: h.mul(ktok[0:L, :], pk[0][0:L, 0:256], 0.125), reads=[pk[1]], writes=[Bktok])
                pv_ = self.psbank()
                for kc in range(KD):
                    self.mm(pv_[0][0:L, :], self.xbf[:, kc, t0:t0 + L], wv[:, kc, 512:1024], kc == 0, kc == KD - 1, Bxb + [Bwi[kc]], [pv_[1]])
                S.op("dve", lambda h, pv_=pv_, L=L: h.tensor_copy(vext[0:L, :, 0:128], pv_[0][0:L, :].rearrange("p (h v) -> p h v", v=128)),
                     reads=[pv_[1]], writes=[Bvext])
                po = self.psbank()
                for kc in range(KD):
                    self.mm(po[0][0:L, :], self.xbf[:, kc, t0:t0 + L], wv[:, kc, 1024:1536], kc == 0, kc == KD - 1, Bxb + [Bwi[kc]], [po[1]])
                S.op("act", lambda h, po=po, L=L: h.activation(out=sgt[0:L, :], in_=po[0][0:L, :], func=AF.Exp, scale=-1.0), reads=[po[1]], writes=[Bsgt])
                S.op("act", lambda h, L=L: h.activation(out=sgt[0:L, :], in_=sgt[0:L, :], func=AF.Ln, bias=1.0, scale=1.0), reads=[Bsgt], writes=[Bsgt])
                S.op("act", lambda h, L=L: h.activation(out=gs[0:L, :], in_=sgt[0:L, :], func=AF.Exp, scale=-1.0), reads=[Bsgt], writes=[Bgs])
                S.op("dve", lambda h, L=L: h.tensor_tensor(gs[0:L, :], gs[0:L, :], gnorm[0:L, :], ALU.mult), reads=[Bgs] + Bpar, writes=[Bgs])
                if not samp:
                    ini_c = 0.0 if ci == 0 else carry[:, 0:1]
                    ini_m = 0.0 if ci == 0 else carry[:, 1:2]
                    S.op("dve", lambda h, Gc=Gc, ini_c=ini_c, L=L: h.tensor_tensor_scan(Gc[:, CSP, 0:L], onesf[0:4, 0:L], Gc[:, SP, 0:L], ini_c, ALU.mult, ALU.add),
                         reads=[BGc, Bcarry, Bmc], writes=[BGc])
                    S.op("dve", lambda h, Gc=Gc, L=L: h.tensor_tensor(Gc[:, A_, 0:L], Gc[:, IG, 0:L], Gc[:, CSP, 0:L], ALU.add), reads=[BGc], writes=[BGc])
                    S.op("dve", lambda h, Gc=Gc, ini_m=ini_m, L=L: h.tensor_tensor_scan(Gc[:, M_, 0:L], Gc[:, A_, 0:L], Gc[:, A_, 0:L], ini_m, ALU.max, ALU.max),
                         reads=[BGc, Bcarry], writes=[BGc])
                    Mprev = zc[:, 0:1] if ci == 0 else carry[:, 1:2]
                    Me = Gc[:, M_, L - 1:L]
                    cspe = Gc[:, CSP, L - 1:L]
                else:
                    v3 = lambda slot, Gc=Gc: Gc[:, slot, 0:L].rearrange("p (s t) -> p s t", t=blen)
                    S.op("dve", lambda h, v3=v3: h.tensor_copy(v3(CSP)[:, :, 0:1], v3(SP)[:, :, 0:1]), reads=[BGc], writes=[BGc])
                    for t in range(1, blen):
                        S.op("dve", lambda h, v3=v3, t=t: h.tensor_tensor(v3(CSP)[:, :, t:t + 1], v3(CSP)[:, :, t - 1:t], v3(SP)[:, :, t:t + 1], ALU.add),
                             reads=[BGc], writes=[BGc])
                    S.op("dve", lambda h, Gc=Gc, L=L: h.tensor_tensor(Gc[:, A_, 0:L], Gc[:, IG, 0:L], Gc[:, CSP, 0:L], ALU.add), reads=[BGc], writes=[BGc])
                    S.op("dve", lambda h, v3=v3: h.tensor_tensor(v3(M_)[:, :, 0:1], v3(A_)[:, :, 0:1], m0T[:, :].unsqueeze(2), ALU.max),
                         reads=[BGc] + Bpar, writes=[BGc])
                    for t in range(1, blen):
                        S.op("dve", lambda h, v3=v3, t=t: h.tensor_tensor(v3(M_)[:, :, t:t + 1], v3(M_)[:, :, t - 1:t], v3(A_)[:, :, t:t + 1], ALU.max),
                             reads=[BGc], writes=[BGc])
                    Mprev = m0T[:, 0:nblk]
                    Me = v3(M_)[:, :, blen - 1]
                    cspe = v3(CSP)[:, :, blen - 1]
                b3 = lambda ap, L=L, nblk=nblk, blen=blen: ap.unsqueeze(2).broadcast_to([4, nblk, blen])
                g3 = lambda slot, Gc=Gc, L=L, blen=blen: Gc[:, slot, 0:L].rearrange("p (s t) -> p s t", t=blen)
                rdg = [BGc, Bcarry] + Bpar
                S.op("dve", lambda h, g3=g3, b3=b3, Mprev=Mprev: h.tensor_tensor(g3(TMP), b3(Mprev), g3(M_), ALU.subtract), reads=rdg, writes=[BGc])
                S.op("act", lambda h, Gc=Gc, L=L: h.activation(out=RW[:, 0:L], in_=Gc[:, TMP, 0:L], func=AF.Exp), reads=[BGc], writes=[BRW])
                S.op("dve", lambda h, g3=g3, b3=b3, Me=Me: h.tensor_tensor(g3(TMP), g3(A_), b3(Me), ALU.subtract), reads=rdg, writes=[BGc])
                S.op("act", lambda h, Gc=Gc, L=L: h.activation(out=Gc[:, WE, 0:L], in_=Gc[:, TMP, 0:L], func=AF.Exp), reads=[BGc], writes=[BGc])
                S.op("dve", lambda h, Gc=Gc, L=L: h.tensor_tensor(Gc[:, TMP, 0:L], Gc[:, CSP, 0:L], Gc[:, M_, 0:L], ALU.subtract), reads=[BGc], writes=[BGc])
                S.op("act", lambda h, Gc=Gc, L=L: h.activation(out=Gc[:, ENM, 0:L], in_=Gc[:, TMP, 0:L], func=AF.Exp), reads=[BGc], writes=[BGc])
                S.op("dve", lambda h, Gc=Gc, L=L: h.tensor_scalar(Gc[:, NEGM, 0:L], Gc[:, M_, 0:L], -1.0, None, ALU.mult), reads=[BGc], writes=[BGc])
                S.op("dve", lambda h, Mprev=Mprev, Me=Me, nblk=nblk: h.tensor_tensor(blk[:, 0, 0:nblk], Mprev, Me, ALU.subtract), reads=rdg, writes=[Bblk])
                S.op("act", lambda h, nblk=nblk: h.activation(out=RW[:, 128:128 + nblk], in_=blk[:, 0, 0:nblk], func=AF.Exp), reads=[Bblk], writes=[BRW])
                S.op("dve", lambda h, Me=Me, cspe=cspe, nblk=nblk: h.tensor_tensor(blk[:, 1, 0:nblk], Me, cspe, ALU.subtract), reads=rdg, writes=[Bblk])
                if samp:
                    S.op("sp", lambda h: h.dma_start(out=d["s_m"][j].rearrange("s h -> h s"), in_=blk[:, 1, 0:NS], allow_slow_non_contiguous=True),
                         reads=[Bblk], dma=True)
                elif ci == nprompt - 1:
                    S.op("sp", lambda h: h.dma_start(out=d["p_m"][j].rearrange("(h o) -> h o", o=1), in_=blk[:, 1, 0:1]), reads=[Bblk], dma=True)
                S.op("dve", lambda h, Gc=Gc, L=L: h.tensor_tensor(negMD[:, :, 0:L], Gc[:, NEGM, 0:L].unsqueeze(1).broadcast_to([4, 4, L]),
                                                                 id4.unsqueeze(2).broadcast_to([4, 4, L]), ALU.mult),
                     reads=[BGc, self.Bconst], writes=[BnMD])
                S.op("dve", lambda h: h.tensor_tensor(RWD[:, :, :], RW[:, :].unsqueeze(1).broadcast_to([4, 4, 160]),
                                                      id4.unsqueeze(2).broadcast_to([4, 4, 160]), ALU.mult),
                     reads=[BRW, self.Bconst], writes=[BRWD])
                pc = self.psbank()
                for qi, slot in enumerate((A_, WE, ENM)):
                    self.mm(pc[0][0:L, qi * 4:qi * 4 + 4], Gc[:, slot, 0:L], id4, True, True, [BGc, self.Bconst], [pc[1]])
                S.op("dve", lambda h, pc=pc, L=L: h.tensor_copy(cols[0:L, :], pc[0][0:L, 0:12]), reads=[pc[1]], writes=[Bcols])
                mneg = MC["mneg_s"] if samp else MC["mneg_p"]
                for p in range(2):
                    if samp:
                        BCf, BCb = BCfs, None
                        srcC = d["st_mC"][j][:, 2 * p:2 * p + 2, :, :].rearrange("s hh d v -> (hh d) s v")
                        srcn = d["st_mn"][j][:, 2 * p:2 * p + 2, :].rearrange("s hh d -> (hh d) s")
                        for q4 in range(0, NS, 4):
                            S.op("sp", lambda h, srcC=srcC, q4=q4: h.dma_start(out=Cfs[:, q4:q4 + 4, 0:128], in_=srcC[:, q4:q4 + 4, :]), writes=[BCfs], dma=True)
                        for q4 in range(0, NS, 4):
                            S.op("sp", lambda h, srcn=srcn, q4=q4: h.dma_start(out=Cfs[:, q4:q4 + 4, 128:129], in_=srcn[:, q4:q4 + 4].unsqueeze(2), allow_slow_non_contiguous=True),
                                 reads=[BCfs], writes=[BCfs], dma=True)
                        Cfv, Cbv = Cfs[:, :, :], None
                    else:
                        BCf, BCb = BCfp[p], BCbp[p]
                        Cfv = Cfp[:, p:p + 1, :]
                        Cbv = Cbp[:, p:p + 1, :]
                    pw = self.psbank()
                    for hh in range(2):
                        self.mm(pw[0][hh * 64:(hh + 1) * 64, 0:160], onesf[0:4, 0:64], RWD[:, 2 * p + hh, :], True, True, [BRWD, Bmc], [pw[1]])
                    S.op("dve", lambda h, pw=pw, p=p, L=L: h.tensor_tensor(qsT[:, 0:L], qT[:, p, 0:L], pw[0][:, 0:L], ALU.mult), reads=[pw[1], BqT], writes=[BqsT])
                    S.op("act", lambda h, pw=pw, nseq=nseq: h.activation(out=dec[:, 0:nseq], in_=pw[0][:, 128:128 + nseq], func=AF.Copy), reads=[pw[1]], writes=[Bdec])
                    PN = [(self.pb[4], self.pbB[4]), (self.pb[5], self.pbB[5])]
                    pab = []
                    for hh in range(2):
                        hd = 2 * p + hh
                        o = hh * 64
                        pa = self.psbank()
                        self.mm(pa[0][0:L, 0:L], kT[o:o + 64, p, 0:L], qT[o:o + 64, p, 0:L], True, True, [BkT, BqT], [pa[1]])
                        pb_ = self.psbank()
                        self.mm(pb_[0][0:L, 0:L], onesf[0:4, 0:L], negMD[:, hd, 0:L], True, False, [BnMD, Bmc], [pb_[1]])
                        self.mm(pb_[0][0:L, 0:L], self.ident[0:L, 0:L], mneg[0:L, 0:L], False, True, [Bmc, self.Bconst], [pb_[1]])
                        pab.append((pa, pb_))
                    for hh in range(2):
                        hd = 2 * p + hh
                        o = hh * 64
                        e_, Be_ = E[hd % 2], BE[hd % 2]
                        pt_, Bpt_ = PT[hd % 2], BPT[hd % 2]
                        pa, pb_ = pab[hh]
                        S.op("act", lambda h, e_=e_, pb_=pb_, hd=hd, L=L: h.activation(out=e_[0:L, 0:L], in_=pb_[0][0:L, 0:L], func=AF.Exp, bias=cols[0:L, hd:hd + 1], scale=1.0),
                             reads=[pb_[1], Bcols], writes=[Be_])
                        S.op("dve", lambda h, e_=e_, pt_=pt_, pa=pa, L=L: h.tensor_tensor(pt_[0:L, 0:L], e_[0:L, 0:L], pa[0][0:L, 0:L], ALU.mult),
                             reads=[Be_, pa[1]], writes=[Bpt_])
                        self.mm(PN[hh][0][0:L, 0:129], pt_[0:L, 0:L], vext[0:L, hd, :], True, False, [Bpt_, Bvext], [PN[hh][1]])
                    for jj in range(nseq):
                        if samp:
                            qx_, Bqx_ = qsx[jj % 2], Bqsxj[jj % 2]
                            cb_, Bcb_ = Cbj[jj % 2], BCbj[jj % 2]
                            S.op("dve", lambda h, qx_=qx_, jj=jj, L=L: h.tensor_tensor(qx_[:, 0:L], qsT[:, 0:L], MC["qmask"][:, jj, 0:L], ALU.mult),
                                 reads=[BqsT, Bmc], writes=[Bqx_])
                            S.op("act", lambda h, cb_=cb_, jj=jj: h.activation(out=cb_[:, :], in_=Cfs[:, jj, :], func=AF.Copy), reads=[BCfs], writes=[Bcb_])
                        for hh in range(2):
                            o = hh * 64
                            if samp:
                                self.mm(PN[hh][0][0:L, 0:129], qx_[o:o + 64, 0:L], cb_[o:o + 64, :], False, jj == nseq - 1, [Bqx_, Bcb_], [PN[hh][1]])
                            else:
                                self.mm(PN[hh][0][0:L, 0:129], qsT[o:o + 64, 0:L], Cbv[o:o + 64, jj, :], False, jj == nseq - 1, [BqsT, BCb], [PN[hh][1]])
                    for hh in range(2):
                        hd = 2 * p + hh
                        o = hh * 64
                        pn = PN[hh]
                        S.op("act", lambda h, pn=pn, hd=hd, L=L: h.activation(out=dn[0:L, hd:hd + 1], in_=pn[0][0:L, 128:129], func=AF.Abs),
                             reads=[pn[1]], writes=[Bdn])
                        S.op("dve", lambda h, hd=hd, L=L: h.tensor_tensor(dn[0:L, hd:hd + 1], dn[0:L, hd:hd + 1], cols[0:L, 8 + hd:9 + hd], ALU.max),
                             reads=[Bdn, Bcols], writes=[Bdn])
                        S.op("dve", lambda h, hd=hd, L=L: h.reciprocal(dn[0:L, hd:hd + 1], dn[0:L, hd:hd + 1]), reads=[Bdn], writes=[Bdn])
                        S.op("dve", lambda h, pn=pn, hd=hd, L=L: h.tensor_scalar(htok[0:L, hd, :], pn[0][0:L, 0:128], dn[0:L, hd:hd + 1], None, ALU.mult),
                             reads=[pn[1], Bdn], writes=[Bhtok])
                        S.op("dve", lambda h, hd=hd, L=L: h.bn_stats(stats[0:L, hd, 0:6], htok[0:L, hd, :]), reads=[Bhtok], writes=[Bstats])
                        S.op("dve", lambda h, hd=hd, L=L: h.bn_aggr(mv[0:L, hd, :], stats[0:L, hd, 0:6]), reads=[Bstats], writes=[Bmv])
                        S.op("dve", lambda h, hd=hd, L=L: h.tensor_scalar(kw[0:L, :], ktok[0:L, hd * 64:(hd + 1) * 64], cols[0:L, 4 + hd:5 + hd], None, ALU.mult),
                             reads=[Bktok, Bcols], writes=[Bkw])
                        for r0 in range(0, nseq, 4):
                            nr = min(4, nseq - r0)
                            pu = self.psbank()
                            for jj in range(r0, r0 + nr):
                                if samp:
                                    kx_, Bkx_ = kwxj[jj % 2], Bkwxj[jj % 2]
                                    S.op("dve", lambda h, kx_=kx_, jj=jj, L=L: h.tensor_scalar(kx_[0:L, :], kw[0:L, :], MC["tokmask"][0:L, jj:jj + 1], None, ALU.mult),
                                         reads=[Bkw, Bmc], writes=[Bkx_])
                                    self.mm(pu[0][o:o + 64, (jj - r0) * 128:(jj - r0 + 1) * 128], kx_[0:L, :], vext[0:L, hd, 0:128], True, True, [Bkx_, Bvext], [pu[1]])
                                else:
                                    self.mm(pu[0][o:o + 64, (jj - r0) * 128:(jj - r0 + 1) * 128], kw[0:L, :], vext[0:L, hd, 0:128], True, True, [Bkw, Bvext], [pu[1]])
                            cf = Cfv[o:o + 64, r0:r0 + nr, 0:128]
                            S.op("dve", lambda h, cf=cf, r0=r0, nr=nr, o=o: h.tensor_tensor(cf, cf, dec[o:o + 64, r0:r0 + nr].unsqueeze(2).broadcast_to([64, nr, 128]), ALU.mult),
                                 reads=[BCf, Bdec], writes=[BCf])
                            S.op("dve", lambda h, cf=cf, pu=pu, nr=nr, o=o: h.tensor_tensor(cf, cf, pu[0][o:o + 64, 0:nr * 128].rearrange("p (j v) -> p j v", v=128), ALU.add),
                                 reads=[BCf, pu[1]], writes=[BCf])
                        pn2 = self.psbank()
                        rhs_n = MC["tokmask"][0:L, 0:nseq] if samp else self.ones_bf[0:L, 0:1]
                        self.mm(pn2[0][o:o + 64, 0:nseq], kw[0:L, :], rhs_n, True, True, [Bkw, Bmc, self.Bconst], [pn2[1]])
                        cn = Cfv[o:o + 64, 0:nseq, 128:129]
                        S.op("dve", lambda h, cn=cn, nseq=nseq, o=o: h.tensor_tensor(cn, cn, dec[o:o + 64, 0:nseq].unsqueeze(2), ALU.mult), reads=[BCf, Bdec], writes=[BCf])
                        S.op("dve", lambda h, cn=cn, pn2=pn2, nseq=nseq, o=o: h.tensor_tensor(cn, cn, pn2[0][o:o + 64, 0:nseq].unsqueeze(2), ALU.add),
                             reads=[BCf, pn2[1]], writes=[BCf])
                    if not samp:
                        S.op("act", lambda h, Cfv=Cfv, Cbv=Cbv: h.activation(out=Cbv, in_=Cfv, func=AF.Copy), reads=[BCf], writes=[BCb])
                    if samp:
                        dstC = d["s_C"][j][:, 2 * p:2 * p + 2, :, :].rearrange("s hh d v -> (hh d) s v")
                        dstn = d["s_n"][j][:, 2 * p:2 * p + 2, :].rearrange("s hh d -> (hh d) s")
                        for q4 in range(0, NS, 4):
                            S.op("sp", lambda h, dstC=dstC, q4=q4: h.dma_start(out=dstC[:, q4:q4 + 4, :], in_=Cfs[:, q4:q4 + 4, 0:128]), reads=[BCfs], dma=True)
                        for q4 in range(0, NS, 4):
                            S.op("sp", lambda h, dstn=dstn, q4=q4: h.dma_start(out=dstn[:, q4:q4 + 4].unsqueeze(2), in_=Cfs[:, q4:q4 + 4, 128:129], allow_slow_non_contiguous=True),
                                 reads=[BCfs], dma=True)
                    elif ci == nprompt - 1:
                        dstC = d["p_C"][j][2 * p:2 * p + 2, :, :].rearrange("hh d v -> (hh d) v")
                        dstn = d["p_n"][j][2 * p:2 * p + 2, :].rearrange("hh (d o) -> (hh d) o", o=1)
                        S.op("sp", lambda h, dstC=dstC, p=p: h.dma_start(out=dstC, in_=Cfp[:, p, 0:128]), reads=[BCfp[p]], dma=True)
                        S.op("sp", lambda h, dstn=dstn, p=p: h.dma_start(out=dstn, in_=Cfp[:, p, 128:129]), reads=[BCfp[p]], dma=True)
                S.op("dve", lambda h, L=L: h.tensor_scalar(rstd[0:L, :], mv[0:L, :, 1], LN_EPS, None, ALU.add), reads=[Bmv], writes=[Brstd])
                S.op("act", lambda h, L=L: h.activation(out=rstd[0:L, :], in_=rstd[0:L, :], func=AF.Ln), reads=[Brstd], writes=[Brstd])
                S.op("act", lambda h, L=L: h.activation(out=rstd[0:L, :], in_=rstd[0:L, :], func=AF.Exp, scale=-0.5), reads=[Brstd], writes=[Brstd])
                for hd in range(4):
                    S.op("dve", lambda h, hd=hd, L=L: h.tensor_scalar(htok[0:L, hd, :], htok[0:L, hd, :], mv[0:L, hd, 0:1], rstd[0:L, hd:hd + 1], ALU.subtract, ALU.mult),
                         reads=[Bhtok, Bmv, Brstd], writes=[Bhtok])
                S.op("dve", lambda h, L=L: h.tensor_tensor(mltok[0:L, :], htok[0:L, :, :].rearrange("p h v -> p (h v)"), gs[0:L, :], ALU.mult),
                     reads=[Bhtok, Bgs], writes=[Bmltok])
                S.op("dve", lambda h, Gc=Gc, L=L: h.tensor_copy(carry[:, 0:1], Gc[:, CSP, L - 1:L]), reads=[BGc], writes=[Bcarry])
                S.op("dve", lambda h, Gc=Gc, L=L: h.tensor_copy(carry[:, 1:2], Gc[:, M_, L - 1:L]), reads=[BGc], writes=[Bcarry])
                gs0 = (t0 // 512) * 512 if t0 < cfg.seq else t0
                toff = t0 - gs0
                for hd in range(4):
                    self.tr(self.pbf[:, hd * 128:hd * 128 + L], mltok[0:L, hd * 128:(hd + 1) * 128], self.ident[0:L, 0:L], [Bmltok, self.Bconst], [self.BpbfB])
                S.op("act", lambda h, L=L, toff=toff: h.activation(out=mlT[:, :, toff:toff + L], in_=self.pbf[:, 0:512].rearrange("p (h t) -> p h t", t=128)[:, :, 0:L], func=AF.Copy),
                     reads=[self.BpbfB], writes=[BmlT])
                gend = t0 + L
                if t0 >= cfg.seq or gend % 512 == 0 or gend == cfg.seq:
                    self.out_proj(wo, Bwo, 4, mlT, BmlT, gs0, gend - gs0, first)
        S.barrier()

    def gla_phase(self, l, j, first):
        cfg, S, d = self.cfg, self.S, self.dram
        self.rot = [0, 1, 2, 3, 6]
        self.rr = 0
        NS = cfg.nseq
        with ExitStack() as st:
            MC = self.mixer_consts(st)
            Bmc = MC["B"]
            onesf = MC["onesf"]
            w_in = d["ab_w_in"][j]
            w_out = d["ab_w_out"][j]
            (wv, wo), (Bwi, Bwo) = self.take_w()
            wa2 = self.sb(st, "gl_wa2", [16, 256], BF16)
            nba = self.sb(st, "gl_nba", [128, 2], F32)
            gnorm = self.sb(st, "gl_gnorm", [128, 512], BF16)
            Bq1, Bq2, Bq3 = Buf(), Buf(), Buf()
            S.op("pool", lambda h: h.dma_start(out=wa2[:], in_=d["ab_gla_wa2"][j]), writes=[Bq1], dma=True)
            S.op("sp", lambda h: h.dma_start(out=nba[:], in_=d["ab_gla_ba"][j].rearrange("(c p) -> p c", p=128), allow_slow_non_contiguous=True),
                 writes=[Bq2], dma=True)
            S.op("dve", lambda h: h.tensor_scalar(nba[:], nba[:], -1.0, None, ALU.mult), reads=[Bq2], writes=[Bq2])
            S.op("pool", lambda h: h.dma_start(out=gnorm[:], in_=d["ab_gla_norm"][j].partition_broadcast(128)), writes=[Bq3], dma=True)
            Bpar = [Bq1, Bq2, Bq3]
            gaT = self.sb(st, "gl_gaT", [16, 128], BF16)
            BgaT = Buf()
            spT = self.sb(st, "gl_spT", [128, 2, 128], F32)
            spc = self.sb(st, "gl_spc", [128, 2, 128], F32)
            eA = self.sb(st, "gl_eA", [128, 2, 128], F32)
            enA = self.sb(st, "gl_enA", [128, 2, 128], F32)
            BspT, Bspc, BeA, BenA = Buf(), Buf(), Buf(), Buf()
            qT = self.sb(st, "gl_qT", [128, 2, 128], BF16)
            kT = self.sb(st, "gl_kT", [128, 2, 128], BF16)
            BqT, BkT = Buf(), Buf()
            ktl = self.sb(st, "gl_ktl", [128, 128], BF16)
            kxj = [self.sb(st, "gl_kxj%d" % i, [128, 128], BF16) for i in range(2)]
            qxj = [self.sb(st, "gl_qxj%d" % i, [128, 128], BF16) for i in range(2)]
            Sbj = [self.sb(st, "gl_Sbj%d" % i, [128, 128], BF16) for i in range(2)]
            Bkxj, Bqxj, BSbj = [Buf(), Buf()], [Buf(), Buf()], [Buf(), Buf()]
            Bktl = Buf()
            vtok = self.sb(st, "gl_vtok", [128, 4, 128], BF16)
            gs = self.sb(st, "gl_gs", [128, 512], BF16)
            Bvtok, Bgs = Buf(), Buf()
            PT = [self.sb(st, "gl_PT%d" % i, [128, 128], BF16) for i in range(2)]
            BPT = [Buf(), Buf()]
            otok = self.sb(st, "gl_otok", [128, 4, 128], F32)
            stats = self.sb(st, "gl_stats", [128, 4, 8], F32)
            mv = self.sb(st, "gl_mv", [128, 4, 2], F32)
            rstd = self.sb(st, "gl_rstd", [128, 4], F32)
            Botok, Bstats, Bmv, Brstd = Buf(), Buf(), Buf(), Buf()
            gltok = self.sb(st, "gl_gltok", [128, 512], BF16)
            Bgltok = Buf()
            glT = self.sb(st, "gl_glT", [128, 4, 512], BF16)
            BglT = Buf()
            Sfp = self.sb(st, "gl_Sfp", [128, 2, 128], F32)
            Sbp = self.sb(st, "gl_Sbp", [128, 2, 128], BF16)
            Sfs = self.sb(st, "gl_Sfs", [128, NS, 128], F32)
            BSfp, BSbp = [Buf(), Buf()], [Buf(), Buf()]
            BSfs = Buf()
            S.op("pool", lambda h: h.memset(Sfp[:], 0.0), writes=BSfp)
            S.op("pool", lambda h: h.memset(Sbp[:], 0.0), writes=BSbp)
            chunks = self.chunks()
            nprompt = len(chunks) - 1
            grp_start = 0
            for ci, (t0, L, nseq, blen, samp) in enumerate(chunks):
                nblk = L // blen
                Bxb = self.tile_bufs(self.Bxbf, t0, L)
                xk = lambda kc: self.xbf[:, kc, t0:t0 + L]
                pg = self.psbank()
                for kc in range(KD):
                    self.mm(pg[0][0:16, 0:L], wv[:, kc, 1536:1552], xk(kc), kc == 0, kc == KD - 1, Bxb + [Bwi[kc]], [pg[1]])
                S.op("act", lambda h, pg=pg, L=L: h.activation(out=gaT[:, 0:L], in_=pg[0][0:16, 0:L], func=AF.Copy), reads=[pg[1]], writes=[BgaT])
                pz = self.psbank()
                for p in range(2):
                    self.mm(pz[0][:, p * 128:p * 128 + L], wa2[:, p * 128:(p + 1) * 128], gaT[:, 0:L], True, True, [BgaT] + Bpar, [pz[1]])
                for p in range(2):
                    S.op("act", lambda h, pz=pz, p=p, L=L: h.activation(out=spT[:, p, 0:L], in_=pz[0][:, p * 128:p * 128 + L], func=AF.Exp, bias=nba[:, p:p + 1], scale=-1.0),
                         reads=[pz[1]] + Bpar, writes=[BspT])
                S.op("act", lambda h, L=L: h.activation(out=spT[:, :, 0:L], in_=spT[:, :, 0:L], func=AF.Ln, bias=1.0, scale=1.0), reads=[BspT], writes=[BspT])
                for p in range(2):
                    d0 = MC["rst_s"][:, 0:L] if samp else onesf[:, 0:L]
                    S.op("dve", lambda h, p=p, d0=d0, L=L: h.tensor_tensor_scan(spc[:, p, 0:L], d0, spT[:, p, 0:L], 0.0, ALU.mult, ALU.add),
                         reads=[BspT, Bmc], writes=[Bspc])
                S.op("act", lambda h, L=L: h.activation(out=eA[:, :, 0:L], in_=spc[:, :, 0:L], func=AF.Exp, scale=-1.0 / 16.0), reads=[Bspc], writes=[BeA])
                S.op("act", lambda h, L=L: h.activation(out=enA[:, :, 0:L], in_=spc[:, :, 0:L], func=AF.Exp, scale=1.0 / 16.0), reads=[Bspc], writes=[BenA])
                pq = self.psbank()
                for i4 in range(4):
                    for kc in range(KD):
                        self.mm(pq[0][:, i4 * 128:i4 * 128 + L], wv[:, kc, i4 * 128:(i4 + 1) * 128], xk(kc), kc == 0, kc == KD - 1,
                                Bxb + [Bwi[kc]], [pq[1]])
                pqv = pq[0][:, :].rearrange("p (c t) -> p c t", t=128)
                S.op("dve", lambda h, pqv=pqv, L=L: h.scalar_tensor_tensor(qT[:, :, 0:L], pqv[:, 0:2, 0:L], 0.125, eA[:, :, 0:L], ALU.mult, ALU.mult),
                     reads=[pq[1], BeA], writes=[BqT])
                S.op("dve", lambda h, pqv=pqv, L=L: h.tensor_tensor(kT[:, :, 0:L], pqv[:, 2:4, 0:L], enA[:, :, 0:L], ALU.mult),
                     reads=[pq[1], BenA], writes=[BkT])
                pv_ = self.psbank()
                for kc in range(KD):
                    self.mm(pv_[0][0:L, :], self.xbf[:, kc, t0:t0 + L], wv[:, kc, 512:1024], kc == 0, kc == KD - 1, Bxb + [Bwi[kc]], [pv_[1]])
                S.op("act", lambda h, pv_=pv_, L=L: h.activation(out=vtok[0:L, :, :], in_=pv_[0][0:L, :].rearrange("p (h v) -> p h v", v=128), func=AF.Copy),
                     reads=[pv_[1]], writes=[Bvtok])
                po = self.psbank()
                for kc in range(KD):
                    self.mm(po[0][0:L, :], self.xbf[:, kc, t0:t0 + L], wv[:, kc, 1024:1536], kc == 0, kc == KD - 1, Bxb + [Bwi[kc]], [po[1]])
                S.op("act", lambda h, po=po, L=L: h.activation(out=gs[0:L, :], in_=po[0][0:L, :], func=AF.Silu), reads=[po[1]], writes=[Bgs])
                S.op("dve", lambda h, L=L: h.tensor_tensor(gs[0:L, :], gs[0:L, :], gnorm[0:L, :], ALU.mult), reads=[Bgs] + Bpar, writes=[Bgs])
                m01 = MC["m01_s"] if samp else MC["m01_p"]
                for p in range(2):
                    if samp:
                        BSf, BSb = BSfs, None
                        srcS = d["st_gS"][j][:, 2 * p:2 * p + 2, :, :].rearrange("s hh d v -> (hh d) s v")
                        for q4 in range(0, NS, 4):
                            S.op("sp", lambda h, srcS=srcS, q4=q4: h.dma_start(out=Sfs[:, q4:q4 + 4, :], in_=srcS[:, q4:q4 + 4, :]), writes=[BSfs], dma=True)
                        Sfv, Sbv = Sfs[:, :, :], None
                    else:
                        BSf, BSb = BSfp[p], BSbp[p]
                        Sfv = Sfp[:, p:p + 1, :]
                        Sbv = Sbp[:, p:p + 1, :]
                    self.tr(self.pbf[0:L, 0:128], kT[:, p, 0:L], self.ident[:, :], [BkT, self.Bconst], [self.BpbfB])
                    S.op("act", lambda h, L=L: h.activation(out=ktl[0:L, :], in_=self.pbf[0:L, 0:128], func=AF.Copy), reads=[self.BpbfB], writes=[Bktl])
                    eAL = eA[:, p, 0:L].rearrange("p (s t) -> p s t", t=blen)[:, :, blen - 1]
                    PN = [(self.pb[4], self.pbB[4]), (self.pb[5], self.pbB[5])]
                    pas = []
                    for hh in range(2):
                        o = hh * 64
                        pa = self.psbank()
                        self.mm(pa[0][0:L, 0:L], kT[o:o + 64, p, 0:L], qT[o:o + 64, p, 0:L], True, True, [BkT, BqT], [pa[1]])
                        pas.append(pa)
                    for hh in range(2):
                        hd = 2 * p + hh
                        o = hh * 64
                        pt_, Bpt_ = PT[hd % 2], BPT[hd % 2]
                        pa = pas[hh]
                        S.op("dve", lambda h, pt_=pt_, pa=pa, L=L, m01=m01: h.tensor_tensor(pt_[0:L, 0:L], pa[0][0:L, 0:L], m01[0:L, 0:L], ALU.mult),
                             reads=[pa[1], Bmc], writes=[Bpt_])
                        self.mm(PN[hh][0][0:L, 0:128], pt_[0:L, 0:L], vtok[0:L, hd, :], True, False, [Bpt_, Bvtok], [PN[hh][1]])
                    for jj in range(nseq):
                        if samp:
                            qx_, Bqx_ = qxj[jj % 2], Bqxj[jj % 2]
                            sb_, Bsb_ = Sbj[jj % 2], BSbj[jj % 2]
                            S.op("dve", lambda h, qx_=qx_, jj=jj, p=p, L=L: h.tensor_tensor(qx_[:, 0:L], qT[:, p, 0:L], MC["qmask"][:, jj, 0:L], ALU.mult),
                                 reads=[BqT, Bmc], writes=[Bqx_])
                            S.op("act", lambda h, sb_=sb_, jj=jj: h.activation(out=sb_[:, :], in_=Sfs[:, jj, :], func=AF.Copy), reads=[BSfs], writes=[Bsb_])
                        for hh in range(2):
                            o = hh * 64
                            if samp:
                                self.mm(PN[hh][0][0:L, 0:128], qx_[o:o + 64, 0:L], sb_[o:o + 64, :], False, jj == nseq - 1, [Bqx_, Bsb_], [PN[hh][1]])
                            else:
                                self.mm(PN[hh][0][0:L, 0:128], qT[o:o + 64, p, 0:L], Sbv[o:o + 64, jj, :], False, jj == nseq - 1, [BqT, BSb], [PN[hh][1]])
                    for hh in range(2):
                        hd = 2 * p + hh
                        o = hh * 64
                        pn = PN[hh]
                        S.op("act", lambda h, pn=pn, hd=hd, L=L: h.activation(out=otok[0:L, hd, :], in_=pn[0][0:L, 0:128], func=AF.Copy), reads=[pn[1]], writes=[Botok])
                        S.op("dve", lambda h, hd=hd, L=L: h.bn_stats(stats[0:L, hd, 0:6], otok[0:L, hd, :]), reads=[Botok], writes=[Bstats])
                        S.op("dve", lambda h, hd=hd, L=L: h.bn_aggr(mv[0:L, hd, :], stats[0:L, hd, 0:6]), reads=[Bstats], writes=[Bmv])
                        for r0 in range(0, nseq, 4):
                            nr = min(4, nseq - r0)
                            pu = self.psbank()
                            for jj in range(r0, r0 + nr):
                                if samp:
                                    kx_, Bkx_ = kxj[jj % 2], Bkxj[jj % 2]
                                    S.op("dve", lambda h, kx_=kx_, jj=jj, o=o, L=L: h.tensor_scalar(kx_[0:L, 0:64], ktl[0:L, o:o + 64], MC["tokmask"][0:L, jj:jj + 1], None, ALU.mult),
                                         reads=[Bktl, Bmc], writes=[Bkx_])
                                    self.mm(pu[0][o:o + 64, (jj - r0) * 128:(jj - r0 + 1) * 128], kx_[0:L, 0:64], vtok[0:L, hd, :], True, True, [Bkx_, Bvtok], [pu[1]])
                                else:
                                    self.mm(pu[0][o:o + 64, (jj - r0) * 128:(jj - r0 + 1) * 128], ktl[0:L, o:o + 64], vtok[0:L, hd, :], True, True, [Bktl, Bvtok], [pu[1]])
                            sf = Sfv[o:o + 64, r0:r0 + nr, :]
                            S.op("dve", lambda h, sf=sf, pu=pu, nr=nr, o=o: h.tensor_tensor(sf, sf, pu[0][o:o + 64, 0:nr * 128].rearrange("p (j v) -> p j v", v=128), ALU.add),
                                 reads=[BSf, pu[1]], writes=[BSf])
                            S.op("dve", lambda h, sf=sf, eAL=eAL, r0=r0, nr=nr, o=o: h.tensor_tensor(sf, sf, eAL[o:o + 64, r0:r0 + nr].unsqueeze(2).broadcast_to([64, nr, 128]), ALU.mult),
                                 reads=[BSf, BeA], writes=[BSf])
                    if not samp:
                        S.op("act", lambda h, Sfv=Sfv, Sbv=Sbv: h.activation(out=Sbv, in_=Sfv, func=AF.Copy), reads=[BSf], writes=[BSb])
                    if samp:
                        dstS = d["s_S"][j][:, 2 * p:2 * p + 2, :, :].rearrange("s hh d v -> (hh d) s v")
                        for q4 in range(0, NS, 4):
                            S.op("sp", lambda h, dstS=dstS, q4=q4: h.dma_start(out=dstS[:, q4:q4 + 4, :], in_=Sfs[:, q4:q4 + 4, :]), reads=[BSfs], dma=True)
                    elif ci == nprompt - 1:
                        dstS = d["p_S"][j][2 * p:2 * p + 2, :, :].rearrange("hh d v -> (hh d) v")
                        S.op("sp", lambda h, dstS=dstS, p=p: h.dma_start(out=dstS, in_=Sfp[:, p, :]), reads=[BSfp[p]], dma=True)
                S.op("dve", lambda h, L=L: h.tensor_scalar(rstd[0:L, :], mv[0:L, :, 1], LN_EPS, None, ALU.add), reads=[Bmv], writes=[Brstd])
                S.op("act", lambda h, L=L: h.activation(out=rstd[0:L, :], in_=rstd[0:L, :], func=AF.Ln), reads=[Brstd], writes=[Brstd])
                S.op("act", lambda h, L=L: h.activation(out=rstd[0:L, :], in_=rstd[0:L, :], func=AF.Exp, scale=-0.5), reads=[Brstd], writes=[Brstd])
                for hd in range(4):
                    S.op("dve", lambda h, hd=hd, L=L: h.tensor_scalar(otok[0:L, hd, :], otok[0:L, hd, :], mv[0:L, hd, 0:1], rstd[0:L, hd:hd + 1], ALU.subtract, ALU.mult),
                         reads=[Botok, Bmv, Brstd], writes=[Botok])
                S.op("dve", lambda h, L=L: h.tensor_tensor(gltok[0:L, :], otok[0:L, :, :].rearrange("p h v -> p (h v)"), gs[0:L, :], ALU.mult),
                     reads=[Botok, Bgs], writes=[Bgltok])
                gs0 = (t0 // 512) * 512 if t0 < cfg.seq else t0
                toff = t0 - gs0
                for hd in range(4):
                    self.tr(self.pbf[:, hd * 128:hd * 128 + L], gltok[0:L, hd * 128:(hd + 1) * 128], self.ident[0:L, 0:L], [Bgltok, self.Bconst], [self.BpbfB])
                S.op("act", lambda h, L=L, toff=toff: h.activation(out=glT[:, :, toff:toff + L], in_=self.pbf[:, 0:512].rearrange("p (h t) -> p h t", t=128)[:, :, 0:L], func=AF.Copy),
                     reads=[self.BpbfB], writes=[BglT])
                gend = t0 + L
                if getattr(cfg, "dbg_stop", None) == "gla_c0" and ci == 0:
                    S.cut = S.count
                if t0 >= cfg.seq or gend % 512 == 0 or gend == cfg.seq:
                    self.out_proj(wo, Bwo, 4, glT, BglT, gs0, gend - gs0, first)
        S.barrier()

    def ssd_layer(self, l):
        cfg, S, d = self.cfg, self.S, self.dram
        j = l // 2
        with ExitStack() as st:
            cw = self.sb(st, "sd_cw", [128, 24, 4], F32)
            cb = self.sb(st, "sd_cb", [128, 24], F32)
            Bcw = [Buf() for _ in range(5)]
            cwraw = self.sb(st, "sd_cwraw", [128, 128], F32)
            Braw = [Buf(), Buf(), Buf()]
            S.op("pool", lambda h: h.memset(cwraw[:], 0.0), writes=[Braw[2]])
            S.op("sp", lambda h: h.dma_start(out=cwraw[0:96, :], in_=d["ssd_conv_w"][j].rearrange("w (c p) -> (w c) p", p=128)), reads=[Braw[2]], writes=[Braw[0]], dma=True)
            S.op("sp", lambda h: h.dma_start(out=cwraw[96:120, :], in_=d["ssd_conv_b"][j].rearrange("(c p) -> c p", p=128)), reads=[Braw[2]], writes=[Braw[1]], dma=True)
            pcw, Bpcw = self.psbank()
            self.mm(pcw[:, 0:128], cwraw[:, :], self.identf[:, :], True, True, Braw + [self.Bconst], [Bpcw])
            S.op("dve", lambda h, pcw=pcw: h.tensor_copy(cw[:, :, :], pcw[:, 0:96].rearrange("p (w c) -> p c w", w=4)), reads=[Bpcw], writes=[Bcw[0]])
            S.op("dve", lambda h, pcw=pcw: h.tensor_copy(cb[:, :], pcw[:, 96:120]), reads=[Bpcw], writes=[Bcw[4]])
            groups = getattr(cfg, "ssd_groups", (0, 1, 2, 3))
            for gi, g in enumerate(groups):
                self.ssd_group(l, j, g, gi == 0, cw, cb, Bcw)
            sc = self.ln_scratch(st)
            for (t0, n) in cfg.groups:
                self.layer_norm("mix", l, t0, n, sc)
        S.barrier()

    def ssd_group(self, l, j, g, first, cw, cb, Bcw):
        cfg, S, d = self.cfg, self.S, self.dram
        NS = cfg.nseq
        with ExitStack() as st:
            MC = self.mixer_consts(st, small=True)
            Bmc = MC["B"]
            onesf = MC["onesf"]
            w_in = d["ssd_w_in"][j]
            w_out = d["ssd_w_out"][j]
            (wz, wx, wB, wC, wdt, wo), (Bz, Bwx, BwB, BwC, Bwdt, Bwo) = self.take_w()
            CH = [g * 512 + cc * 128 for cc in range(4)] + [2048 + g * 128, 2560 + g * 128]
            CHI = [c // 128 for c in CH]
            COLS = [(g * 512, 0, 512), (2048 + g * 128, 512, 128), (2560 + g * 128, 640, 128)]
            par8 = self.sb(st, "sd_par8", [8, 2], F32)
            Dbc = self.sb(st, "sd_Dbc", [128, 8], F32)
            normg = self.sb(st, "sd_normg", [128, 512], BF16)
            Bp = [Buf() for _ in range(4)]
            S.op("sp", lambda h: h.dma_start(out=par8[:, 0:1], in_=d["ssd_dt_bias"][j][g * 8:(g + 1) * 8].rearrange("(h o) -> h o", o=1)), writes=[Bp[0]], dma=True)
            S.op("sp", lambda h: h.dma_start(out=par8[:, 1:2], in_=d["ssd_a_log"][j][g * 8:(g + 1) * 8].rearrange("(h o) -> h o", o=1)), writes=[Bp[1]], dma=True)
            S.op("act", lambda h: h.activation(out=par8[:, 1:2], in_=par8[:, 1:2], func=AF.Exp), reads=[Bp[1]], writes=[Bp[1]])
            S.op("dve", lambda h: h.tensor_scalar(par8[:, 1:2], par8[:, 1:2], -1.0, None, ALU.mult), reads=[Bp[1]], writes=[Bp[1]])
            S.op("sp", lambda h: h.dma_start(out=Dbc[:], in_=d["ssd_d"][j][g * 8:(g + 1) * 8].partition_broadcast(128)), writes=[Bp[2]], dma=True)
            S.op("pool", lambda h: h.dma_start(out=normg[:], in_=d["ssd_norm"][j][g * 512:(g + 1) * 512].partition_broadcast(128)), writes=[Bp[3]], dma=True)
            id8 = self.identf[0:8, 0:8]
            DT, CS, NCS, ECS, WL, TMP = range(6)
            G8 = self.sb(st, "sd_G8", [8, 6, 128], F32)
            BG8 = Buf()
            edL = self.sb(st, "sd_edL", [8, 32], F32)
            edLD = self.sb(st, "sd_edLD", [8, 8, 16], F32)
            csD = self.sb(st, "sd_csD", [8, 4, 128], F32)
            BedL, BedLD, BcsD = Buf(), Buf(), Buf()
            cols2 = [self.sb(st, "sd_cols%d" % i, [128, 32], F32) for i in range(2)]
            Bcols2 = [Buf(), Buf()]
            decb = self.sb(st, "sd_decb", [128, 8, 16], F32)
            Bdecb = Buf()
            ext = self.sb(st, "sd_ext", [128, 6, 232], F32)
            Bext = Buf()
            nct = self.sb(st, "sd_nct", [128, 6, 48], F32)
            Bnct = Buf()
            acc = self.sb(st, "sd_acc", [128, 1, 128], F32)
            Bacc1 = Buf()
            Bacc = [Bacc1, Bacc1]
            xc = self.sb(st, "sd_xc", [128, 6, 128], BF16)
            Bxc = Buf()
            cvt = self.sb(st, "sd_cvt", [48, 768], F32)
            Bcvt = Buf()
            zs2 = [self.sb(st, "sd_zs%d" % i, [128, 512], BF16) for i in range(2)]
            Bzs2 = [Buf(), Buf()]
            xD2 = [self.sb(st, "sd_xD%d" % i, [128, 512], BF16) for i in range(2)]
            BxD2 = [Buf(), Buf()]
            xtok = self.sb(st, "sd_xtok", [128, 640], BF16)
            xdt = self.sb(st, "sd_xdt", [128, 512], BF16)
            xw = self.sb(st, "sd_xw", [128, 512], BF16)
            Bxtok, Bxdt, Bxw = Buf(), Buf(), Buf()
            CBT = self.sb(st, "sd_CBT", [128, 128], BF16)
            BCBT = Buf()
            E = [self.sb(st, "sd_E%d" % i, [128, 128], BF16) for i in range(2)]
            PT = [self.sb(st, "sd_PT%d" % i, [128, 128], BF16) for i in range(2)]
            BE, BPT = [Buf(), Buf()], [Buf(), Buf()]
            Cxj = self.sb(st, "sd_Cxj", [128, 128], BF16)
            Bxj = self.sb(st, "sd_Bxj", [128, 128], BF16)
            BCxj, BBxj = Buf(), Buf()
            ytok = self.sb(st, "sd_ytok", [128, 512], F32)
            yn = self.sb(st, "sd_yn", [128, 512], BF16)
            ss = self.sb(st, "sd_ss", [128, 1], F32)
            Bytok, Byn, Bss = Buf(), Buf(), Buf()
            ynT = self.wbuf[self.cur_slot][:, 14400:14400 + 2048].rearrange("p (h t) -> p h t", t=512)
            BynT = Buf()
            hnat = ytok[:, :].rearrange("p (r n) -> p r n", n=128)
            Bhnat = Bytok
            hTf = self.sb(st, "sd_hTf", [128, 512], F32)
            hTb = self.sb(st, "sd_hTb", [128, 512], BF16)
            BhTf, BhTb = Buf(), Buf()
            PY, BPY = self.pb[4], self.pbB[4]
            PI, BPI = self.pb[5], self.pbB[5]
            S.op("pool", lambda h: h.memset(hTf[:], 0.0), writes=[BhTf])
            S.op("pool", lambda h: h.memset(hTb[:], 0.0), writes=[BhTb])
            S.op("pool", lambda h: h.memset(ext[:], 0.0), writes=[Bext])
            S.op("pool", lambda h: h.memset(Cxj[:], 0.0), writes=[BCxj])
            chunks = self.chunks()
            nprompt = len(chunks) - 1
            pending = None
            for ci, (t0, L, nseq, blen, samp) in enumerate(chunks):
                nblk = L // blen
                W = 3 + blen
                Bxb = self.tile_bufs(self.Bxbf, t0, L)
                xk = lambda kc: self.xbf[:, kc, t0:t0 + L]
                extv = ext[:, :, 0:nblk * W].rearrange("p c (b w) -> p c b w", w=W)
                g8 = lambda slot, blen=blen, L=L: G8[:, slot, 0:L].rearrange("p (b t) -> p b t", t=blen)
                cols, Bcols = cols2[ci % 2], Bcols2[ci % 2]
                zs, Bzs = zs2[ci % 2], Bzs2[ci % 2]
                xD, BxD = xD2[ci % 2], BxD2[ci % 2]
                if samp:
                    S.op("pool", lambda h: h.memset(ext[:], 0.0), writes=[Bext])
                    for (c0, lc, n) in COLS:
                        S.op("sp", lambda h, c0=c0, lc=lc, n=n: h.dma_start(out=cvt[:, lc:lc + n], in_=d["st_cv"][j].rearrange("s w c -> (s w) c")[:, c0:c0 + n]),
                             writes=[Bcvt], dma=True)
                    pcv = self.psbank()
                    for cc in range(6):
                        self.mm(pcv[0][:, cc * 48:(cc + 1) * 48], cvt[0:48, cc * 128:(cc + 1) * 128], self.identf[0:48, 0:48], True, True, [Bcvt, self.Bconst], [pcv[1]])
                    S.op("dve", lambda h, pcv=pcv, extv=extv: h.tensor_copy(extv[:, :, 0:16, 0:3], pcv[0][:, 0:288].rearrange("p (c s w) -> p c s w", s=16, w=3)),
                         reads=[pcv[1]], writes=[Bext])
                elif ci > 0:
                    S.op("dve", lambda h, extv=extv, blen=blen: h.tensor_copy(extv[:, :, 0, 0:3], extv[:, :, 0, blen:blen + 3]), reads=[Bext], writes=[Bext])
                pd = self.psbank()
                for kc in range(KD):
                    self.mm(pd[0][0:8, 0:L], wdt[:, kc, :], xk(kc), kc == 0, kc == KD - 1, Bxb + [Bwdt[kc]], [pd[1]])
                px = self.psbank()
                for cc in range(4):
                    for kc in range(KD):
                        self.mm(px[0][:, cc * 128:cc * 128 + L], wx[:, kc, cc * 128:(cc + 1) * 128], xk(kc), kc == 0, kc == KD - 1, Bxb + [Bwx[kc]], [px[1]])
                pbc = self.psbank()
                for kc in range(KD):
                    self.mm(pbc[0][:, 0:L], wB[:, kc, :], xk(kc), kc == 0, kc == KD - 1, Bxb + [BwB[kc]], [pbc[1]])
                for kc in range(KD):
                    self.mm(pbc[0][:, 128:128 + L], wC[:, kc, :], xk(kc), kc == 0, kc == KD - 1, Bxb + [BwC[kc]], [pbc[1]])
                pz = (self.pb[6], self.pbB[6])
                for kc in range(KD):
                    self.mm(pz[0][0:L, :], self.xbf[:, kc, t0:t0 + L], wz[:, kc, :], kc == 0, kc == KD - 1, Bxb + [Bz[kc]], [pz[1]])
                S.op("act", lambda h, pd=pd, L=L: h.activation(out=G8[:, TMP, 0:L], in_=pd[0][0:8, 0:L], func=AF.Exp, bias=par8[:, 0:1], scale=1.0),
                     reads=[pd[1], Bp[0]], writes=[BG8])
                S.op("act", lambda h, L=L: h.activation(out=G8[:, DT, 0:L], in_=G8[:, TMP, 0:L], func=AF.Ln, bias=1.0, scale=1.0), reads=[BG8], writes=[BG8])
                S.op("act", lambda h, px=px, extv=extv, blen=blen: h.activation(out=extv[:, 0:4, :, 3:3 + blen], in_=px[0][:, :].rearrange("p (c b t) -> p c b t", c=4, t=blen), func=AF.Copy),
                     reads=[px[1]], writes=[Bext])
                S.op("act", lambda h, pbc=pbc, extv=extv, blen=blen: h.activation(out=extv[:, 4:6, :, 3:3 + blen], in_=pbc[0][:, 0:256].rearrange("p (c b t) -> p c b t", c=2, t=blen), func=AF.Copy),
                     reads=[pbc[1]], writes=[Bext])
                S.op("dve", lambda h, L=L: h.tensor_scalar(G8[:, TMP, 0:L], G8[:, DT, 0:L], par8[:, 1:2], None, ALU.mult), reads=[BG8, Bp[1]], writes=[BG8])
                d0 = MC["rst_s"][0:8, 0:L] if samp else onesf[0:8, 0:L]
                S.op("dve", lambda h, d0=d0, L=L: h.tensor_tensor_scan(G8[:, CS, 0:L], d0, G8[:, TMP, 0:L], 0.0, ALU.mult, ALU.add), reads=[BG8, Bmc], writes=[BG8])
                csL = g8(CS)[:, :, blen - 1]
                S.op("dve", lambda h, g8=g8, csL=csL, nblk=nblk, blen=blen: h.tensor_tensor(g8(TMP), csL.unsqueeze(2).broadcast_to([8, nblk, blen]), g8(CS), ALU.subtract),
                     reads=[BG8], writes=[BG8])
                S.op("act", lambda h, L=L: h.activation(out=G8[:, WL, 0:L], in_=G8[:, TMP, 0:L], func=AF.Exp), reads=[BG8], writes=[BG8])
                S.op("dve", lambda h, L=L: h.tensor_tensor(G8[:, WL, 0:L], G8[:, WL, 0:L], G8[:, DT, 0:L], ALU.mult), reads=[BG8], writes=[BG8])
                S.op("dve", lambda h, L=L: h.tensor_scalar(G8[:, NCS, 0:L], G8[:, CS, 0:L], -1.0, None, ALU.mult), reads=[BG8], writes=[BG8])
                S.op("act", lambda h, L=L: h.activation(out=G8[:, ECS, 0:L], in_=G8[:, CS, 0:L], func=AF.Exp), reads=[BG8], writes=[BG8])
                S.op("act", lambda h, csL=csL, nblk=nblk: h.activation(out=edL[:, 0:nblk], in_=csL, func=AF.Exp), reads=[BG8], writes=[BedL])
                pc = self.psbank()
                for qi, slot in enumerate((NCS, ECS, DT, WL)):
                    self.mm(pc[0][0:L, qi * 8:qi * 8 + 8], G8[:, slot, 0:L], id8, True, True, [BG8, self.Bconst], [pc[1]])
                S.op("dve", lambda h, pc=pc, L=L, cols=cols: h.tensor_copy(cols[0:L, :], pc[0][0:L, 0:32]), reads=[pc[1]], writes=[Bcols])
                S.op("dve", lambda h, nseq=nseq: h.tensor_tensor(edLD[:, :, 0:nseq], edL[:, 0:nseq].unsqueeze(1).broadcast_to([8, 8, nseq]),
                                                                 id8.unsqueeze(2).broadcast_to([8, 8, nseq]), ALU.mult),
                     reads=[BedL, self.Bconst], writes=[BedLD])
                pdc = self.psbank()
                if nseq == 16:
                    self.mm(pdc[0][:, 0:128], onesf[0:8, 0:128], edLD[:, :, :].rearrange("p h s -> p (h s)"), True, True, [BedLD, Bmc], [pdc[1]])
                    S.op("act", lambda h, pdc=pdc: h.activation(out=decb[:, :, 0:16], in_=pdc[0][:, 0:128].rearrange("p (h s) -> p h s", s=16), func=AF.Copy),
                         reads=[pdc[1]], writes=[Bdecb])
                else:
                    assert nseq == 1
                    self.mm(pdc[0][:, 0:8], onesf[0:8, 0:128], edLD[:, :, 0], True, True, [BedLD, Bmc], [pdc[1]])
                    S.op("act", lambda h, pdc=pdc: h.activation(out=decb[:, :, 0:1], in_=pdc[0][:, 0:8].unsqueeze(2), func=AF.Copy),
                         reads=[pdc[1]], writes=[Bdecb])
                if samp or ci == nprompt - 1:
                    nrow = 48 if samp else 3
                    if samp:
                        S.op("dve", lambda h, extv=extv: h.tensor_copy(nct[:, :, :].rearrange("p c (s w) -> p c s w", w=3), extv[:, :, 0:16, 4:7]), reads=[Bext], writes=[Bnct])
                    else:
                        S.op("dve", lambda h, extv=extv, blen=blen: h.tensor_copy(nct[:, :, 0:3], extv[:, :, 0, blen:blen + 3]), reads=[Bext], writes=[Bnct])
                    for half, (c_lo, c_hi) in enumerate(((0, 4), (4, 6))):
                        pco = self.psbank()
                        for cc in range(c_lo, c_hi):
                            lhs = nct[:, cc, 0:nrow]
                            self.mm(pco[0][0:nrow, (cc - c_lo) * 128:(cc - c_lo + 1) * 128], lhs, self.identf[:, :], True, True, [Bnct, self.Bconst], [pco[1]])
                        ncol = (c_hi - c_lo) * 128
                        S.op("dve", lambda h, pco=pco, c_lo=c_lo, ncol=ncol, nrow=nrow: h.tensor_copy(cvt[0:nrow, c_lo * 128:c_lo * 128 + ncol], pco[0][0:nrow, 0:ncol]),
                             reads=[pco[1]], writes=[Bcvt])
                    for (c0, lc, n) in COLS:
                        if samp:
                            dst = d["s_cv"][j].rearrange("s w c -> (s w) c")[:, c0:c0 + n]
                        else:
                            dst = d["p_cv"][j][:, c0:c0 + n]
                        S.op("sp", lambda h, dst=dst, lc=lc, n=n, nrow=nrow: h.dma_start(out=dst, in_=cvt[0:nrow, lc:lc + n]), reads=[Bcvt], dma=True)
                for cc in range(6):
                    eng = "dve"
                    a_ = acc[:, 0, 0:L].rearrange("p (b t) -> p b t", t=blen)
                    Ba_ = Bacc[cc % 2]
                    ci_ = CHI[cc]
                    S.op(eng, lambda h, a_=a_, cc=cc, ci_=ci_, extv=extv, blen=blen: h.tensor_scalar(a_, extv[:, cc, :, 0:blen], cw[:, ci_, 0:1], cb[:, ci_:ci_ + 1], ALU.mult, ALU.add),
                         reads=[Bext] + Bcw, writes=[Ba_])
                    for w in range(1, 4):
                        if eng == "dve":
                            S.op(eng, lambda h, a_=a_, cc=cc, ci_=ci_, w=w, extv=extv, blen=blen: h.scalar_tensor_tensor(a_, extv[:, cc, :, w:w + blen], cw[:, ci_, w:w + 1], a_, ALU.mult, ALU.add),
                                 reads=[Bext, Ba_] + Bcw, writes=[Ba_])
                        else:
                            t_ = ctmp[:, 0:L].rearrange("p (b t) -> p b t", t=blen)
                            S.op(eng, lambda h, t_=t_, cc=cc, ci_=ci_, w=w, extv=extv, blen=blen: h.tensor_scalar(t_, extv[:, cc, :, w:w + blen], cw[:, ci_, w:w + 1], None, ALU.mult),
                                 reads=[Bext] + Bcw, writes=[Bctmp])
                            S.op(eng, lambda h, a_=a_, t_=t_: h.tensor_tensor(a_, a_, t_, ALU.add), reads=[Ba_, Bctmp], writes=[Ba_])
                    S.op("act", lambda h, cc=cc, L=L: h.activation(out=xc[:, cc, 0:L], in_=acc[:, 0, 0:L], func=AF.Silu), reads=[Ba_], writes=[Bxc])
                S.op("act", lambda h, pz=pz, L=L, zs=zs: h.activation(out=zs[0:L, :], in_=pz[0][0:L, :], func=AF.Silu), reads=[pz[1]], writes=[Bzs])
                for cc in range(5):
                    self.tr(self.pbf[0:L, cc * 128:(cc + 1) * 128], xc[:, cc, 0:L], self.ident[:, :], [Bxc, self.Bconst], [self.BpbfB])
                S.op("act", lambda h, L=L: h.activation(out=xtok[0:L, :], in_=self.pbf[0:L, 0:640], func=AF.Copy), reads=[self.BpbfB], writes=[Bxtok])
                x3 = lambda t, L=L: t[0:L, 0:512].rearrange("p (h e) -> p h e", e=64)
                c3 = lambda k, L=L, cols=cols: cols[0:L, k * 8:(k + 1) * 8].unsqueeze(2).broadcast_to([L, 8, 64])
                S.op("dve", lambda h, x3=x3, c3=c3: h.tensor_tensor(x3(xdt), x3(xtok), c3(2), ALU.mult), reads=[Bxtok, Bcols], writes=[Bxdt])
                S.op("dve", lambda h, x3=x3, c3=c3: h.tensor_tensor(x3(xw), x3(xtok), c3(3), ALU.mult), reads=[Bxtok, Bcols], writes=[Bxw])
                pcb = self.psbank()
                self.mm(pcb[0][0:L, 0:L], xc[:, 4, 0:L], xc[:, 5, 0:L], True, True, [Bxc], [pcb[1]])
                S.op("act", lambda h, pcb=pcb, L=L: h.activation(out=CBT[0:L, 0:L], in_=pcb[0][0:L, 0:L], func=AF.Copy), reads=[pcb[1]], writes=[BCBT])
                S.op("dve", lambda h, x3=x3, L=L, xD=xD: h.tensor_tensor(x3(xD), x3(xtok), Dbc[0:L, :].unsqueeze(2).broadcast_to([L, 8, 64]), ALU.mult),
                     reads=[Bxtok, Bp[2]], writes=[BxD])
                if pending is not None:
                    pending()
                mneg = MC["mneg_s"] if samp else MC["mneg_p"]
                def half_front(hf, L=L, mneg=mneg):
                    S.op("dve", lambda h, hf=hf, L=L: h.tensor_tensor(csD[:, :, 0:L], G8[:, CS, 0:L].unsqueeze(1).broadcast_to([8, 4, L]),
                                                                     self.identf[0:8, 4 * hf:4 * hf + 4].unsqueeze(2).broadcast_to([8, 4, L]), ALU.mult),
                         reads=[BG8, self.Bconst], writes=[BcsD])
                    pbq = self.psbank()
                    self.mm(pbq[0][0:L, 0:512], onesf[0:8, 0:L], csD[:, :, :].rearrange("p a t -> p (a t)"), True, False, [BcsD, Bmc], [pbq[1]])
                    for q in range(4):
                        self.mm(pbq[0][0:L, q * 128:q * 128 + L], self.ident[0:L, 0:L], mneg[0:L, 0:L], False, q == 3, [Bmc, self.Bconst], [pbq[1]])
                    return pbq
                assert L == 128
                pbh = {0: half_front(0)}
                for hh in range(8):
                    e_, Be_ = E[hh % 2], BE[hh % 2]
                    pt_, Bpt_ = PT[hh % 2], BPT[hh % 2]
                    pbq = pbh[hh // 4]
                    q = hh % 4
                    S.op("act", lambda h, e_=e_, pbq=pbq, hh=hh, q=q, L=L, cols=cols: h.activation(out=e_[0:L, 0:L], in_=pbq[0][0:L, q * 128:q * 128 + L], func=AF.Exp, bias=cols[0:L, hh:hh + 1], scale=1.0),
                         reads=[pbq[1], Bcols], writes=[Be_])
                    S.op("dve", lambda h, e_=e_, pt_=pt_, L=L: h.tensor_tensor(pt_[0:L, 0:L], e_[0:L, 0:L], CBT[0:L, 0:L], ALU.mult),
                         reads=[Be_, BCBT], writes=[Bpt_])
                    if hh == 0:
                        pbh[1] = half_front(1)
                    self.mm(PY[0:L, hh * 64:(hh + 1) * 64], pt_[0:L, 0:L], xdt[0:L, hh * 64:(hh + 1) * 64], True, True, [Bpt_, Bxdt], [BPY])
                d3 = lambda jj: decb[:, :, jj:jj + 1].broadcast_to([128, 8, 64])
                h3 = hTf[:, :].rearrange("p (h e) -> p h e", e=64)
                if not samp:
                    self.mm(PI[0:L, :], xc[:, 5, 0:L], hTb[:, :], True, True, [Bxc, BhTb], [BPI])
                    pu = self.psbank()
                    self.mm(pu[0][:, :], xtok[0:L, 512:640], xw[0:L, :], True, True, [Bxtok, Bxw], [pu[1]])
                    S.op("dve", lambda h, d3=d3: h.tensor_tensor(h3, h3, d3(0), ALU.mult), reads=[BhTf, Bdecb], writes=[BhTf])
                    S.op("dve", lambda h, pu=pu: h.tensor_tensor(hTf[:, :], hTf[:, :], pu[0][:, :], ALU.add), reads=[BhTf, pu[1]], writes=[BhTf])
                    S.op("act", lambda h: h.activation(out=hTb[:, :], in_=hTf[:, :], func=AF.Copy), reads=[BhTf], writes=[BhTb])
                    if ci == nprompt - 1:
                        pt2 = self.psbank()
                        for pr in range(4):
                            self.mm(pt2[0][:, pr * 128:(pr + 1) * 128], hTf[:, pr * 128:(pr + 1) * 128], self.identf[:, :], True, True, [BhTf, self.Bconst], [pt2[1]])
                        S.op("act", lambda h, pt2=pt2: h.activation(out=hnat[:, :, :], in_=pt2[0][:, :].rearrange("p (r n) -> p r n", n=128), func=AF.Copy), reads=[pt2[1]], writes=[Bhnat])
                        dsth = d["p_h"][j][g * 8:(g + 1) * 8].rearrange("(pr hh) p n -> (hh p) pr n", hh=2)
                        S.op("sp", lambda h, dsth=dsth: h.dma_start(out=dsth, in_=hnat[:, :, :]), reads=[Bhnat], dma=True)
                else:
                    extf = ext[:, :, :].rearrange("p c w -> p (c w)")
                    hin = [extf[:, k * 512:(k + 1) * 512].rearrange("p (r n) -> p r n", n=128) for k in range(2)]
                    Bhin = [Buf(), Buf()]
                    S.op("dve", lambda h: h.memset(extf[:, 1024:1026], 0.0), writes=[Bext] + Bhin)
                    def load_state(jj):
                        srch = d["st_sh"][j][jj, g * 8:(g + 1) * 8].rearrange("(pr hh) p n -> (hh p) pr n", hh=2)
                        S.op("sp", lambda h, srch=srch, k=jj % 2: h.dma_start(out=hin[k], in_=srch), writes=[Bhin[jj % 2]], dma=True)
                    load_state(0)
                    for jj in range(nseq):
                        if jj + 1 < nseq:
                            load_state(jj + 1)
                        hin_, Bhin_ = hin[jj % 2], Bhin[jj % 2]
                        pt1 = self.psbank()
                        for pr in range(4):
                            self.mm(pt1[0][:, pr * 128:(pr + 1) * 128], hin_[:, pr, :], self.identf[:, :], True, True, [Bhin_, self.Bconst], [pt1[1]])
                        S.op("act", lambda h, pt1=pt1: h.activation(out=hTb[:, :], in_=pt1[0][:, :], func=AF.Copy), reads=[pt1[1]], writes=[BhTb])
                        S.op("dve", lambda h, jj=jj: h.tensor_tensor(Cxj[:, 0:64], xc[:, 5, 0:64], MC["qmask"][:, jj, 0:64], ALU.mult), reads=[Bxc, Bmc], writes=[BCxj])
                        self.mm(PI[0:L, :], Cxj[:, 0:L], hTb[:, :], jj == 0, jj == nseq - 1, [BCxj, BhTb], [BPI])
                        S.op("dve", lambda h, jj=jj, L=L: h.tensor_scalar(Bxj[0:L, :], xtok[0:L, 512:640], MC["tokmask"][0:L, jj:jj + 1], None, ALU.mult),
                             reads=[Bxtok, Bmc], writes=[BBxj])
                        pu = self.psbank()
                        self.mm(pu[0][:, :], Bxj[0:L, :], xw[0:L, :], True, True, [BBxj, Bxw], [pu[1]])
                        S.op("dve", lambda h, d3=d3, jj=jj, pt1=pt1: h.tensor_tensor(h3, pt1[0][:, :].rearrange("p (h e) -> p h e", e=64), d3(jj), ALU.mult),
                             reads=[pt1[1], Bdecb], writes=[BhTf])
                        S.op("dve", lambda h, pu=pu: h.tensor_tensor(hTf[:, :], hTf[:, :], pu[0][:, :], ALU.add), reads=[BhTf, pu[1]], writes=[BhTf])
                        pt2 = self.psbank()
                        for pr in range(4):
                            self.mm(pt2[0][:, pr * 128:(pr + 1) * 128], hTf[:, pr * 128:(pr + 1) * 128], self.identf[:, :], True, True, [BhTf, self.Bconst], [pt2[1]])
                        S.op("act", lambda h, pt2=pt2: h.activation(out=hnat[:, :, :], in_=pt2[0][:, :].rearrange("p (r n) -> p r n", n=128), func=AF.Copy), reads=[pt2[1]], writes=[Bhnat])
                        dsth = d["s_h"][j][jj, g * 8:(g + 1) * 8].rearrange("(pr hh) p n -> (hh p) pr n", hh=2)
                        S.op("sp", lambda h, dsth=dsth: h.dma_start(out=dsth, in_=hnat[:, :, :]), reads=[Bhnat], dma=True)
                def tail(t0=t0, L=L, cols=cols, Bcols=Bcols, zs=zs, Bzs=Bzs, xD=xD, BxD=BxD, c3=c3):
                    y3 = ytok[0:L, :].rearrange("p (h e) -> p h e", e=64)
                    S.op("dve", lambda h, y3=y3, c3=c3, L=L: h.tensor_tensor(y3, PI[0:L, :].rearrange("p (h e) -> p h e", e=64), c3(1), ALU.mult), reads=[BPI, Bcols], writes=[Bytok])
                    S.op("dve", lambda h, L=L: h.tensor_tensor(ytok[0:L, :], ytok[0:L, :], PY[0:L, :], ALU.add), reads=[Bytok, BPY], writes=[Bytok])
                    S.op("dve", lambda h, L=L, xD=xD: h.tensor_tensor(ytok[0:L, :], ytok[0:L, :], xD[0:L, :], ALU.add), reads=[Bytok, BxD], writes=[Bytok])
                    S.op("dve", lambda h, L=L, zs=zs: h.tensor_tensor(ytok[0:L, :], ytok[0:L, :], zs[0:L, :], ALU.mult), reads=[Bytok, Bzs], writes=[Bytok])
                    S.op("act", lambda h, L=L: h.activation(out=yn[0:L, :], in_=ytok[0:L, :], func=AF.Square, accum_out=ss[0:L, 0:1]), reads=[Bytok], writes=[Byn, Bss])
                    S.op("dve", lambda h, L=L: h.tensor_scalar(ss[0:L, :], ss[0:L, :], 1.0 / 512.0, LN_EPS, ALU.mult, ALU.add), reads=[Bss], writes=[Bss])
                    S.op("act", lambda h, L=L: h.activation(out=ss[0:L, :], in_=ss[0:L, :], func=AF.Ln), reads=[Bss], writes=[Bss])
                    S.op("act", lambda h, L=L: h.activation(out=ss[0:L, :], in_=ss[0:L, :], func=AF.Exp, scale=-0.5), reads=[Bss], writes=[Bss])
                    S.op("dve", lambda h, L=L: h.scalar_tensor_tensor(yn[0:L, :], ytok[0:L, :], ss[0:L, 0:1], normg[0:L, :], ALU.mult, ALU.mult),
                         reads=[Bytok, Bss, Bp[3], Byn], writes=[Byn])
                    for cc in range(4):
                        self.tr(self.pbf[:, cc * 128:cc * 128 + L], yn[0:L, cc * 128:(cc + 1) * 128], self.ident[0:L, 0:L], [Byn, self.Bconst], [self.BpbfB])
                    gs0 = (t0 // 512) * 512 if t0 < cfg.seq else t0
                    toff = t0 - gs0
                    S.op("act", lambda h, L=L, toff=toff: h.activation(out=ynT[:, :, toff:toff + L], in_=self.pbf[:, 0:512].rearrange("p (h t) -> p h t", t=128)[:, :, 0:L], func=AF.Copy),
                         reads=[self.BpbfB], writes=[BynT])
                    gend = t0 + L
                    if t0 >= cfg.seq or gend % 512 == 0 or gend == cfg.seq:
                        self.out_proj(wo, Bwo, 4, ynT, BynT, gs0, gend - gs0, first)
                pending = tail
            pending()
        S.barrier()


_W_NAMES = ("ab_w_in", "ab_ig_bias", "ab_fg_bias", "ab_ml_norm", "ab_gla_wa2", "ab_gla_ba", "ab_gla_norm",
            "ab_w_out", "ssd_w_in", "ssd_conv_w", "ssd_conv_b", "ssd_dt_bias", "ssd_a_log", "ssd_d", "ssd_norm",
            "ssd_w_out", "mlp_w1", "mlp_w2", "ln_mix_g", "ln_mix_b", "ln_mlp_g", "ln_mlp_b")


def run(cfg, inputs, ncores=8):
    prog = Prog(cfg)
    nc = prog.build()
    f = lambda a: np.ascontiguousarray(np.asarray(a, dtype=np.float32))
    W = {k: f(inputs[k]) for k in _W_NAMES}
    ns = cfg.nseq
    in_maps = []
    for c in range(ncores):
        m = dict(W)
        m["xp"] = f(inputs["x_prompt"][c])
        m["xs"] = f(inputs["x_sample"][c * ns:(c + 1) * ns]).reshape(cfg.ts_real, D)
        m["st_mC"] = f(inputs["state_mlstm_C"][:, c * ns:(c + 1) * ns])
        m["st_mn"] = f(inputs["state_mlstm_n"][:, c * ns:(c + 1) * ns])
        m["st_mm"] = f(inputs["state_mlstm_m"][:, c * ns:(c + 1) * ns])
        m["st_gS"] = f(inputs["state_gla_S"][:, c * ns:(c + 1) * ns])
        m["st_sh"] = f(inputs["state_ssd_h"][:, c * ns:(c + 1) * ns])
        m["st_cv"] = f(inputs["state_ssd_conv"][:, c * ns:(c + 1) * ns])
        in_maps.append(m)
    res = run_bass_kernel_spmd(nc, in_maps, core_ids=list(range(ncores)))
    R = res.results
    cat = lambda k, ax: np.concatenate([np.expand_dims(r[k], ax) if False else r[k] for r in R], axis=ax)
    y_prompt = np.stack([r["yp"] for r in R], 0)
    y_sample = np.concatenate([r["ys"].reshape(ns, cfg.slen, D) for r in R], 0)
    outs = [y_prompt, y_sample]
    for k in ("p_C", "p_n", "p_m", "p_S", "p_h", "p_cv"):
        outs.append(np.stack([r[k] for r in R], 1))
    for k in ("s_C", "s_n", "s_m", "s_S", "s_h", "s_cv"):
        outs.append(np.concatenate([r[k] for r in R], 1))
    return tuple(np.ascontiguousarray(o.astype(np.float32)) for o in outs)


def kernel(**inputs):
    return run(Cfg(), inputs)
```
